# Optimizing a Trainium2 kernel written in Bass

```python
import jax, jax.numpy as jnp
from jax import lax
import numpy as np

D_MODEL = 2048
BATCH = 16
SEQ = 2048
DEPTH = 4

CHUNK = 64
Q_BLOCK = 128
HEAD_DIM = 128
H_SB = 6
H_DIFF = 5
H_FOX = 5
DIFF_DIM = HEAD_DIM // 2
D_SB = H_SB * HEAD_DIM
D_DIFF = H_DIFF * HEAD_DIM
D_FOX = H_FOX * HEAD_DIM
N_BRANCH = 3
D_IN = 3 * D_SB + 3 * D_DIFF + 3 * D_FOX + H_FOX
D_FF = -(-8 * D_MODEL // (3 * 256)) * 256
EPS = 1e-6
ALIBI_MAX = 8.0

kernel_name = "hybrid_stickbreak_diff_fox_trunk"


def _rms(x, g):
    xf = x.astype(jnp.float32)
    y = xf * lax.rsqrt(jnp.mean(xf * xf, axis=-1, keepdims=True) + EPS)
    return (y * g.astype(jnp.float32)).astype(x.dtype)


def _heads(t, n, d):
    b, s, _ = t.shape
    return t.reshape(b, s, n, d).transpose(0, 2, 1, 3)


def _merge(o):
    b, h, s, d = o.shape
    return o.transpose(0, 2, 1, 3).reshape(b, s, h * d)


def _stick_breaking(q, k, v):
    seq, d = q.shape[2], q.shape[3]
    scale = d ** -0.5
    outs = []
    for i in range(seq // Q_BLOCK):
        q0, q1 = i * Q_BLOCK, (i + 1) * Q_BLOCK
        z = jnp.einsum('bhqd,bhkd->bhqk', q[:, :, q0:q1], k[:, :, :q1]).astype(jnp.float32) * scale
        tq = jnp.arange(q0, q1)[:, None]
        sk = jnp.arange(q1)[None, :]
        before = sk < tq
        log_not = jnp.where(before, -jax.nn.softplus(z), 0.0)
        tail = lax.cumsum(log_not, axis=3, reverse=True) - log_not
        w = jnp.where(before, jnp.exp(jax.nn.log_sigmoid(z) + tail), 0.0)
        outs.append(jnp.einsum('bhqk,bhkd->bhqd', w.astype(v.dtype), v[:, :, :q1]))
    return jnp.concatenate(outs, axis=2)


def _diff_attention(q1, q2, k1, k2, v, lam, slopes):
    seq = q1.shape[2]
    scale = DIFF_DIM ** -0.5
    outs = []
    for i in range(seq // Q_BLOCK):
        q0, qe = i * Q_BLOCK, (i + 1) * Q_BLOCK
        tq = jnp.arange(q0, qe)[:, None]
        sk = jnp.arange(qe)[None, :]
        visible = (sk // CHUNK) <= (tq // CHUNK)
        bias = -slopes[:, None, None] * jnp.abs(tq - sk).astype(jnp.float32)

        def probs(qa, ka):
            s = jnp.einsum('bhqd,bhkd->bhqk', qa[:, :, q0:qe], ka[:, :, :qe]).astype(jnp.float32) * scale + bias
            return jax.nn.softmax(jnp.where(visible, s, -jnp.inf), axis=-1)

        a = probs(q1, k1) - lam * probs(q2, k2)
        outs.append(jnp.einsum('bhqk,bhkd->bhqd', a.astype(v.dtype), v[:, :, :qe]))
    return jnp.concatenate(outs, axis=2)


def _forgetting_attention(q, k, v, log_f):
    seq, d = q.shape[2], q.shape[3]
    scale = d ** -0.5
    cum = jnp.cumsum(log_f, axis=-1)
    outs = []
    for i in range(seq // Q_BLOCK):
        q0, q1 = i * Q_BLOCK, (i + 1) * Q_BLOCK
        tq = jnp.arange(q0, q1)[:, None]
        sk = jnp.arange(q1)[None, :]
        s = jnp.einsum('bhqd,bhkd->bhqk', q[:, :, q0:q1], k[:, :, :q1]).astype(jnp.float32) * scale
        s = s + cum[:, :, q0:q1, None] - cum[:, :, None, :q1]
        p = jax.nn.softmax(jnp.where(sk <= tq, s, -jnp.inf), axis=-1)
        outs.append(jnp.einsum('bhqk,bhkd->bhqd', p.astype(v.dtype), v[:, :, :q1]))
    return jnp.concatenate(outs, axis=2)


def setup_inputs(seed: int = 0) -> dict:
    key = jax.random.key(seed)
    ks = jax.random.split(key, 24)
    L, D = DEPTH, D_MODEL

    def w(k, shape, fan_in):
        return jax.random.normal(k, shape, jnp.float32) * fan_in ** -0.5

    def gain(k, shape):
        return 1.0 + 0.02 * jax.random.normal(k, shape, jnp.float32)

    return {
        "x": jax.random.normal(ks[0], (BATCH, SEQ, D), jnp.float32),
        "norm_mix": gain(ks[1], (L, D)),
        "w_in": w(ks[2], (L, D, D_IN), D),
        "b_forget": 3.0 + 0.5 * jax.random.normal(ks[3], (L, H_FOX), jnp.float32),
        "q_norm_diff": gain(ks[4], (L, DIFF_DIM)),
        "k_norm_diff": gain(ks[5], (L, DIFF_DIM)),
        "lambda_q1": 0.1 * jax.random.normal(ks[6], (L, DIFF_DIM), jnp.float32),
        "lambda_k1": 0.1 * jax.random.normal(ks[7], (L, DIFF_DIM), jnp.float32),
        "lambda_q2": 0.1 * jax.random.normal(ks[8], (L, DIFF_DIM), jnp.float32),
        "lambda_k2": 0.1 * jax.random.normal(ks[9], (L, DIFF_DIM), jnp.float32),
        "sub_norm_diff": gain(ks[10], (L, HEAD_DIM)),
        "q_norm_fox": gain(ks[11], (L, HEAD_DIM)),
        "k_norm_fox": gain(ks[12], (L, HEAD_DIM)),
        "w_branch_sb": w(ks[13], (L, D_SB, D), D_SB),
        "w_branch_diff": w(ks[14], (L, D_DIFF, D), D_DIFF),
        "w_branch_fox": w(ks[15], (L, D_FOX, D), D_FOX),
        "w_gate": w(ks[16], (L, D, N_BRANCH * D), D),
        "w_out": w(ks[17], (L, D, D), D),
        "norm_ffn": gain(ks[18], (L, D)),
        "w_ff_gate": w(ks[19], (L, D, D_FF), D),
        "w_ff_up": w(ks[20], (L, D, D_FF), D),
        "w_ff_down": w(ks[21], (L, D_FF, D), D_FF),
    }


def reference(x, norm_mix, w_in, b_forget, q_norm_diff, k_norm_diff, lambda_q1, lambda_k1,
              lambda_q2, lambda_k2, sub_norm_diff, q_norm_fox, k_norm_fox, w_branch_sb,
              w_branch_diff, w_branch_fox, w_gate, w_out, norm_ffn, w_ff_gate, w_ff_up, w_ff_down):
    b, s, d = x.shape
    slopes = 2.0 ** (-ALIBI_MAX * jnp.arange(1, H_DIFF + 1, dtype=jnp.float32) / H_DIFF)
    o1 = 3 * D_SB
    o2 = o1 + 3 * D_DIFF
    o3 = o2 + 3 * D_FOX
    for l in range(DEPTH):
        lambda_init = 0.8 - 0.6 * float(np.exp(-0.3 * l))
        xn = _rms(x, norm_mix[l])
        h = xn @ w_in[l]

        q_a, k_a, v_a = jnp.split(h[..., :o1], 3, axis=-1)
        o_a = _stick_breaking(_heads(q_a, H_SB, HEAD_DIM), _heads(k_a, H_SB, HEAD_DIM),
                              _heads(v_a, H_SB, HEAD_DIM))

        q_b, k_b, v_b = jnp.split(h[..., o1:o2], 3, axis=-1)
        qd = _rms(_heads(q_b, 2 * H_DIFF, DIFF_DIM), q_norm_diff[l]).reshape(b, H_DIFF, 2, s, DIFF_DIM)
        kd = _rms(_heads(k_b, 2 * H_DIFF, DIFF_DIM), k_norm_diff[l]).reshape(b, H_DIFF, 2, s, DIFF_DIM)
        lam = (jnp.exp(jnp.sum(lambda_q1[l].astype(jnp.float32) * lambda_k1[l].astype(jnp.float32)))
               - jnp.exp(jnp.sum(lambda_q2[l].astype(jnp.float32) * lambda_k2[l].astype(jnp.float32)))
               + lambda_init)
        o_b = _diff_attention(qd[:, :, 0], qd[:, :, 1], kd[:, :, 0], kd[:, :, 1],
                              _heads(v_b, H_DIFF, HEAD_DIM), lam, slopes)
        o_b = _rms(o_b, sub_norm_diff[l]) * (1.0 - lambda_init)

        q_c, k_c, v_c = jnp.split(h[..., o2:o3], 3, axis=-1)
        log_f = jax.nn.log_sigmoid(h[..., o3:].astype(jnp.float32)
                                   + b_forget[l].astype(jnp.float32)).transpose(0, 2, 1)
        o_c = _forgetting_attention(_rms(_heads(q_c, H_FOX, HEAD_DIM), q_norm_fox[l]),
                                    _rms(_heads(k_c, H_FOX, HEAD_DIM), k_norm_fox[l]),
                                    _heads(v_c, H_FOX, HEAD_DIM), log_f)

        gates = jax.nn.sigmoid((xn @ w_gate[l]).astype(jnp.float32)).astype(x.dtype).reshape(b, s, N_BRANCH, d)
        y = (gates[:, :, 0] * (_merge(o_a) @ w_branch_sb[l])
             + gates[:, :, 1] * (_merge(o_b) @ w_branch_diff[l])
             + gates[:, :, 2] * (_merge(o_c) @ w_branch_fox[l]))
        x = x + y @ w_out[l]

        xf = _rms(x, norm_ffn[l])
        x = x + (jax.nn.silu(xf @ w_ff_gate[l]) * (xf @ w_ff_up[l])) @ w_ff_down[l]
    return x
```

```python
import numpy as np
import ml_dtypes
from contextlib import ExitStack
import concourse.bass as bass
import concourse.mybir as mybir
from concourse.bass_utils import run_bass_kernel_spmd

F32 = mybir.dt.float32
BF16 = mybir.dt.bfloat16
AF = mybir.ActivationFunctionType
ALU = mybir.AluOpType
AX = mybir.AxisListType
NPBF = ml_dtypes.bfloat16

D = 2048
KC = 16
S = 2048
TB = 512
NTB = 4
DFF = 5632
FC = 44
L_ALL = 4
EPS = 1e-6
NEG = -30000.0
N_CORES = 8
SEQ_PER_CORE = 2
HEADS = [("sb", i) for i in range(6)] + [("diff", i) for i in range(5)] + [("fox", i) for i in range(5)]
SLOPES = [2.0 ** (-8.0 * (i + 1) / 5.0) for i in range(5)]
ENGS = ("pe", "act", "dve", "pool", "sp")
SAME_ENGINE_SYNC = True

PV_GMIX = 0
PV_GFFN = 16
PV_QD = 32
PV_KD = 33
PV_QF = 34
PV_KF = 35
PV_SUB = 36
PV_BF = 37
PV_LAM = 38
PV_PER_L = 38 + 256
CB_IDENT = 0
CB_NEGTRI = 128
CB_NEGONES = 256
CB_ONES = 384
CB_BLK64 = 512
CB_MSB = 640
CB_MFOX = 768
CB_DCORR = 896
CB_N = 896 + 1280


class Op:
    __slots__ = ("eng", "fn", "waits", "sig", "dsem", "val")


class Prog:
    def __init__(self, nc, es):
        self.nc = nc
        self.es = es
        self.q = {e: [] for e in ENGS}
        self.semh = {}
        for e in ENGS:
            self.semh[e] = es.enter_context(nc.semaphore("s_" + e))
        self.last = {e: None for e in ENGS}
        self.sp_dma_last = {}
        self.nsem = 0

    def new_sem(self, name):
        self.semh[name] = self.es.enter_context(self.nc.semaphore(name))
        return name

    def op(self, eng, fn, waits=(), dsem=None):
        o = Op()
        o.eng, o.fn, o.dsem, o.sig, o.val = eng, fn, dsem, False, None
        ws = []
        for w in waits:
            if w is None:
                continue
            if w.dsem is None and w.eng == eng:
                if eng == "pe" or not SAME_ENGINE_SYNC:
                    continue
            if w.dsem is None:
                w.sig = True
            ws.append(w)
        o.waits = ws
        self.q[eng].append(o)
        if fn is not None:
            self.last[eng] = o
            if dsem is not None and eng == "sp":
                self.sp_dma_last[dsem] = o
        return o

    def pe(self, fn, waits=()):
        return self.op("pe", fn, waits)

    def act(self, fn, waits=()):
        return self.op("act", fn, waits)

    def dve(self, fn, waits=()):
        return self.op("dve", fn, waits)

    def barrier(self):
        lasts = [self.last[e] for e in ("pe", "act", "dve") if self.last[e] is not None]
        dm = list(self.sp_dma_last.values())
        for e in ("pe", "act", "dve", "sp"):
            self.op(e, None, waits=[w for w in lasts if w.eng != e] + dm)

    def build(self):
        cnt = {}
        for e in ENGS:
            c = 0
            for o in self.q[e]:
                if o.fn is None:
                    continue
                if o.dsem is not None:
                    cnt[o.dsem] = cnt.get(o.dsem, 0) + 16
                    o.val = (o.dsem, cnt[o.dsem])
                elif o.sig:
                    c += 1
                    o.val = (e, c)
        semh = self.semh

        def run_engine(name, e):
            waited = {}
            for o in self.q[name]:
                for w in o.waits:
                    sn, v = w.val
                    if waited.get(sn, 0) >= v:
                        continue
                    waited[sn] = v
                    e.wait_ge(semh[sn], v)
                if o.fn is None:
                    continue
                ins = o.fn(e)
                if o.dsem is not None:
                    ins.then_inc(semh[o.dsem], 16)
                elif o.sig:
                    ins.then_inc(semh[name], 1)

        with self.nc.Block() as block:
            @block.tensor
            def _(e):
                run_engine("pe", e)

            @block.scalar
            def _(e):
                run_engine("act", e)

            @block.vector
            def _(e):
                run_engine("dve", e)

            @block.gpsimd
            def _(e):
                run_engine("pool", e)

            @block.sync
            def _(e):
                run_engine("sp", e)


class Slots:
    def __init__(self, items):
        self.items = list(items)
        self.n = len(self.items)
        self.i = 0
        self.users = [[] for _ in range(self.n)]

    def acquire(self):
        k = self.i
        self.i = (self.i + 1) % self.n
        w = self.users[k]
        self.users[k] = []
        return k, self.items[k], w

    def use(self, k, *ops):
        for o in ops:
            if o is not None:
                self.users[k].append(o)


class Builder:
    def __init__(self, n_layers=L_ALL, n_seq=SEQ_PER_CORE, dbg=False, stop_after=None):
        self.n_layers = n_layers
        self.n_seq = n_seq
        self.dbg = dbg
        self.stop_after = stop_after
        self.nc = bass.Bass("TRN2", target_bir_lowering=False)

    def view(self, off, nbytes, dtype=BF16, pattern=None, **kw):
        a = self.arena[:, off // 2:(off + nbytes) // 2]
        if dtype == F32:
            a = a.bitcast(F32)
        if pattern is not None:
            a = a.rearrange(pattern, **kw)
        return a

    def setup(self, es):
        nc = self.nc
        L, NS = self.n_layers, self.n_seq
        dt = nc.dram_tensor
        self.xT = dt("xT", [NS, KC, 128, S], F32, kind="ExternalInput").ap()
        self.outT = dt("outT", [NS, KC, 128, S], F32, kind="ExternalOutput").ap()
        self.w_in = dt("w_in", [L, 49, 128, KC, 128], F32, kind="ExternalInput").ap()
        self.w_gate = dt("w_gate", [L, 48, 128, KC, 128], F32, kind="ExternalInput").ap()
        self.w_bsb = dt("w_bsb", [L, 16, 128, 6, 128], F32, kind="ExternalInput").ap()
        self.w_bdf = dt("w_bdf", [L, 16, 128, 5, 128], F32, kind="ExternalInput").ap()
        self.w_bfx = dt("w_bfx", [L, 16, 128, 5, 128], F32, kind="ExternalInput").ap()
        self.w_out = dt("w_out", [L, 16, 128, KC, 128], F32, kind="ExternalInput").ap()
        self.w_fg = dt("w_fg", [L, FC, 128, KC, 128], F32, kind="ExternalInput").ap()
        self.w_fu = dt("w_fu", [L, FC, 128, KC, 128], F32, kind="ExternalInput").ap()
        self.w_fd = dt("w_fd", [L, 16, 128, FC, 128], F32, kind="ExternalInput").ap()
        self.pvec_d = dt("pvec", [128, L_ALL * PV_PER_L], F32, kind="ExternalInput").ap()
        self.cbf_d = dt("cbf", [128, CB_N], BF16, kind="ExternalInput").ap()
        self.augk_d = dt("augk", [5, 4, S], BF16, kind="ExternalInput").ap()
        self.augq_d = dt("augq", [5, 4, S], BF16, kind="ExternalInput").ap()
        self.foxinit_d = dt("foxinit", [5, 2, 4, S], BF16, kind="ExternalInput").ap()
        skind = "ExternalOutput" if self.dbg else "Internal"
        self.xres = dt("xres", [NS, KC, 128, S], F32, kind=skind).ap()
        self.oT_d = dt("oT_s", [KC, 128, S], BF16, kind=skind).ap()
        self.yT_d = dt("yT_s", [KC, 128, S], BF16, kind=skind).ap()
        self.hT_d = dt("hT_s", [FC, 128, S], BF16, kind=skind).ap()
        self.fox_d = dt("fox_s", [5, 2, 4, S], BF16, kind=skind).ap()
        self.xbT_d = dt("xbT_s", [KC, 128, S], BF16, kind=skind).ap()
        if self.dbg:
            self.dbgA = dt("dbgA", [KC, 128, S], BF16, kind="ExternalOutput").ap()

        self.P = Prog(nc, es)
        P = self.P
        total = nc.sbuf_bytes_remaining
        ARENA = 207 * 1024
        assert total >= ARENA, total
        self.arena = nc.alloc_sbuf_tensor("arena", [128, ARENA // 2], BF16)
        self.ps = nc.alloc_psum_tensor("ps", [128, 8, 512], F32)
        K = 1024
        self.A = self.view(0, 64 * K, BF16, "p (k t) -> p k t", k=KC)
        self.B = self.view(64 * K, 64 * K, BF16, "p (k t) -> p k t", k=KC)
        self.offB = 64 * K
        off = 128 * K
        self.wslots = []
        for i in range(6):
            self.wslots.append(self.view(off, 4 * K, BF16))
            off += 4 * K
        self.wring = Slots(self.wslots)
        self.wsem = [P.new_sem("w%d" % i) for i in range(6)]
        self.tmp_off = off
        tl = []
        for i in range(8):
            tl.append(off)
            off += 2 * K
        self.tmps = Slots(tl)
        self.rb_off = off
        off += 28 * K
        self.cbf = self.view(off, CB_N * 2, BF16)
        off += CB_N * 2
        npv = L_ALL * PV_PER_L
        self.pvec = self.view(off, npv * 4, F32)
        off += npv * 4
        self.pmisc = self.view(off, 64 * 4, F32)
        off += 64 * 4
        self.lamtmp = self.view(off, 64 * 4, F32)
        off += 64 * 4
        assert off <= ARENA, off
        self.ident = self.cbf[:, CB_IDENT:CB_IDENT + 128]
        self.negtri = self.cbf[:, CB_NEGTRI:CB_NEGTRI + 128]
        self.negones = self.cbf[:, CB_NEGONES:CB_NEGONES + 128]
        self.ones = self.cbf[:, CB_ONES:CB_ONES + 128]
        self.blk64 = self.cbf[:, CB_BLK64:CB_BLK64 + 128]
        self.msb = self.cbf[:, CB_MSB:CB_MSB + 128]
        self.mfox = self.cbf[:, CB_MFOX:CB_MFOX + 128]
        self.bulk_sems = [P.new_sem("bulk%d" % i) for i in range(FC)]
        self.xs_sems = [P.new_sem("xs%d" % i) for i in range(2)]
        self.st_sems = [P.new_sem("st%d" % i) for i in range(2)]
        self.aug_sems = [P.new_sem("aug%d" % i) for i in range(2)]
        self.fox_sem = P.new_sem("foxs")
        self.xb_sems = [P.new_sem("xb%d" % i) for i in range(2)]
        self.misc_sem = P.new_sem("misc")

    def bank(self, b):
        return self.ps[:, b, :]

    def tmp(self, dtype=F32, n=512):
        k, off, w = self.tmps.acquire()
        nb = n * (4 if dtype == F32 else 2)
        return k, self.view(off, nb, dtype), w

    def sp_dma(self, out, in_, sem, waits=()):
        return self.P.op("sp", lambda e: e.dma_start(out=out, in_=in_), waits=waits, dsem=sem)

    def bulk_load(self, out, in_, idx, waits=()):
        return self.sp_dma(out, in_, self.bulk_sems[idx], waits)

    def wfetch(self, src, kcn):
        k, slot, w = self.wring.acquire()
        dst = slot[:, 0:kcn * 128].rearrange("p (k m) -> p k m", k=kcn)
        op = self.P.op("pool", lambda e: e.dma_start(out=dst, in_=src), waits=w, dsem=self.wsem[k])
        return k, dst, op

    def init_phase(self):
        P = self.P
        l1 = self.bulk_load(self.cbf, self.cbf_d, 0)
        l2 = self.bulk_load(self.pvec, self.pvec_d, 1)
        l3 = self.P.op("sp", lambda e: e.dma_start(out=self.fox_d, in_=self.foxinit_d), dsem=self.misc_sem)
        self.fox_init_op = l3
        self.const_ops = [l1, l2]
        last = None
        for l in range(self.n_layers):
            base = l * PV_PER_L
            lam_init = 0.8 - 0.6 * float(np.exp(-0.3 * l))
            pm = self.pmisc
            pv = self.pvec

            def ts(dst, src, mul):
                return P.dve(lambda e: e.tensor_scalar(out=dst, in0=src, scalar1=float(mul), scalar2=None, op0=ALU.mult), waits=[l2])
            ts(pm[:, l * 8 + 0:l * 8 + 1], pv[:, base + PV_QD:base + PV_QD + 1], 64.0 ** 0.25)
            ts(pm[:, l * 8 + 1:l * 8 + 2], pv[:, base + PV_KD:base + PV_KD + 1], 64.0 ** 0.25)
            ts(pm[:, l * 8 + 2:l * 8 + 3], pv[:, base + PV_QF:base + PV_QF + 1], 128.0 ** 0.25)
            ts(pm[:, l * 8 + 3:l * 8 + 4], pv[:, base + PV_KF:base + PV_KF + 1], 128.0 ** 0.25)
            ts(pm[:, l * 8 + 4:l * 8 + 5], pv[:, base + PV_SUB:base + PV_SUB + 1], 1.0 - lam_init)
            ts(pm[:, l * 8 + 6:l * 8 + 7], pv[:, base + PV_BF:base + PV_BF + 1], -1.0)
            lt = self.lamtmp
            lq1 = pv[:, base + PV_LAM:base + PV_LAM + 64]
            lk1 = pv[:, base + PV_LAM + 64:base + PV_LAM + 128]
            lq2 = pv[:, base + PV_LAM + 128:base + PV_LAM + 192]
            lk2 = pv[:, base + PV_LAM + 192:base + PV_LAM + 256]
            prod = self.view(self.rb_off, 64 * 4, F32)
            sums = lt[:, l * 4:l * 4 + 2]
            a1 = P.dve(lambda e, lq1=lq1, lk1=lk1: e.tensor_tensor(out=prod, in0=lq1, in1=lk1, op=ALU.mult), waits=[l2, last])
            a2 = P.dve(lambda e, sums=sums: e.reduce_sum(out=sums[:, 0:1], in_=prod, axis=AX.X), waits=[a1])
            a3 = P.dve(lambda e, lq2=lq2, lk2=lk2: e.tensor_tensor(out=prod, in0=lq2, in1=lk2, op=ALU.mult), waits=[a2])
            a4 = P.dve(lambda e, sums=sums: e.reduce_sum(out=sums[:, 1:2], in_=prod, axis=AX.X), waits=[a3])
            ex = lt[:, l * 4 + 2:l * 4 + 4]
            a5 = P.act(lambda e, sums=sums, ex=ex: e.activation(out=ex, in_=sums, func=AF.Exp), waits=[a4])
            dd = lt[:, 32 + l:33 + l]
            a6 = P.dve(lambda e, ex=ex, dd=dd: e.tensor_tensor(out=dd, in0=ex[:, 1:2], in1=ex[:, 0:1], op=ALU.subtract), waits=[a5])
            nl = pm[:, l * 8 + 5:l * 8 + 6]
            last = P.dve(lambda e, dd=dd, nl=nl, li=lam_init: e.tensor_scalar(out=nl, in0=dd, scalar1=float(-li), scalar2=None, op0=ALU.add), waits=[a6])
        P.barrier()

    def finish_norm(self, X, rs, ss_bank0, last_mm, xb_ops):
        P = self.P
        t2s = []
        for tb in range(NTB):
            r = rs[:, tb * TB:(tb + 1) * TB]
            t1 = P.act(lambda e, r=r, tb=tb: e.activation(out=r, in_=self.bank(ss_bank0 + tb), func=AF.Ln, bias=self.eps_col, scale=1.0 / D), waits=[last_mm[tb]])
            t2s.append(P.act(lambda e, r=r: e.activation(out=r, in_=r, func=AF.Exp, scale=-0.5), waits=[t1]))
        self.norm_done = [None] * NTB

        def normalize(tb):
            r = rs[:, tb * TB:(tb + 1) * TB]
            for kc in range(KC):
                a = X[:, kc, tb * TB:(tb + 1) * TB]
                self.norm_done[tb] = P.dve(lambda e, a=a, r=r: e.tensor_tensor(out=a, in0=a, in1=r, op=ALU.mult), waits=[t2s[tb], xb_ops[kc]])
        normalize(0)
        P.barrier()
        for tb in range(1, NTB):
            normalize(tb)

    def norm_phase(self, src, gcol_base):
        P = self.P
        K = 1024
        oB = self.offB
        stage = Slots([self.view(oB + i * 8 * K, 8 * K, F32) for i in range(2)])
        sqs = Slots([self.view(oB + 16 * K + i * 4 * K, 4 * K, BF16) for i in range(2)])
        rs = self.view(self.rb_off + 12 * K, 8 * K, F32)
        last_mm = [None] * NTB
        xb_ops = []
        for kc in range(KC):
            k, st, w = stage.acquire()
            ld = self.sp_dma(st, src[kc], self.xs_sems[k], waits=w)
            k2, sq, w2 = sqs.acquire()
            sqo = P.act(lambda e, sq=sq, st=st: e.activation(out=sq, in_=st, func=AF.Square), waits=[ld] + w2)
            g = self.pvec[:, gcol_base + kc:gcol_base + kc + 1]
            xb = P.dve(lambda e, kc=kc, st=st, g=g: e.tensor_scalar(out=self.A[:, kc, :], in0=st, scalar1=g, scalar2=None, op0=ALU.mult), waits=[ld])
            xb_ops.append(xb)
            stage.use(k, sqo, xb)
            for tb in range(NTB):
                last_mm[tb] = P.pe(lambda e, tb=tb, sq=sq, kc=kc: e.matmul(self.bank(tb), lhsT=self.ones, rhs=sq[:, tb * TB:(tb + 1) * TB],
                                                                      start=(kc == 0), stop=(kc == KC - 1)), waits=[sqo])
            sqs.use(k2, last_mm[NTB - 1])
        self.finish_norm(self.A, rs, 0, last_mm, xb_ops)

    def attn_setup(self):
        K = 1024
        oB = self.offB
        self.hb = []
        off = oB
        for par in range(2):
            d = {}
            for nm in ("R0", "R1", "R2", "R3"):
                d[nm] = self.view(off, 4 * K, BF16)
                off += 4 * K
            d["V"] = self.view(off, 4 * K, BF16, "p (t d) -> p t d", t=16)
            off += 4 * K
            self.hb.append(d)
        self.sp_slots = [self.view(off + i * K, K, BF16) for i in range(3)]
        off += 3 * K
        self.acc_bufs = [self.view(off, K, BF16)]
        off += K
        self.pT_slots = [self.view(off + i * K, K, BF16) for i in range(4)]
        off += 4 * K
        self.orow_slots = [self.view(off + i * 4 * K, 4 * K, BF16) for i in range(2)]
        off += 8 * K
        ro = self.rb_off + 12 * K
        self.dO = [self.view(ro + i * 2 * K, 2 * K, F32) for i in range(2)]
        self.epi_bufs = [(self.view(ro + 4 * K + g * 4 * K, 2 * K, F32), self.view(ro + 6 * K + g * 4 * K, 2 * K, F32)) for g in range(2)]
        assert off <= oB + 64 * K, off - oB

    def proj_gen(self, l, hidx, par, hb_free):
        P = self.P
        typ, hi = HEADS[hidx]
        hb = self.hb[par]
        if typ == "sb":
            pq, pk, pvn = hi, 6 + hi, 12 + hi
        elif typ == "diff":
            pq, pk, pvn = 18 + hi, 23 + hi, 28 + hi
        else:
            pq, pk, pvn = 33 + hi, 38 + hi, 43 + hi
        done = []
        pm = self.pmisc
        first_writes = list(hb_free)
        if typ == "diff":
            z = []
            for nm, lo in (("R0", 64), ("R1", 64), ("R2", 0), ("R3", 0)):
                buf = hb[nm]
                z.append(P.dve(lambda e, buf=buf, lo=lo: e.memset(buf[lo:lo + 64, :], 0.0), waits=first_writes))
            a1 = P.op("sp", lambda e: e.dma_start(out=hb["R0"][64:68, :], in_=self.augq_d[hi]), waits=[z[0]], dsem=self.aug_sems[par])
            a2 = P.op("sp", lambda e: e.dma_start(out=hb["R1"][64:68, :], in_=self.augk_d[hi]), waits=[z[1]], dsem=self.aug_sems[par])
            a3 = P.op("sp", lambda e: e.dma_start(out=hb["R2"][0:4, :], in_=self.augq_d[hi]), waits=[z[2]], dsem=self.aug_sems[par])
            a4 = P.op("sp", lambda e: e.dma_start(out=hb["R3"][0:4, :], in_=self.augk_d[hi]), waits=[z[3]], dsem=self.aug_sems[par])
            done += [a1, a2, a3, a4]
        elif typ == "fox":
            z = []
            for nm in ("R2", "R3"):
                buf = hb[nm]
                z.append(P.dve(lambda e, buf=buf: e.memset(buf[:, :], 0.0), waits=first_writes))
            a1 = P.op("sp", lambda e: e.dma_start(out=hb["R2"][0:4, :], in_=self.fox_d[hi, 0]), waits=[z[0], self.fox_rows_op], dsem=self.aug_sems[par])
            a2 = P.op("sp", lambda e: e.dma_start(out=hb["R3"][0:4, :], in_=self.fox_d[hi, 1]), waits=[z[1], self.fox_rows_op], dsem=self.aug_sems[par])
            done += [a1, a2]
        yield
        for which, pidx in (("q", pq), ("k", pk)):
            wk, wp, wop = self.wfetch(self.w_in[l, pidx], KC)
            lastmm = None
            for tb in range(NTB):
                bk, bank_i, bw = self.projbanks.acquire()
                pb = self.bank(bank_i)
                for kc in range(KC):
                    lastmm = P.pe(lambda e, pb=pb, wp=wp, kc=kc, tb=tb: e.matmul(pb, lhsT=wp[:, kc, :], rhs=self.A[:, kc, tb * TB:(tb + 1) * TB],
                                                                                start=(kc == 0), stop=(kc == KC - 1)), waits=[wop, self.norm_done[tb]] + bw)
                sl = slice(tb * TB, (tb + 1) * TB)
                if typ == "sb":
                    dst = hb["R0"][:, sl] if which == "q" else hb["R1"][:, sl]
                    if which == "q":
                        ep = P.dve(lambda e, dst=dst, pb=pb: e.tensor_scalar(out=dst, in0=pb, scalar1=float(128.0 ** -0.5), scalar2=None, op0=ALU.mult),
                                   waits=[lastmm] + first_writes)
                    else:
                        ep = P.dve(lambda e, dst=dst, pb=pb: e.tensor_copy(out=dst, in_=pb), waits=[lastmm] + first_writes)
                    self.projbanks.use(bk, ep)
                    done.append(ep)
                else:
                    dd = 64.0 if typ == "diff" else 128.0
                    qk_, (sqt, q32), qw = self.pjbufs.acquire()
                    sqo = P.act(lambda e, sqt=sqt, pb=pb: e.activation(out=sqt, in_=pb, func=AF.Square), waits=[lastmm] + qw)
                    cp = P.dve(lambda e, q32=q32, pb=pb: e.tensor_copy(out=q32, in_=pb), waits=[lastmm, sqo] + qw)
                    self.projbanks.use(bk, sqo, cp)
                    yield
                    bk2, bank2, bw2 = self.projbanks.acquire()
                    pb2 = self.bank(bank2)
                    red = self.blk64 if typ == "diff" else self.ones
                    ssm = P.pe(lambda e, pb2=pb2, red=red, sqt=sqt: e.matmul(pb2, lhsT=red, rhs=sqt, start=True, stop=True), waits=[sqo] + bw2)
                    t2k, rt, t2w = self.tmp(F32)
                    r1 = P.act(lambda e, rt=rt, pb2=pb2, dd=dd: e.activation(out=rt, in_=pb2, func=AF.Ln, bias=self.epsd_col[dd]), waits=[ssm] + t2w)
                    r2 = P.act(lambda e, rt=rt: e.activation(out=rt, in_=rt, func=AF.Exp, scale=-0.5), waits=[r1])
                    self.projbanks.use(bk2, r1)
                    if typ == "diff":
                        gcol = pm[:, l * 8 + (0 if which == "q" else 1):l * 8 + (0 if which == "q" else 1) + 1]
                        d1 = (hb["R0"] if which == "q" else hb["R1"])
                        d2 = (hb["R2"] if which == "q" else hb["R3"])
                        e1 = P.dve(lambda e, d1=d1, q32=q32, gcol=gcol, rt=rt, sl=sl: e.scalar_tensor_tensor(
                            out=d1[0:64, sl], in0=q32[0:64, :], scalar=gcol[0:64, :], in1=rt[0:64, :], op0=ALU.mult, op1=ALU.mult), waits=[r2, cp] + z)
                        e2 = P.dve(lambda e, d2=d2, q32=q32, gcol=gcol, rt=rt, sl=sl: e.scalar_tensor_tensor(
                            out=d2[64:128, sl], in0=q32[64:128, :], scalar=gcol[64:128, :], in1=rt[64:128, :], op0=ALU.mult, op1=ALU.mult), waits=[r2, cp] + z)
                        self.tmps.use(t2k, r1, r2, e1, e2)
                        self.pjbufs.use(qk_, sqo, cp, ssm, e1, e2)
                        done += [e1, e2]
                    else:
                        gcol = pm[:, l * 8 + (2 if which == "q" else 3):l * 8 + (2 if which == "q" else 3) + 1]
                        d1 = (hb["R0"] if which == "q" else hb["R1"])
                        e1 = P.dve(lambda e, d1=d1, q32=q32, gcol=gcol, rt=rt, sl=sl: e.scalar_tensor_tensor(
                            out=d1[:, sl], in0=q32, scalar=gcol, in1=rt, op0=ALU.mult, op1=ALU.mult), waits=[r2, cp] + first_writes)
                        self.tmps.use(t2k, r1, r2, e1)
                        self.pjbufs.use(qk_, sqo, cp, ssm, e1)
                        done += [e1]
                    continue
                yield
            self.wring.use(wk, lastmm)
        wk, wp, wop = self.wfetch(self.w_in[l, pvn], KC)
        lastmm = None
        for g4 in range(4):
            bk, bank_i, bw = self.projbanks.acquire()
            pb = self.bank(bank_i)
            for t in range(4):
                tt = g4 * 4 + t
                for kc in range(KC):
                    lastmm = P.pe(lambda e, pb=pb, wp=wp, kc=kc, t=t, tt=tt: e.matmul(pb[:, t * 128:(t + 1) * 128], lhsT=self.A[:, kc, tt * 128:(tt + 1) * 128],
                                                                                     rhs=wp[:, kc, :], start=(kc == 0), stop=(kc == KC - 1)), waits=[wop, self.norm_done[g4]] + bw)
            dst = hb["V"][:, g4 * 4:(g4 + 1) * 4, :]
            ep = P.dve(lambda e, dst=dst, pb=pb: e.tensor_copy(out=dst, in_=pb.rearrange("p (t d) -> p t d", t=4)), waits=[lastmm] + first_writes)
            self.projbanks.use(bk, ep)
            done.append(ep)
            yield
        self.wring.use(wk, lastmm)
        self.proj_done[par] = done

    def fox_prep(self, l):
        P = self.P
        wk, wp, wop = self.wfetch(self.w_in[l, 48], KC)
        negb = self.pmisc[:, l * 8 + 6:l * 8 + 7]
        prev_cl = None
        prev_scan = None
        stores = []
        lastmm = None
        NP = 32
        if not hasattr(self, "foxhl"):
            self.foxhl = Slots([self.view(self.rb_off + 4096 + i * 4096, 4096, BF16) for i in range(2)])
        for tb in range(NTB):
            bk, bank_i, bw = self.projbanks.acquire()
            pb = self.bank(bank_i)
            for kc in range(KC):
                lastmm = P.pe(lambda e, pb=pb, wp=wp, kc=kc, tb=tb: e.matmul(pb, lhsT=wp[:, kc, :], rhs=self.A[:, kc, tb * TB:(tb + 1) * TB],
                                                                            start=(kc == 0), stop=(kc == KC - 1)), waits=[wop, self.norm_done[tb]] + bw)
            k1, et, w1 = self.tmp(F32)
            e1 = P.act(lambda e, et=et, pb=pb: e.activation(out=et[0:NP, :], in_=pb[0:NP, :], func=AF.Exp, bias=negb[0:NP, :], scale=-1.0), waits=[lastmm] + w1)
            self.projbanks.use(bk, e1)
            e2 = P.act(lambda e, et=et: e.activation(out=et[0:NP, :], in_=et[0:NP, :], func=AF.Ln, bias=self.one_col[0:NP, :]), waits=[e1])
            k2, cl, w2 = self.tmp(F32)
            init = 0.0 if prev_cl is None else prev_cl[0:NP, TB - 1:TB]
            sc = P.dve(lambda e, cl=cl, et=et, init=init: e.tensor_tensor_scan(out=cl[0:NP, :], data0=self.onesf[0:NP, :], data1=et[0:NP, :], initial=init,
                                                                               op0=ALU.mult, op1=ALU.add), waits=[e2, prev_scan] + w2)
            self.tmps.use(k1, e1, e2, sc)
            k3, hl, w3 = self.foxhl.acquire()
            hl4 = hl.rearrange("p (a t) -> p a t", a=4)
            k4, rem, w4 = self.tmp(F32)
            h1 = P.dve(lambda e, hl4=hl4, cl=cl: e.tensor_copy(out=hl4[0:NP, 0, :], in_=cl[0:NP, :]), waits=[sc] + w3)
            h2 = P.dve(lambda e, hl4=hl4, cl=cl, rem=rem: e.tensor_tensor(out=rem[0:NP, :], in0=cl[0:NP, :], in1=hl4[0:NP, 0, :], op=ALU.subtract), waits=[h1] + w4)
            h3 = P.dve(lambda e, hl4=hl4, rem=rem: e.tensor_copy(out=hl4[0:NP, 1, :], in_=rem[0:NP, :]), waits=[h2])
            h4 = P.dve(lambda e, hl4=hl4: e.tensor_scalar(out=hl4[0:NP, 2:4, :], in0=hl4[0:NP, 0:2, :], scalar1=-1.0, scalar2=None, op0=ALU.mult), waits=[h3])
            s1 = self.P.op("sp", lambda e, hl4=hl4, tb=tb: e.dma_start(out=self.fox_d[:, 0, 2:4, tb * TB:(tb + 1) * TB], in_=hl4[0:5, 0:2, :]),
                           waits=[h4, self.fox_init_op], dsem=self.fox_sem)
            s2 = self.P.op("sp", lambda e, hl4=hl4, tb=tb: e.dma_start(out=self.fox_d[:, 1, 0:2, tb * TB:(tb + 1) * TB], in_=hl4[0:5, 2:4, :]),
                           waits=[h4], dsem=self.fox_sem)
            self.foxhl.use(k3, h1, h2, h3, h4, s1, s2)
            self.tmps.use(k4, h2, h3)
            if prev_cl is not None:
                self.tmps.use(self.prev_cl_k, sc)
            self.prev_cl_k = k2
            self.tmps.use(k2, sc, h1, h2)
            prev_cl = cl
            prev_scan = sc
            stores = [s1, s2]
        self.wring.use(wk, lastmm)
        self.fox_rows_op = stores[-1]

    def run_deferred(self, n):
        while n > 0 and self.dq:
            self.dq.pop(0)[1]()
            n -= 1

    def flush_upto(self, tag):
        while self.dq and self.dq[0][0] <= tag:
            self.dq.pop(0)[1]()

    def defer(self, fn):
        self.dq.append((self.grp, fn))

    def att_gen(self, l, s, hidx, par):
        P = self.P
        typ, hi = HEADS[hidx]
        hb = self.hb[par]
        pdone = self.proj_done[par]
        pm = self.pmisc
        if typ == "diff":
            maps = [(hb["R0"], hb["R1"]), (hb["R2"], hb["R3"])]
        else:
            maps = [(hb["R0"], hb["R1"])]
        V = hb["V"]
        sbanks = self.sbanks
        if hidx >= 2 and self.head_last_grp[hidx - 2] is not None:
            self.flush_upto(self.head_last_grp[hidx - 2])
        ok, orow, ow = self.orows.acquire()
        orow_first = list(ow)
        last_pe = None
        o_parts = []
        acc = self.acc_bufs[0]
        Ob = self.bank(4)
        Db = self.bank(5)
        LAG = 3
        LB, LC = 2, 3
        for qb in range(4):
            q0 = qb * TB
            mres = []
            for mi, (Q, Kt) in enumerate(maps):
                tiles = list(range(4 * qb + 4))
                if typ == "sb":
                    tiles = tiles[::-1]
                    mz = P.dve(lambda e: e.memset(acc[:, :], 0.0), waits=self.acc_users)
                    self.acc_users = [mz]
                nt = len(tiles)
                stB, stX, stC, xres = [], [], [], {}
                lastpv = None
                lastdn = None
                accw_o = self.o_free
                accw_d = self.d_free
                for ti, kt in enumerate(tiles):
                    j = kt - 4 * qb
                    diag = j >= 0
                    c0 = 128 * j if j > 0 else 0
                    cs = slice(c0, TB)
                    ks = slice(kt * 128, (kt + 1) * 128)
                    qs = slice(q0 + c0, q0 + TB)
                    sk, sbi, sw = sbanks.acquire()
                    Sb = self.bank(sbi)
                    simple = (typ == "diff") and (not diag)
                    mm = P.pe(lambda e, Sb=Sb, Kt=Kt, Q=Q, cs=cs, ks=ks, qs=qs, simple=simple: e.matmul(Sb[:, cs], lhsT=Kt[:, ks], rhs=Q[:, qs], start=True, stop=simple),
                              waits=sw + pdone)
                    if typ == "fox":
                        LT, RT = hb["R2"], hb["R3"]
                        mm = P.pe(lambda e, Sb=Sb, LT=LT, RT=RT, cs=cs, ks=ks, qs=qs, diag=diag: e.matmul(Sb[:, cs], lhsT=LT[:, ks], rhs=RT[:, qs], start=False, stop=(not diag)))
                    if diag:
                        ds = slice(c0, c0 + 128)
                        if typ == "sb":
                            mm = P.pe(lambda e, Sb=Sb, ds=ds: e.matmul(Sb[:, ds], lhsT=self.ident, rhs=self.msb, start=False, stop=False))
                        elif typ == "fox":
                            mm = P.pe(lambda e, Sb=Sb, ds=ds: e.matmul(Sb[:, ds], lhsT=self.ident, rhs=self.mfox, start=False, stop=True))
                        else:
                            dh = self.cbf[:, CB_DCORR + hi * 256:CB_DCORR + hi * 256 + 128]
                            dl = self.cbf[:, CB_DCORR + hi * 256 + 128:CB_DCORR + hi * 256 + 256]
                            P.pe(lambda e, Sb=Sb, ds=ds, dh=dh: e.matmul(Sb[:, ds], lhsT=self.ident, rhs=dh, start=False, stop=False))
                            mm = P.pe(lambda e, Sb=Sb, ds=ds, dl=dl: e.matmul(Sb[:, ds], lhsT=self.ident, rhs=dl, start=False, stop=True))
                    if typ != "sb":
                        pk_, pT, pw = self.pTs.acquire()
                        ex = P.act(lambda e, pT=pT, Sb=Sb, cs=cs: e.activation(out=pT[:, cs], in_=Sb[:, cs], func=AF.Exp), waits=[mm] + pw)
                        sbanks.use(sk, ex)

                        def fC(pT=pT, cs=cs, kt=kt, ti=ti, ex=ex, pk_=pk_, nt=nt, accw_o=accw_o, accw_d=accw_d):
                            pv = P.pe(lambda e: e.matmul(Ob[:, cs], lhsT=V[:, kt, :], rhs=pT[:, cs], start=(ti == 0), stop=(ti == nt - 1)),
                                      waits=[ex] + (accw_o if ti == 0 else []))
                            dn = P.pe(lambda e: e.matmul(Db[:, cs], lhsT=self.ones, rhs=pT[:, cs], start=(ti == 0), stop=(ti == nt - 1)),
                                      waits=(accw_d if ti == 0 else []))
                            self.pTs.use(pk_, dn)
                            return pv, dn
                        stC.append(fC)
                        if ti >= LAG:
                            lastpv, lastdn = stC[ti - LAG]()
                    else:
                        ek, e32, ew = self.tmp(F32)
                        spk, spb, spw = self.sps.acquire()
                        x1 = P.act(lambda e, e32=e32, Sb=Sb, cs=cs: e.activation(out=e32[:, cs], in_=Sb[:, cs], func=AF.Exp), waits=[mm] + ew)

                        def fB(Sb=Sb, spb=spb, cs=cs, ti=ti, spk=spk, x2h=None):
                            x2 = x2h[0]
                            b1 = P.pe(lambda e: e.matmul(Sb[:, cs], lhsT=self.negtri, rhs=spb[:, cs], start=False, stop=(ti == 0)), waits=[x2])
                            if ti > 0:
                                b1 = P.pe(lambda e: e.matmul(Sb[:, cs], lhsT=self.negones, rhs=acc[:, cs], start=False, stop=True), waits=self.acc_users)
                            au = P.dve(lambda e: e.tensor_tensor(out=acc[:, cs], in0=acc[:, cs], in1=spb[:, cs], op=ALU.add), waits=[x2, b1] + self.acc_users)
                            self.acc_users = [au]
                            self.sps.use(spk, b1, au)
                            return b1

                        def fX(b1, Sb=Sb, cs=cs, sk=sk):
                            pk_, pT, pw = self.pTs.acquire()
                            ex = P.act(lambda e: e.activation(out=pT[:, cs], in_=Sb[:, cs], func=AF.Exp), waits=[b1] + pw)
                            sbanks.use(sk, ex)
                            return pk_, pT, ex

                        def fC(pk_, pT, ex, cs=cs, kt=kt, ti=ti, nt=nt, accw_o=accw_o):
                            pv = P.pe(lambda e: e.matmul(Ob[:, cs], lhsT=V[:, kt, :], rhs=pT[:, cs], start=(ti == 0), stop=(ti == nt - 1)),
                                      waits=[ex] + (accw_o if ti == 0 else []))
                            self.pTs.use(pk_, pv)
                            return pv, None
                        x2h = [None]
                        stB.append((fB, x2h))
                        stX.append(fX)
                        stC.append(fC)
                        if ti >= LB:
                            fb, xh = stB[ti - LB]
                            xres[ti - LB] = stX[ti - LB](fb(x2h=xh))
                        x2 = P.act(lambda e, e32=e32, spb=spb, cs=cs: e.activation(out=spb[:, cs], in_=e32[:, cs], func=AF.Ln, bias=self.one_col), waits=[x1] + spw)
                        x2h[0] = x2
                        self.tmps.use(ek, x1, x2)
                        if ti >= LC:
                            lastpv, _ = stC[ti - LC](*xres[ti - LC])
                    self.run_deferred(2)
                    yield
                if typ != "sb":
                    for r in range(max(0, nt - LAG), nt):
                        lastpv, lastdn = stC[r]()
                else:
                    for i in range(nt, nt + LC):
                        if 0 <= i - LB < nt:
                            fb, xh = stB[i - LB]
                            xres[i - LB] = stX[i - LB](fb(x2h=xh))
                        if 0 <= i - LC < nt:
                            lastpv, _ = stC[i - LC](*xres[i - LC])
                last_pe = lastpv if lastdn is None else lastdn
                self.grp += 1
                self.flush_upto(self.grp - 2)
                osl = orow[:, q0:q0 + TB]
                if typ == "sb":
                    ep = P.dve(lambda e, osl=osl: e.tensor_copy(out=osl, in_=Ob), waits=[lastpv] + orow_first)
                    self.o_free = [ep]
                    o_parts.append(ep)
                else:
                    gp = self.epi_par
                    self.epi_par ^= 1
                    o32, d32 = self.epi_bufs[gp]
                    users = self.epi_users[gp]
                    cpo = P.dve(lambda e, o32=o32: e.tensor_copy(out=o32, in_=Ob), waits=[lastpv] + users)
                    cpd = P.act(lambda e, d32=d32: e.activation(out=d32, in_=Db, func=AF.Copy), waits=[lastdn] + users)
                    self.o_free = [cpo]
                    self.d_free = [cpd]
                    rec = []
                    state = {"ops": [cpo, cpd]}
                    self.epi_users[gp] = state["ops"]
                    for pc in range(4):
                        psl = slice(pc * 128, (pc + 1) * 128)

                        def frec(psl=psl, d32=d32, cpd=cpd, state=state):
                            r = P.dve(lambda e: e.reciprocal(out=d32[:, psl], in_=d32[:, psl]), waits=[cpd])
                            state["ops"].append(r)
                            state["lastrec"] = r
                        self.defer(frec)
                    if typ == "fox":
                        def fmul(o32=o32, d32=d32, cpo=cpo, osl=osl, state=state):
                            ep = P.dve(lambda e: e.tensor_tensor(out=osl, in0=o32, in1=d32, op=ALU.mult), waits=[cpo, state["lastrec"]] + orow_first)
                            state["ops"].append(ep)
                            o_parts.append(ep)
                        self.defer(fmul)
                    else:
                        om = self.dO[mi]

                        def fmul(o32=o32, d32=d32, cpo=cpo, om=om, mi=mi, state=state):
                            r2 = P.dve(lambda e: e.tensor_tensor(out=om, in0=o32, in1=d32, op=ALU.mult), waits=[cpo, state["lastrec"]] + self.dO_users[mi])
                            state["ops"].append(r2)
                            self.dO_users[mi] = [r2]
                            self.dO_last[mi] = r2
                        self.defer(fmul)
            if typ == "diff":
                o1, o2 = self.dO
                neglam = pm[:, l * 8 + 5:l * 8 + 6]
                gsub = pm[:, l * 8 + 4:l * 8 + 5]
                osl = orow[:, q0:q0 + TB]
                st = {}

                def f1(st=st, neglam=neglam):
                    st["c1"] = P.dve(lambda e: e.scalar_tensor_tensor(out=o1, in0=o2, scalar=neglam, in1=o1, op0=ALU.mult, op1=ALU.add),
                                     waits=[self.dO_last[0], self.dO_last[1]])

                def f2a(st=st):
                    sqt = self.csq
                    c2 = P.act(lambda e: e.activation(out=sqt, in_=o1, func=AF.Square), waits=[st["c1"]] + self.csq_users)
                    st.update(c2=c2)

                def f2b(st=st):
                    sqt = self.csq
                    c2 = st["c2"]
                    bk2, bank2, bw2 = self.projbanks.acquire()
                    pb2 = self.bank(bank2)
                    c3 = P.pe(lambda e: e.matmul(pb2, lhsT=self.ones, rhs=sqt, start=True, stop=True), waits=[c2] + bw2)
                    c4 = P.act(lambda e: e.activation(out=o2, in_=pb2, func=AF.Ln, bias=self.eps_col, scale=1.0 / 128.0), waits=[c3, st["c1"]])
                    c5 = P.act(lambda e: e.activation(out=o2, in_=o2, func=AF.Exp, scale=-0.5), waits=[c4])
                    self.projbanks.use(bk2, c4)
                    self.csq_users = [c2, c3]
                    st.update(c4=c4, c5=c5)

                def nop():
                    pass

                def f3(st=st, osl=osl, gsub=gsub):
                    ep = P.dve(lambda e: e.scalar_tensor_tensor(out=osl, in0=o1, scalar=gsub, in1=o2, op0=ALU.mult, op1=ALU.mult), waits=[st["c5"]] + orow_first)
                    self.dO_users[0] = [st["c1"], st["c2"], ep]
                    self.dO_users[1] = [st["c1"], st["c4"], st["c5"], ep]
                    o_parts.append(ep)
                for fn in (f1, f2a, nop, f2b, nop, nop, nop, f3):
                    self.defer(fn)
        def fstore():
            st = self.sp_dma(self.oT_d[hidx], orow, self.st_sems[ok], waits=o_parts)
            self.orows.use(ok, st)
        self.defer(fstore)
        self.head_last_grp[hidx] = self.grp
        self.att_last_pe[par] = [last_pe]
        yield

    def attention_phase(self, l, s):
        P = self.P
        self.projbanks = Slots([6, 7])
        self.sbanks = Slots([0, 1, 2, 3])
        self.pTs = Slots(self.pT_slots)
        self.sps = Slots(self.sp_slots)
        self.orows = Slots(self.orow_slots)
        self.acc_users = []
        self.dO_users = [[], []]
        self.dO_last = [None, None]
        self.o_free = []
        self.d_free = []
        self.epi_par = 0
        self.epi_users = [[], []]
        self.dq = []
        self.grp = 0
        self.head_last_grp = [None] * 16
        self.csq = self.view(self.offB + 56 * 1024, 1024, BF16)
        self.csq_users = []
        K = 1024
        ro = self.rb_off
        self.pjbufs = Slots([(self.view(ro, K, BF16), self.view(ro + 2 * K, 2 * K, F32)),
                             (self.view(ro + K, K, BF16), self.view(ro + 24 * K, 2 * K, F32))])
        self.att_last_pe = [[], []]
        self.proj_done = [None, None]
        self.fox_prep(l)
        g = self.proj_gen(l, 0, 0, [])
        for _ in g:
            pass
        for h in range(16):
            par = h % 2
            ag = self.att_gen(l, s, h, par)
            pg = self.proj_gen(l, h + 1, 1 - par, self.att_last_pe[1 - par]) if h + 1 < 16 else iter(())
            cnt = 0
            for _ in ag:
                cnt += 1
                if cnt % 3 == 0:
                    next(pg, None)
            for _ in pg:
                pass
        self.run_deferred(100000)
        P.barrier()

    def gate_phase(self, l):
        P = self.P
        K = 1024
        lds = []
        for c in range(KC):
            lds.append(self.bulk_load(self.B[:, c, :], self.oT_d[c], c))
        ro = self.rb_off
        ysum = self.view(ro, 8 * K, F32)
        gsb = Slots([self.view(ro + 8 * K + i * 4 * K, 4 * K, BF16) for i in range(2)])
        yrows = Slots([self.view(ro + 16 * K + i * 4 * K, 4 * K, BF16) for i in range(2)])
        gbanks = Slots([0, 1, 2])
        bbanks = Slots([3, 4, 5])
        branches = [(self.w_bsb, 6, 0), (self.w_bdf, 5, 6), (self.w_bfx, 5, 11)]
        ysum_users = []
        for m in range(KC):
            yk, yrow, yw = yrows.acquire()
            for i in range(3):
                wk, wp, wop = self.wfetch(self.w_gate[l, i * 16 + m], KC)
                gk, gs, gw = gsb.acquire()
                lastmm = None
                sigs = []
                for tb in range(NTB):
                    bk, bi, bw = gbanks.acquire()
                    pb = self.bank(bi)
                    for kc in range(KC):
                        lastmm = P.pe(lambda e, pb=pb, wp=wp, kc=kc, tb=tb: e.matmul(pb, lhsT=wp[:, kc, :], rhs=self.A[:, kc, tb * TB:(tb + 1) * TB],
                                                                                    start=(kc == 0), stop=(kc == KC - 1)), waits=[wop] + bw)
                    sg = P.act(lambda e, gs=gs, pb=pb, tb=tb: e.activation(out=gs[:, tb * TB:(tb + 1) * TB], in_=pb, func=AF.Sigmoid), waits=[lastmm] + gw)
                    gbanks.use(bk, sg)
                    sigs.append(sg)
                self.wring.use(wk, lastmm)
                wsrc, kcn, coff = branches[i]
                wk, wp, wop = self.wfetch(wsrc[l, m], kcn)
                for tb in range(NTB):
                    bk, bi, bw = bbanks.acquire()
                    pb = self.bank(bi)
                    for kc in range(kcn):
                        lastmm = P.pe(lambda e, pb=pb, wp=wp, kc=kc, tb=tb, coff=coff, kcn=kcn: e.matmul(pb, lhsT=wp[:, kc, :], rhs=self.B[:, coff + kc, tb * TB:(tb + 1) * TB],
                                                                                                        start=(kc == 0), stop=(kc == kcn - 1)), waits=[wop] + bw + [lds[coff + kc]])
                    sl = slice(tb * TB, (tb + 1) * TB)
                    if i == 0:
                        d = P.dve(lambda e, pb=pb, gs=gs, sl=sl: e.tensor_tensor(out=ysum[:, sl], in0=pb, in1=gs[:, sl], op=ALU.mult), waits=[lastmm, sigs[tb]] + ysum_users)
                        bbanks.use(bk, d)
                    else:
                        tk, tt, tw = self.tmp(F32)
                        d0 = P.dve(lambda e, pb=pb, gs=gs, sl=sl, tt=tt: e.tensor_tensor(out=tt, in0=pb, in1=gs[:, sl], op=ALU.mult), waits=[lastmm, sigs[tb]] + tw)
                        bbanks.use(bk, d0)
                        if i == 1:
                            d = P.dve(lambda e, sl=sl, tt=tt: e.tensor_tensor(out=ysum[:, sl], in0=ysum[:, sl], in1=tt, op=ALU.add), waits=[d0])
                        else:
                            d = P.dve(lambda e, sl=sl, tt=tt, yrow=yrow: e.tensor_tensor(out=yrow[:, sl], in0=ysum[:, sl], in1=tt, op=ALU.add), waits=[d0] + yw)
                            ysum_users = [d]
                        self.tmps.use(tk, d0, d)
                    gsb.use(gk, d)
                self.wring.use(wk, lastmm)
            st = self.sp_dma(self.yT_d[m], yrow, self.st_sems[yk], waits=[d])
            yrows.use(yk, st)
        P.barrier()

    def out_phase(self, l, src, dst, gcol_base):
        P = self.P
        K = 1024
        lds = []
        for c in range(KC):
            lds.append(self.bulk_load(self.A[:, c, :], self.yT_d[c], c))
        ro = self.rb_off
        xst = Slots([self.view(ro + i * 8 * K, 8 * K, F32) for i in range(2)])
        sqs = Slots([self.view(ro + 16 * K + i * 4 * K, 4 * K, BF16) for i in range(2)])
        rs = self.view(self.tmp_off, 8 * K, F32)
        banks = Slots([0, 1, 2, 3])
        last_mm = [None] * NTB
        xb_ops = []
        pending = []
        for m in range(KC):
            xk, xs, xw = xst.acquire()
            xl = self.sp_dma(xs, src[m], self.xs_sems[xk], waits=xw)
            wk, wp, wop = self.wfetch(self.w_out[l, m], KC)
            adds = []
            for tb in range(NTB):
                bk, bi, bw = banks.acquire()
                pb = self.bank(bi)
                for kc in range(KC):
                    lastmm = P.pe(lambda e, pb=pb, wp=wp, kc=kc, tb=tb: e.matmul(pb, lhsT=wp[:, kc, :], rhs=self.A[:, kc, tb * TB:(tb + 1) * TB],
                                                                                start=(kc == 0), stop=(kc == KC - 1)), waits=[wop] + bw + [lds[kc]])
                sl = slice(tb * TB, (tb + 1) * TB)
                a = P.dve(lambda e, pb=pb, xs=xs, sl=sl: e.tensor_tensor(out=xs[:, sl], in0=pb, in1=xs[:, sl], op=ALU.add), waits=[lastmm, xl])
                banks.use(bk, a)
                adds.append(a)
            self.wring.use(wk, lastmm)
            while pending:
                pending.pop(0)()
            st = self.sp_dma(dst[m], xs, self.st_sems[xk], waits=adds)
            k2, sq, w2 = sqs.acquire()
            sqo = P.act(lambda e, sq=sq, xs=xs: e.activation(out=sq, in_=xs, func=AF.Square), waits=adds + w2)
            g = self.pvec[:, gcol_base + m:gcol_base + m + 1]
            xb = P.dve(lambda e, m=m, xs=xs, g=g: e.tensor_scalar(out=self.B[:, m, :], in0=xs, scalar1=g, scalar2=None, op0=ALU.mult), waits=adds)
            xb_ops.append(xb)
            def ssmm(sq=sq, m=m, sqo=sqo, k2=k2):
                for tb in range(NTB):
                    last_mm[tb] = P.pe(lambda e, tb=tb: e.matmul(self.bank(4 + tb), lhsT=self.ones, rhs=sq[:, tb * TB:(tb + 1) * TB],
                                                                 start=(m == 0), stop=(m == KC - 1)), waits=[sqo])
                sqs.use(k2, last_mm[NTB - 1])
            pending.append(ssmm)
            xst.use(xk, st, sqo, xb)
        while pending:
            pending.pop(0)()
        self.finish_norm(self.B, rs, 4, last_mm, xb_ops)

    def ffn_up_phase(self, l, X):
        P = self.P
        K = 1024
        oB = self.offB
        ro = self.rb_off
        srows = Slots([self.view(ro + i * 4 * K, 4 * K, BF16) for i in range(2)])
        hrows = Slots([self.view(ro + 8 * K + i * 4 * K, 4 * K, BF16) for i in range(2)])
        gb = Slots([0, 1, 2, 3])
        ub = Slots([4, 5, 6, 7])
        for m in range(FC):
            wk, wp, wop = self.wfetch(self.w_fg[l, m], KC)
            sk, sr, sw = srows.acquire()
            sil = []
            for tb in range(NTB):
                bk, bi, bw = gb.acquire()
                pb = self.bank(bi)
                for kc in range(KC):
                    lastmm = P.pe(lambda e, pb=pb, wp=wp, kc=kc, tb=tb: e.matmul(pb, lhsT=wp[:, kc, :], rhs=X[:, kc, tb * TB:(tb + 1) * TB],
                                                                                start=(kc == 0), stop=(kc == KC - 1)), waits=[wop, self.norm_done[tb]] + bw)
                sg = P.act(lambda e, sr=sr, pb=pb, tb=tb: e.activation(out=sr[:, tb * TB:(tb + 1) * TB], in_=pb, func=AF.Silu), waits=[lastmm] + sw)
                gb.use(bk, sg)
                sil.append(sg)
            self.wring.use(wk, lastmm)
            wk, wp, wop = self.wfetch(self.w_fu[l, m], KC)
            hk, hr, hw = hrows.acquire()
            muls = []
            for tb in range(NTB):
                bk, bi, bw = ub.acquire()
                pb = self.bank(bi)
                for kc in range(KC):
                    lastmm = P.pe(lambda e, pb=pb, wp=wp, kc=kc, tb=tb: e.matmul(pb, lhsT=wp[:, kc, :], rhs=X[:, kc, tb * TB:(tb + 1) * TB],
                                                                                start=(kc == 0), stop=(kc == KC - 1)), waits=[wop, self.norm_done[tb]] + bw)
                sl = slice(tb * TB, (tb + 1) * TB)
                d = P.dve(lambda e, pb=pb, sr=sr, hr=hr, sl=sl: e.tensor_tensor(out=hr[:, sl], in0=pb, in1=sr[:, sl], op=ALU.mult), waits=[lastmm, sil[tb]] + hw)
                ub.use(bk, d)
                muls.append(d)
            self.wring.use(wk, lastmm)
            srows.use(sk, *muls)
            st = self.sp_dma(self.hT_d[m], hr, self.st_sems[hk], waits=muls)
            hrows.use(hk, st)
        P.barrier()

    def ffn_down_phase(self, l, src, dst, next_g=None):
        P = self.P
        K = 1024
        HT = 1024
        H = self.view(0, FC * HT * 2, BF16, "p (k t) -> p k t", k=FC)
        xoff = FC * HT * 2
        self.pre_ss = [None] * NTB
        for half in range(2):
            t0 = half * HT
            lds = []
            for c in range(FC):
                lds.append(self.bulk_load(H[:, c, :], self.hT_d[c][:, t0:t0 + HT], c))
            xst = Slots([self.view(xoff + i * 4 * K, 4 * K, F32) for i in range(2)])
            sqs = Slots([self.view(xoff + 8 * K + i * 2 * K, 2 * K, BF16) for i in range(2)])
            xbs = Slots([self.view(xoff + 12 * K + i * 2 * K, 2 * K, BF16) for i in range(2)])
            banks = Slots([0, 1, 2, 3])
            pending = []
            for m in range(KC):
                xk, xs, xw = xst.acquire()
                xl = self.sp_dma(xs, src[m][:, t0:t0 + HT], self.xs_sems[xk], waits=xw)
                parts = []
                for (k0, k1) in ((0, 16), (16, 32), (32, 44)):
                    wk, wp, wop = self.wfetch(self.w_fd[l, m][:, k0:k1, :], k1 - k0)
                    parts.append((wk, wp, wop, k0, k1))
                adds = []
                for tbh in range(2):
                    bk, bi, bw = banks.acquire()
                    pb = self.bank(bi)
                    for (wk, wp, wop, k0, k1) in parts:
                        for kc in range(k0, k1):
                            lastmm = P.pe(lambda e, pb=pb, wp=wp, kc=kc, k0=k0, tbh=tbh: e.matmul(pb, lhsT=wp[:, kc - k0, :], rhs=H[:, kc, tbh * TB:(tbh + 1) * TB],
                                                                                                 start=(kc == 0), stop=(kc == FC - 1)), waits=[wop] + bw + [lds[kc]])
                    sl = slice(tbh * TB, (tbh + 1) * TB)
                    a = P.dve(lambda e, pb=pb, xs=xs, sl=sl: e.tensor_tensor(out=xs[:, sl], in0=pb, in1=xs[:, sl], op=ALU.add), waits=[lastmm, xl])
                    banks.use(bk, a)
                    adds.append(a)
                for (wk, wp, wop, k0, k1) in parts:
                    self.wring.use(wk, lastmm)
                while pending:
                    pending.pop(0)()
                st = self.sp_dma(dst[m][:, t0:t0 + HT], xs, self.st_sems[xk], waits=adds)
                users = [st]
                if next_g is not None:
                    k2, sq, w2 = sqs.acquire()
                    sqo = P.act(lambda e, sq=sq, xs=xs: e.activation(out=sq, in_=xs, func=AF.Square), waits=adds + w2)
                    k3, xbt, w3 = xbs.acquire()
                    g = self.pvec[:, next_g + m:next_g + m + 1]
                    xb = P.dve(lambda e, xbt=xbt, xs=xs, g=g: e.tensor_scalar(out=xbt, in0=xs, scalar1=g, scalar2=None, op0=ALU.mult), waits=adds + w3)
                    st2 = self.sp_dma(self.xbT_d[m][:, t0:t0 + HT], xbt, self.xb_sems[k3], waits=[xb])
                    xbs.use(k3, st2)

                    def ssmm(sq=sq, m=m, sqo=sqo, k2=k2, half=half, sqs=sqs):
                        for tbh in range(2):
                            tb = half * 2 + tbh
                            self.pre_ss[tb] = P.pe(lambda e, tb=tb, tbh=tbh: e.matmul(self.bank(4 + tb), lhsT=self.ones, rhs=sq[:, tbh * TB:(tbh + 1) * TB],
                                                                                  start=(m == 0), stop=(m == KC - 1)), waits=[sqo])
                        sqs.use(k2, self.pre_ss[half * 2 + 1])
                    pending.append(ssmm)
                    users += [sqo, xb]
                xst.use(xk, *users)
            while pending:
                pending.pop(0)()
            P.barrier()

    def norm_phase_fast(self):
        K = 1024
        lds = []
        for c in range(KC):
            lds.append(self.bulk_load(self.A[:, c, :], self.xbT_d[c], c))
        rs = self.view(self.rb_off + 12 * K, 8 * K, F32)
        self.finish_norm(self.A, rs, 4, self.pre_ss, lds)

    def build(self):
        with ExitStack() as es:
            self.setup(es)
            P = self.P
            K = 1024
            cc = self.lamtmp
            self.eps_col = cc[:, 40:41]
            self.one_col = cc[:, 41:42]
            self.epsd_col = {64.0: cc[:, 42:43], 128.0: cc[:, 43:44]}
            P.dve(lambda e: e.memset(self.eps_col, EPS))
            P.dve(lambda e: e.memset(self.one_col, 1.0))
            P.dve(lambda e: e.memset(self.epsd_col[64.0], EPS * 64.0))
            P.dve(lambda e: e.memset(self.epsd_col[128.0], EPS * 128.0))
            self.onesf = self.view(self.rb_off + 26 * K, 2 * K, F32)
            P.dve(lambda e: e.memset(self.onesf, 1.0))
            self.init_phase()
            self.attn_setup()
            stop = self.stop_after
            done = False
            for s in range(self.n_seq):
                for l in range(self.n_layers):
                    src = self.xT[s] if l == 0 else self.xres[s]
                    lastl = (l == self.n_layers - 1)
                    base = l * PV_PER_L
                    if l == 0:
                        self.norm_phase(src, base + PV_GMIX)
                    else:
                        self.norm_phase_fast()
                    if self.dbg and s == 0 and l == 0:
                        for c in range(KC):
                            self.sp_dma(self.dbgA[c], self.A[:, c, :], self.misc_sem, waits=[x for x in self.norm_done if x is not None])
                        P.barrier()
                    if stop == "norm1":
                        done = True
                        break
                    self.attention_phase(l, s)
                    if stop == "attn":
                        done = True
                        break
                    self.gate_phase(l)
                    if stop == "gate":
                        done = True
                        break
                    self.out_phase(l, src, self.xres[s], base + PV_GFFN)
                    if stop == "out":
                        done = True
                        break
                    self.ffn_up_phase(l, self.B)
                    if stop == "ffn_up":
                        done = True
                        break
                    self.ffn_down_phase(l, self.xres[s], self.outT[s] if lastl else self.xres[s],
                                        next_g=(None if lastl else (l + 1) * PV_PER_L + PV_GMIX))
                if done:
                    break
            P.barrier()
            P.build()
        return self.nc


def _panels(w, kc, mc):
    L = w.shape[0]
    return np.ascontiguousarray(w.reshape(L, kc, 128, mc, 128).transpose(0, 3, 2, 1, 4))


def _hi_lo(v):
    hi = v.astype(NPBF)
    lo = (v - hi.astype(np.float32)).astype(NPBF)
    return hi, lo


def host_consts():
    cb = np.zeros((128, CB_N), np.float32)
    j = np.arange(128)[:, None]
    sidx = np.arange(128)[None, :]
    cb[:, CB_IDENT:CB_IDENT + 128] = np.eye(128)
    cb[:, CB_NEGTRI:CB_NEGTRI + 128] = np.where(j >= sidx, -1.0, 0.0)
    cb[:, CB_NEGONES:CB_NEGONES + 128] = -1.0
    cb[:, CB_ONES:CB_ONES + 128] = 1.0
    blk = np.zeros((128, 128))
    blk[:64, :64] = 1.0
    blk[64:, 64:] = 1.0
    cb[:, CB_BLK64:CB_BLK64 + 128] = blk
    kl = np.arange(128)[:, None]
    ql = np.arange(128)[None, :]
    cb[:, CB_MSB:CB_MSB + 128] = np.where(kl < ql, 0.0, NEG)
    cb[:, CB_MFOX:CB_MFOX + 128] = np.where(kl <= ql, 0.0, NEG)
    cbb = cb.astype(NPBF)
    for h in range(5):
        sl = SLOPES[h]
        v = np.where(ql >= kl, 0.0, np.where((kl // 64) <= (ql // 64), -2.0 * sl * (kl - ql), NEG)).astype(np.float32)
        hi, lo = _hi_lo(v)
        cbb[:, CB_DCORR + h * 256:CB_DCORR + h * 256 + 128] = hi
        cbb[:, CB_DCORR + h * 256 + 128:CB_DCORR + h * 256 + 256] = lo
    pos = np.arange(S, dtype=np.float64)
    augk = np.zeros((5, 4, S), NPBF)
    augq = np.zeros((5, 4, S), NPBF)
    for h in range(5):
        a = (SLOPES[h] * pos).astype(np.float32)
        r = (-SLOPES[h] * pos).astype(np.float32)
        ah, al = _hi_lo(a)
        rh, rl = _hi_lo(r)
        augk[h, 0] = 1.0
        augk[h, 1] = 1.0
        augk[h, 2] = ah
        augk[h, 3] = al
        augq[h, 0] = rh
        augq[h, 1] = rl
        augq[h, 2] = 1.0
        augq[h, 3] = 1.0
    foxinit = np.zeros((5, 2, 4, S), NPBF)
    foxinit[:, 0, 0:2, :] = 1.0
    foxinit[:, 1, 2:4, :] = 1.0
    return cbb, augk, augq, foxinit


def host_prep(inp):
    f = lambda k: np.asarray(inp[k], dtype=np.float32)
    L = L_ALL
    w_in = f("w_in")
    w_in_p = np.zeros((L, D, 49 * 128), np.float32)
    w_in_p[:, :, :6149] = w_in
    out = {}
    out["w_in"] = _panels(w_in_p, KC, 49)
    out["w_gate"] = _panels(f("w_gate"), KC, 48)
    out["w_bsb"] = _panels(f("w_branch_sb"), 6, 16)
    out["w_bdf"] = _panels(f("w_branch_diff"), 5, 16)
    out["w_bfx"] = _panels(f("w_branch_fox"), 5, 16)
    out["w_out"] = _panels(f("w_out"), KC, 16)
    out["w_fg"] = _panels(f("w_ff_gate"), KC, FC)
    out["w_fu"] = _panels(f("w_ff_up"), KC, FC)
    out["w_fd"] = _panels(f("w_ff_down"), FC, 16)
    pv = np.zeros((128, L * PV_PER_L), np.float32)
    for l in range(L):
        b = l * PV_PER_L
        pv[:, b + PV_GMIX:b + PV_GMIX + 16] = f("norm_mix")[l].reshape(16, 128).T
        pv[:, b + PV_GFFN:b + PV_GFFN + 16] = f("norm_ffn")[l].reshape(16, 128).T
        pv[:, b + PV_QD] = np.tile(f("q_norm_diff")[l], 2)
        pv[:, b + PV_KD] = np.tile(f("k_norm_diff")[l], 2)
        pv[:, b + PV_QF] = f("q_norm_fox")[l]
        pv[:, b + PV_KF] = f("k_norm_fox")[l]
        pv[:, b + PV_SUB] = f("sub_norm_diff")[l]
        pv[0:5, b + PV_BF] = f("b_forget")[l]
        for i, nm in enumerate(("lambda_q1", "lambda_k1", "lambda_q2", "lambda_k2")):
            pv[:, b + PV_LAM + 64 * i:b + PV_LAM + 64 * (i + 1)] = f(nm)[l][None, :]
    out["pvec"] = pv
    cbb, augk, augq, foxinit = host_consts()
    out["cbf"] = cbb
    out["augk"] = augk
    out["augq"] = augq
    out["foxinit"] = foxinit
    return out


_CACHE = {}


def kernel(**inputs):
    x = np.asarray(inputs["x"], dtype=np.float32)
    shared = host_prep(inputs)
    if "nc" not in _CACHE:
        _CACHE["nc"] = Builder().build()
    nc = _CACHE["nc"]
    in_maps = []
    for c in range(N_CORES):
        xs = x[c * SEQ_PER_CORE:(c + 1) * SEQ_PER_CORE]
        xT = np.ascontiguousarray(xs.transpose(0, 2, 1)).reshape(SEQ_PER_CORE, KC, 128, S)
        m = dict(shared)
        m["xT"] = xT
        in_maps.append(m)
    res = run_bass_kernel_spmd(nc, in_maps, core_ids=list(range(N_CORES)))
    outs = []
    for c in range(N_CORES):
        o = np.asarray(res.results[c]["outT"]).reshape(SEQ_PER_CORE, D, S).transpose(0, 2, 1)
        outs.append(o)
    return np.ascontiguousarray(np.concatenate(outs, axis=0)).astype(np.float32)
```

```python
import numpy as np
import ml_dtypes
from contextlib import ExitStack
import concourse.bass as bass
import concourse.mybir as mybir
from concourse.bass_utils import run_bass_kernel_spmd

F32 = mybir.dt.float32
BF16 = mybir.dt.bfloat16
AF = mybir.ActivationFunctionType
ALU = mybir.AluOpType
AX = mybir.AxisListType
NPBF = ml_dtypes.bfloat16

D = 2048
KC = 16
S = 2048
TB = 512
NTB = 4
DFF = 5632
FC = 44
L_ALL = 4
EPS = 1e-6
NEG = -30000.0
N_CORES = 8
SEQ_PER_CORE = 2
HEADS = [("sb", i) for i in range(6)] + [("diff", i) for i in range(5)] + [("fox", i) for i in range(5)]
SLOPES = [2.0 ** (-8.0 * (i + 1) / 5.0) for i in range(5)]
ENGS = ("pe", "act", "dve", "pool", "sp")
SAME_ENGINE_SYNC = True

PV_GMIX = 0
PV_GFFN = 16
PV_QD = 32
PV_KD = 33
PV_QF = 34
PV_KF = 35
PV_SUB = 36
PV_BF = 37
PV_LAM = 38
PV_PER_L = 38 + 256
CB_IDENT = 0
CB_NEGTRI = 128
CB_NEGONES = 256
CB_ONES = 384
CB_BLK64 = 512
CB_MSB = 640
CB_MFOX = 768
CB_DCORR = 896
CB_N = 896 + 1280


class Op:
    __slots__ = ("eng", "fn", "waits", "sig", "dsem", "val")


class Prog:
    def __init__(self, nc, es):
        self.nc = nc
        self.es = es
        self.q = {e: [] for e in ENGS}
        self.semh = {}
        for e in ENGS:
            self.semh[e] = es.enter_context(nc.semaphore("s_" + e))
        self.last = {e: None for e in ENGS}
        self.sp_dma_last = {}
        self.nsem = 0

    def new_sem(self, name):
        self.semh[name] = self.es.enter_context(self.nc.semaphore(name))
        return name

    def op(self, eng, fn, waits=(), dsem=None):
        o = Op()
        o.eng, o.fn, o.dsem, o.sig, o.val = eng, fn, dsem, False, None
        ws = []
        for w in waits:
            if w is None:
                continue
            if w.dsem is None and w.eng == eng:
                if eng == "pe" or not SAME_ENGINE_SYNC:
                    continue
            if w.dsem is None:
                w.sig = True
            ws.append(w)
        o.waits = ws
        self.q[eng].append(o)
        if fn is not None:
            self.last[eng] = o
            if dsem is not None and eng == "sp":
                self.sp_dma_last[dsem] = o
        return o

    def pe(self, fn, waits=()):
        return self.op("pe", fn, waits)

    def act(self, fn, waits=()):
        return self.op("act", fn, waits)

    def dve(self, fn, waits=()):
        return self.op("dve", fn, waits)

    def barrier(self):
        lasts = [self.last[e] for e in ("pe", "act", "dve") if self.last[e] is not None]
        dm = list(self.sp_dma_last.values())
        for e in ("pe", "act", "dve", "sp"):
            self.op(e, None, waits=[w for w in lasts if w.eng != e] + dm)

    def build(self):
        cnt = {}
        for e in ENGS:
            c = 0
            for o in self.q[e]:
                if o.fn is None:
                    continue
                if o.dsem is not None:
                    cnt[o.dsem] = cnt.get(o.dsem, 0) + 16
                    o.val = (o.dsem, cnt[o.dsem])
                elif o.sig:
                    c += 1
                    o.val = (e, c)
        semh = self.semh

        def run_engine(name, e):
            waited = {}
            for o in self.q[name]:
                for w in o.waits:
                    sn, v = w.val
                    if waited.get(sn, 0) >= v:
                        continue
                    waited[sn] = v
                    e.wait_ge(semh[sn], v)
                if o.fn is None:
                    continue
                ins = o.fn(e)
                if o.dsem is not None:
                    ins.then_inc(semh[o.dsem], 16)
                elif o.sig:
                    ins.then_inc(semh[name], 1)

        with self.nc.Block() as block:
            @block.tensor
            def _(e):
                run_engine("pe", e)

            @block.scalar
            def _(e):
                run_engine("act", e)

            @block.vector
            def _(e):
                run_engine("dve", e)

            @block.gpsimd
            def _(e):
                run_engine("pool", e)

            @block.sync
            def _(e):
                run_engine("sp", e)


class Slots:
    def __init__(self, items):
        self.items = list(items)
        self.n = len(self.items)
        self.i = 0
        self.users = [[] for _ in range(self.n)]

    def acquire(self):
        k = self.i
        self.i = (self.i + 1) % self.n
        w = self.users[k]
        self.users[k] = []
        return k, self.items[k], w

    def use(self, k, *ops):
        for o in ops:
            if o is not None:
                self.users[k].append(o)


class Builder:
    def __init__(self, n_layers=L_ALL, n_seq=SEQ_PER_CORE, dbg=False, stop_after=None):
        self.n_layers = n_layers
        self.n_seq = n_seq
        self.dbg = dbg
        self.stop_after = stop_after
        self.nc = bass.Bass("TRN2", target_bir_lowering=False)

    def view(self, off, nbytes, dtype=BF16, pattern=None, **kw):
        a = self.arena[:, off // 2:(off + nbytes) // 2]
        if dtype == F32:
            a = a.bitcast(F32)
        if pattern is not None:
            a = a.rearrange(pattern, **kw)
        return a

    def setup(self, es):
        nc = self.nc
        L, NS = self.n_layers, self.n_seq
        dt = nc.dram_tensor
        self.xT = dt("xT", [NS, KC, 128, S], F32, kind="ExternalInput").ap()
        self.outT = dt("outT", [NS, KC, 128, S], F32, kind="ExternalOutput").ap()
        self.w_in = dt("w_in", [L, 49, 128, KC, 128], F32, kind="ExternalInput").ap()
        self.w_gate = dt("w_gate", [L, 48, 128, KC, 128], F32, kind="ExternalInput").ap()
        self.w_bsb = dt("w_bsb", [L, 16, 128, 6, 128], F32, kind="ExternalInput").ap()
        self.w_bdf = dt("w_bdf", [L, 16, 128, 5, 128], F32, kind="ExternalInput").ap()
        self.w_bfx = dt("w_bfx", [L, 16, 128, 5, 128], F32, kind="ExternalInput").ap()
        self.w_out = dt("w_out", [L, 16, 128, KC, 128], F32, kind="ExternalInput").ap()
        self.w_fg = dt("w_fg", [L, FC, 128, KC, 128], F32, kind="ExternalInput").ap()
        self.w_fu = dt("w_fu", [L, FC, 128, KC, 128], F32, kind="ExternalInput").ap()
        self.w_fd = dt("w_fd", [L, 16, 128, FC, 128], F32, kind="ExternalInput").ap()
        self.pvec_d = dt("pvec", [128, L_ALL * PV_PER_L], F32, kind="ExternalInput").ap()
        self.cbf_d = dt("cbf", [128, CB_N], BF16, kind="ExternalInput").ap()
        self.augk_d = dt("augk", [5, 4, S], BF16, kind="ExternalInput").ap()
        self.augq_d = dt("augq", [5, 4, S], BF16, kind="ExternalInput").ap()
        self.foxinit_d = dt("foxinit", [5, 2, 4, S], BF16, kind="ExternalInput").ap()
        skind = "ExternalOutput" if self.dbg else "Internal"
        self.xres = dt("xres", [NS, KC, 128, S], F32, kind=skind).ap()
        self.oT_d = dt("oT_s", [KC, 128, S], BF16, kind=skind).ap()
        self.yT_d = dt("yT_s", [KC, 128, S], BF16, kind=skind).ap()
        self.hT_d = dt("hT_s", [FC, 128, S], BF16, kind=skind).ap()
        self.fox_d = dt("fox_s", [5, 2, 4, S], BF16, kind=skind).ap()
        self.xbT_d = dt("xbT_s", [KC, 128, S], BF16, kind=skind).ap()
        if self.dbg:
            self.dbgA = dt("dbgA", [KC, 128, S], BF16, kind="ExternalOutput").ap()

        self.P = Prog(nc, es)
        P = self.P
        total = nc.sbuf_bytes_remaining
        ARENA = 207 * 1024
        assert total >= ARENA, total
        self.arena = nc.alloc_sbuf_tensor("arena", [128, ARENA // 2], BF16)
        self.ps = nc.alloc_psum_tensor("ps", [128, 8, 512], F32)
        K = 1024
        self.A = self.view(0, 64 * K, BF16, "p (k t) -> p k t", k=KC)
        self.B = self.view(64 * K, 64 * K, BF16, "p (k t) -> p k t", k=KC)
        self.offB = 64 * K
        off = 128 * K
        self.wslots = []
        for i in range(6):
            self.wslots.append(self.view(off, 4 * K, BF16))
            off += 4 * K
        self.wring = Slots(self.wslots)
        self.wsem = [P.new_sem("w%d" % i) for i in range(6)]
        self.tmp_off = off
        tl = []
        for i in range(8):
            tl.append(off)
            off += 2 * K
        self.tmps = Slots(tl)
        self.rb_off = off
        off += 28 * K
        self.cbf = self.view(off, CB_N * 2, BF16)
        off += CB_N * 2
        npv = L_ALL * PV_PER_L
        self.pvec = self.view(off, npv * 4, F32)
        off += npv * 4
        self.pmisc = self.view(off, 64 * 4, F32)
        off += 64 * 4
        self.lamtmp = self.view(off, 64 * 4, F32)
        off += 64 * 4
        assert off <= ARENA, off
        self.ident = self.cbf[:, CB_IDENT:CB_IDENT + 128]
        self.negtri = self.cbf[:, CB_NEGTRI:CB_NEGTRI + 128]
        self.negones = self.cbf[:, CB_NEGONES:CB_NEGONES + 128]
        self.ones = self.cbf[:, CB_ONES:CB_ONES + 128]
        self.blk64 = self.cbf[:, CB_BLK64:CB_BLK64 + 128]
        self.msb = self.cbf[:, CB_MSB:CB_MSB + 128]
        self.mfox = self.cbf[:, CB_MFOX:CB_MFOX + 128]
        self.bulk_sems = [P.new_sem("bulk%d" % i) for i in range(FC)]
        self.xs_sems = [P.new_sem("xs%d" % i) for i in range(2)]
        self.st_sems = [P.new_sem("st%d" % i) for i in range(2)]
        self.aug_sems = [P.new_sem("aug%d" % i) for i in range(2)]
        self.fox_sem = P.new_sem("foxs")
        self.xb_sems = [P.new_sem("xb%d" % i) for i in range(2)]
        self.misc_sem = P.new_sem("misc")

    def bank(self, b):
        return self.ps[:, b, :]

    def tmp(self, dtype=F32, n=512):
        k, off, w = self.tmps.acquire()
        nb = n * (4 if dtype == F32 else 2)
        return k, self.view(off, nb, dtype), w

    def sp_dma(self, out, in_, sem, waits=()):
        return self.P.op("sp", lambda e: e.dma_start(out=out, in_=in_), waits=waits, dsem=sem)

    def bulk_load(self, out, in_, idx, waits=()):
        return self.sp_dma(out, in_, self.bulk_sems[idx], waits)

    def wfetch(self, src, kcn):
        k, slot, w = self.wring.acquire()
        dst = slot[:, 0:kcn * 128].rearrange("p (k m) -> p k m", k=kcn)
        op = self.P.op("pool", lambda e: e.dma_start(out=dst, in_=src), waits=w, dsem=self.wsem[k])
        return k, dst, op

    def init_phase(self):
        P = self.P
        l1 = self.bulk_load(self.cbf, self.cbf_d, 0)
        l2 = self.bulk_load(self.pvec, self.pvec_d, 1)
        l3 = self.P.op("sp", lambda e: e.dma_start(out=self.fox_d, in_=self.foxinit_d), dsem=self.misc_sem)
        self.fox_init_op = l3
        self.const_ops = [l1, l2]
        last = None
        for l in range(self.n_layers):
            base = l * PV_PER_L
            lam_init = 0.8 - 0.6 * float(np.exp(-0.3 * l))
            pm = self.pmisc
            pv = self.pvec

            def ts(dst, src, mul):
                return P.dve(lambda e: e.tensor_scalar(out=dst, in0=src, scalar1=float(mul), scalar2=None, op0=ALU.mult), waits=[l2])
            ts(pm[:, l * 8 + 0:l * 8 + 1], pv[:, base + PV_QD:base + PV_QD + 1], 64.0 ** 0.25)
            ts(pm[:, l * 8 + 1:l * 8 + 2], pv[:, base + PV_KD:base + PV_KD + 1], 64.0 ** 0.25)
            ts(pm[:, l * 8 + 2:l * 8 + 3], pv[:, base + PV_QF:base + PV_QF + 1], 128.0 ** 0.25)
            ts(pm[:, l * 8 + 3:l * 8 + 4], pv[:, base + PV_KF:base + PV_KF + 1], 128.0 ** 0.25)
            ts(pm[:, l * 8 + 4:l * 8 + 5], pv[:, base + PV_SUB:base + PV_SUB + 1], 1.0 - lam_init)
            ts(pm[:, l * 8 + 6:l * 8 + 7], pv[:, base + PV_BF:base + PV_BF + 1], -1.0)
            lt = self.lamtmp
            lq1 = pv[:, base + PV_LAM:base + PV_LAM + 64]
            lk1 = pv[:, base + PV_LAM + 64:base + PV_LAM + 128]
            lq2 = pv[:, base + PV_LAM + 128:base + PV_LAM + 192]
            lk2 = pv[:, base + PV_LAM + 192:base + PV_LAM + 256]
            prod = self.view(self.rb_off, 64 * 4, F32)
            sums = lt[:, l * 4:l * 4 + 2]
            a1 = P.dve(lambda e, lq1=lq1, lk1=lk1: e.tensor_tensor(out=prod, in0=lq1, in1=lk1, op=ALU.mult), waits=[l2, last])
            a2 = P.dve(lambda e, sums=sums: e.reduce_sum(out=sums[:, 0:1], in_=prod, axis=AX.X), waits=[a1])
            a3 = P.dve(lambda e, lq2=lq2, lk2=lk2: e.tensor_tensor(out=prod, in0=lq2, in1=lk2, op=ALU.mult), waits=[a2])
            a4 = P.dve(lambda e, sums=sums: e.reduce_sum(out=sums[:, 1:2], in_=prod, axis=AX.X), waits=[a3])
            ex = lt[:, l * 4 + 2:l * 4 + 4]
            a5 = P.act(lambda e, sums=sums, ex=ex: e.activation(out=ex, in_=sums, func=AF.Exp), waits=[a4])
            dd = lt[:, 32 + l:33 + l]
            a6 = P.dve(lambda e, ex=ex, dd=dd: e.tensor_tensor(out=dd, in0=ex[:, 1:2], in1=ex[:, 0:1], op=ALU.subtract), waits=[a5])
            nl = pm[:, l * 8 + 5:l * 8 + 6]
            last = P.dve(lambda e, dd=dd, nl=nl, li=lam_init: e.tensor_scalar(out=nl, in0=dd, scalar1=float(-li), scalar2=None, op0=ALU.add), waits=[a6])
        P.barrier()

    def finish_norm(self, X, rs, ss_bank0, last_mm, xb_ops):
        P = self.P
        t2s = []
        for tb in range(NTB):
            r = rs[:, tb * TB:(tb + 1) * TB]
            t1 = P.act(lambda e, r=r, tb=tb: e.activation(out=r, in_=self.bank(ss_bank0 + tb), func=AF.Ln, bias=self.eps_col, scale=1.0 / D), waits=[last_mm[tb]])
            t2s.append(P.act(lambda e, r=r: e.activation(out=r, in_=r, func=AF.Exp, scale=-0.5), waits=[t1]))
        self.norm_done = [None] * NTB

        def normalize(tb):
            r = rs[:, tb * TB:(tb + 1) * TB]
            for kc in range(KC):
                a = X[:, kc, tb * TB:(tb + 1) * TB]
                self.norm_done[tb] = P.dve(lambda e, a=a, r=r: e.tensor_tensor(out=a, in0=a, in1=r, op=ALU.mult), waits=[t2s[tb], xb_ops[kc]])
        normalize(0)
        P.barrier()
        for tb in range(1, NTB):
            normalize(tb)

    def norm_phase(self, src, gcol_base):
        P = self.P
        K = 1024
        oB = self.offB
        stage = Slots([self.view(oB + i * 8 * K, 8 * K, F32) for i in range(2)])
        sqs = Slots([self.view(oB + 16 * K + i * 4 * K, 4 * K, BF16) for i in range(2)])
        rs = self.view(self.rb_off + 12 * K, 8 * K, F32)
        last_mm = [None] * NTB
        xb_ops = []
        for kc in range(KC):
            k, st, w = stage.acquire()
            ld = self.sp_dma(st, src[kc], self.xs_sems[k], waits=w)
            k2, sq, w2 = sqs.acquire()
            sqo = P.act(lambda e, sq=sq, st=st: e.activation(out=sq, in_=st, func=AF.Square), waits=[ld] + w2)
            g = self.pvec[:, gcol_base + kc:gcol_base + kc + 1]
            xb = P.dve(lambda e, kc=kc, st=st, g=g: e.tensor_scalar(out=self.A[:, kc, :], in0=st, scalar1=g, scalar2=None, op0=ALU.mult), waits=[ld])
            xb_ops.append(xb)
            stage.use(k, sqo, xb)
            for tb in range(NTB):
                last_mm[tb] = P.pe(lambda e, tb=tb, sq=sq, kc=kc: e.matmul(self.bank(tb), lhsT=self.ones, rhs=sq[:, tb * TB:(tb + 1) * TB],
                                                                      start=(kc == 0), stop=(kc == KC - 1)), waits=[sqo])
            sqs.use(k2, last_mm[NTB - 1])
        self.finish_norm(self.A, rs, 0, last_mm, xb_ops)

    def attn_setup(self):
        K = 1024
        oB = self.offB
        self.hb = []
        off = oB
        for par in range(2):
            d = {}
            for nm in ("R0", "R1", "R2", "R3"):
                d[nm] = self.view(off, 4 * K, BF16)
                off += 4 * K
            d["V"] = self.view(off, 4 * K, BF16, "p (t d) -> p t d", t=16)
            off += 4 * K
            self.hb.append(d)
        self.sp_slots = [self.view(off + i * K, K, BF16) for i in range(3)]
        off += 3 * K
        self.acc_bufs = [self.view(off, K, BF16)]
        off += K
        self.pT_slots = [self.view(off + i * K, K, BF16) for i in range(4)]
        off += 4 * K
        self.orow_slots = [self.view(off + i * 4 * K, 4 * K, BF16) for i in range(2)]
        off += 8 * K
        ro = self.rb_off + 12 * K
        self.dO = [self.view(ro + i * 2 * K, 2 * K, F32) for i in range(2)]
        self.epi_bufs = [(self.view(ro + 4 * K + g * 4 * K, 2 * K, F32), self.view(ro + 6 * K + g * 4 * K, 2 * K, F32)) for g in range(2)]
        assert off <= oB + 64 * K, off - oB

    def proj_gen(self, l, hidx, par, hb_free):
        P = self.P
        typ, hi = HEADS[hidx]
        hb = self.hb[par]
        if typ == "sb":
            pq, pk, pvn = hi, 6 + hi, 12 + hi
        elif typ == "diff":
            pq, pk, pvn = 18 + hi, 23 + hi, 28 + hi
        else:
            pq, pk, pvn = 33 + hi, 38 + hi, 43 + hi
        done = []
        pm = self.pmisc
        first_writes = list(hb_free)
        need_zero = (hidx < 2) or (HEADS[hidx - 2][0] != typ)
        if typ == "diff":
            z = []
            if need_zero:
                for nm, lo in (("R0", 64), ("R1", 64), ("R2", 0), ("R3", 0)):
                    buf = hb[nm]
                    z.append(P.op("pool", lambda e, buf=buf, lo=lo: e.memset(buf[lo:lo + 64, :], 0.0), waits=first_writes))
            else:
                z = list(first_writes) * 4 if len(first_writes) == 1 else [None] * 4
            a1 = P.op("sp", lambda e: e.dma_start(out=hb["R0"][64:68, :], in_=self.augq_d[hi]), waits=[z[0]], dsem=self.aug_sems[par])
            a2 = P.op("sp", lambda e: e.dma_start(out=hb["R1"][64:68, :], in_=self.augk_d[hi]), waits=[z[1]], dsem=self.aug_sems[par])
            a3 = P.op("sp", lambda e: e.dma_start(out=hb["R2"][0:4, :], in_=self.augq_d[hi]), waits=[z[2]], dsem=self.aug_sems[par])
            a4 = P.op("sp", lambda e: e.dma_start(out=hb["R3"][0:4, :], in_=self.augk_d[hi]), waits=[z[3]], dsem=self.aug_sems[par])
            done += [a1, a2, a3, a4]
        elif typ == "fox":
            z = []
            if need_zero:
                for nm in ("R2", "R3"):
                    buf = hb[nm]
                    z.append(P.op("pool", lambda e, buf=buf: e.memset(buf[:, :], 0.0), waits=first_writes))
            else:
                z = list(first_writes) * 2 if len(first_writes) == 1 else [None] * 2
            a1 = P.op("sp", lambda e: e.dma_start(out=hb["R2"][0:4, :], in_=self.fox_d[hi, 0]), waits=[z[0], self.fox_rows_op], dsem=self.aug_sems[par])
            a2 = P.op("sp", lambda e: e.dma_start(out=hb["R3"][0:4, :], in_=self.fox_d[hi, 1]), waits=[z[1], self.fox_rows_op], dsem=self.aug_sems[par])
            done += [a1, a2]
        yield
        for which, pidx in (("q", pq), ("k", pk)):
            wk, wp, wop = self.wfetch(self.w_in[l, pidx], KC)
            lastmm = None
            for tb in range(NTB):
                bk, bank_i, bw = self.projbanks.acquire()
                pb = self.bank(bank_i)
                for kc in range(KC):
                    lastmm = P.pe(lambda e, pb=pb, wp=wp, kc=kc, tb=tb: e.matmul(pb, lhsT=wp[:, kc, :], rhs=self.A[:, kc, tb * TB:(tb + 1) * TB],
                                                                                start=(kc == 0), stop=(kc == KC - 1)), waits=[wop, self.norm_done[tb]] + bw)
                sl = slice(tb * TB, (tb + 1) * TB)
                if typ == "sb":
                    dst = hb["R0"][:, sl] if which == "q" else hb["R1"][:, sl]
                    if which == "q":
                        ep = P.dve(lambda e, dst=dst, pb=pb: e.tensor_scalar(out=dst, in0=pb, scalar1=float(128.0 ** -0.5), scalar2=None, op0=ALU.mult),
                                   waits=[lastmm] + first_writes)
                    else:
                        ep = P.dve(lambda e, dst=dst, pb=pb: e.tensor_copy(out=dst, in_=pb), waits=[lastmm] + first_writes)
                    self.projbanks.use(bk, ep)
                    done.append(ep)
                else:
                    dd = 64.0 if typ == "diff" else 128.0
                    qk_, (sqt, q32), qw = self.pjbufs.acquire()
                    sqo = P.act(lambda e, sqt=sqt, pb=pb: e.activation(out=sqt, in_=pb, func=AF.Square), waits=[lastmm] + qw)
                    cp = P.dve(lambda e, q32=q32, pb=pb: e.tensor_copy(out=q32, in_=pb), waits=[lastmm, sqo] + qw)
                    self.projbanks.use(bk, sqo, cp)
                    yield
                    bk2, bank2, bw2 = self.projbanks.acquire()
                    pb2 = self.bank(bank2)
                    red = self.blk64 if typ == "diff" else self.ones
                    ssm = P.pe(lambda e, pb2=pb2, red=red, sqt=sqt: e.matmul(pb2, lhsT=red, rhs=sqt, start=True, stop=True), waits=[sqo] + bw2)
                    t2k, rt, t2w = self.tmp(F32)
                    r1 = P.act(lambda e, rt=rt, pb2=pb2, dd=dd: e.activation(out=rt, in_=pb2, func=AF.Ln, bias=self.epsd_col[dd]), waits=[ssm] + t2w)
                    r2 = P.act(lambda e, rt=rt: e.activation(out=rt, in_=rt, func=AF.Exp, scale=-0.5), waits=[r1])
                    self.projbanks.use(bk2, r1)
                    if typ == "diff":
                        gcol = pm[:, l * 8 + (0 if which == "q" else 1):l * 8 + (0 if which == "q" else 1) + 1]
                        d1 = (hb["R0"] if which == "q" else hb["R1"])
                        d2 = (hb["R2"] if which == "q" else hb["R3"])
                        e1 = P.dve(lambda e, d1=d1, q32=q32, gcol=gcol, rt=rt, sl=sl: e.scalar_tensor_tensor(
                            out=d1[0:64, sl], in0=q32[0:64, :], scalar=gcol[0:64, :], in1=rt[0:64, :], op0=ALU.mult, op1=ALU.mult), waits=[r2, cp] + z)
                        e2 = P.dve(lambda e, d2=d2, q32=q32, gcol=gcol, rt=rt, sl=sl: e.scalar_tensor_tensor(
                            out=d2[64:128, sl], in0=q32[64:128, :], scalar=gcol[64:128, :], in1=rt[64:128, :], op0=ALU.mult, op1=ALU.mult), waits=[r2, cp] + z)
                        self.tmps.use(t2k, r1, r2, e1, e2)
                        self.pjbufs.use(qk_, sqo, cp, ssm, e1, e2)
                        done += [e1, e2]
                    else:
                        gcol = pm[:, l * 8 + (2 if which == "q" else 3):l * 8 + (2 if which == "q" else 3) + 1]
                        d1 = (hb["R0"] if which == "q" else hb["R1"])
                        e1 = P.dve(lambda e, d1=d1, q32=q32, gcol=gcol, rt=rt, sl=sl: e.scalar_tensor_tensor(
                            out=d1[:, sl], in0=q32, scalar=gcol, in1=rt, op0=ALU.mult, op1=ALU.mult), waits=[r2, cp] + first_writes)
                        self.tmps.use(t2k, r1, r2, e1)
                        self.pjbufs.use(qk_, sqo, cp, ssm, e1)
                        done += [e1]
                    continue
                yield
            self.wring.use(wk, lastmm)
        wk, wp, wop = self.wfetch(self.w_in[l, pvn], KC)
        lastmm = None
        for g4 in range(4):
            bk, bank_i, bw = self.projbanks.acquire()
            pb = self.bank(bank_i)
            for t in range(4):
                tt = g4 * 4 + t
                for kc in range(KC):
                    lastmm = P.pe(lambda e, pb=pb, wp=wp, kc=kc, t=t, tt=tt: e.matmul(pb[:, t * 128:(t + 1) * 128], lhsT=self.A[:, kc, tt * 128:(tt + 1) * 128],
                                                                                     rhs=wp[:, kc, :], start=(kc == 0), stop=(kc == KC - 1)), waits=[wop, self.norm_done[g4]] + bw)
            dst = hb["V"][:, g4 * 4:(g4 + 1) * 4, :]
            ep = P.dve(lambda e, dst=dst, pb=pb: e.tensor_copy(out=dst, in_=pb.rearrange("p (t d) -> p t d", t=4)), waits=[lastmm] + first_writes)
            self.projbanks.use(bk, ep)
            done.append(ep)
            yield
        self.wring.use(wk, lastmm)
        self.proj_done[par] = done

    def fox_prep(self, l):
        P = self.P
        wk, wp, wop = self.wfetch(self.w_in[l, 48], KC)
        negb = self.pmisc[:, l * 8 + 6:l * 8 + 7]
        prev_cl = None
        prev_scan = None
        stores = []
        lastmm = None
        NP = 32
        if not hasattr(self, "foxhl"):
            self.foxhl = Slots([self.view(self.rb_off + 4096 + i * 4096, 4096, BF16) for i in range(2)])
        for tb in range(NTB):
            bk, bank_i, bw = self.projbanks.acquire()
            pb = self.bank(bank_i)
            for kc in range(KC):
                lastmm = P.pe(lambda e, pb=pb, wp=wp, kc=kc, tb=tb: e.matmul(pb, lhsT=wp[:, kc, :], rhs=self.A[:, kc, tb * TB:(tb + 1) * TB],
                                                                            start=(kc == 0), stop=(kc == KC - 1)), waits=[wop, self.norm_done[tb]] + bw)
            k1, et, w1 = self.tmp(F32)
            e1 = P.act(lambda e, et=et, pb=pb: e.activation(out=et[0:NP, :], in_=pb[0:NP, :], func=AF.Exp, bias=negb[0:NP, :], scale=-1.0), waits=[lastmm] + w1)
            self.projbanks.use(bk, e1)
            e2 = P.act(lambda e, et=et: e.activation(out=et[0:NP, :], in_=et[0:NP, :], func=AF.Ln, bias=self.one_col[0:NP, :]), waits=[e1])
            k2, cl, w2 = self.tmp(F32)
            init = 0.0 if prev_cl is None else prev_cl[0:NP, TB - 1:TB]
            sc = P.dve(lambda e, cl=cl, et=et, init=init: e.tensor_tensor_scan(out=cl[0:NP, :], data0=self.onesf[0:NP, :], data1=et[0:NP, :], initial=init,
                                                                               op0=ALU.mult, op1=ALU.add), waits=[e2, prev_scan] + w2)
            self.tmps.use(k1, e1, e2, sc)
            k3, hl, w3 = self.foxhl.acquire()
            hl4 = hl.rearrange("p (a t) -> p a t", a=4)
            k4, rem, w4 = self.tmp(F32)
            h1 = P.dve(lambda e, hl4=hl4, cl=cl: e.tensor_copy(out=hl4[0:NP, 0, :], in_=cl[0:NP, :]), waits=[sc] + w3)
            h2 = P.dve(lambda e, hl4=hl4, cl=cl, rem=rem: e.tensor_tensor(out=rem[0:NP, :], in0=cl[0:NP, :], in1=hl4[0:NP, 0, :], op=ALU.subtract), waits=[h1] + w4)
            h3 = P.dve(lambda e, hl4=hl4, rem=rem: e.tensor_copy(out=hl4[0:NP, 1, :], in_=rem[0:NP, :]), waits=[h2])
            h4 = P.dve(lambda e, hl4=hl4: e.tensor_scalar(out=hl4[0:NP, 2:4, :], in0=hl4[0:NP, 0:2, :], scalar1=-1.0, scalar2=None, op0=ALU.mult), waits=[h3])
            s1 = self.P.op("sp", lambda e, hl4=hl4, tb=tb: e.dma_start(out=self.fox_d[:, 0, 2:4, tb * TB:(tb + 1) * TB], in_=hl4[0:5, 0:2, :]),
                           waits=[h4, self.fox_init_op], dsem=self.fox_sem)
            s2 = self.P.op("sp", lambda e, hl4=hl4, tb=tb: e.dma_start(out=self.fox_d[:, 1, 0:2, tb * TB:(tb + 1) * TB], in_=hl4[0:5, 2:4, :]),
                           waits=[h4], dsem=self.fox_sem)
            self.foxhl.use(k3, h1, h2, h3, h4, s1, s2)
            self.tmps.use(k4, h2, h3)
            if prev_cl is not None:
                self.tmps.use(self.prev_cl_k, sc)
            self.prev_cl_k = k2
            self.tmps.use(k2, sc, h1, h2)
            prev_cl = cl
            prev_scan = sc
            stores = [s1, s2]
        self.wring.use(wk, lastmm)
        self.fox_rows_op = stores[-1]

    def run_deferred(self, n):
        while n > 0 and self.dq:
            self.dq.pop(0)[1]()
            n -= 1

    def flush_upto(self, tag):
        while self.dq and self.dq[0][0] <= tag:
            self.dq.pop(0)[1]()

    def defer(self, fn):
        self.dq.append((self.grp, fn))

    def att_gen(self, l, s, hidx, par):
        P = self.P
        typ, hi = HEADS[hidx]
        hb = self.hb[par]
        pdone = self.proj_done[par]
        pm = self.pmisc
        if typ == "diff":
            maps = [(hb["R0"], hb["R1"]), (hb["R2"], hb["R3"])]
        else:
            maps = [(hb["R0"], hb["R1"])]
        V = hb["V"]
        sbanks = self.sbanks
        if hidx >= 2 and self.head_last_grp[hidx - 2] is not None:
            self.flush_upto(self.head_last_grp[hidx - 2])
        ok, orow, ow = self.orows.acquire()
        orow_first = list(ow)
        last_pe = None
        o_parts = []
        acc = self.acc_bufs[0]
        Ob = self.bank(4)
        Db = self.bank(5)
        LAG = 3
        LB, LC = 2, 3
        for qb in range(4):
            q0 = qb * TB
            mres = []
            for mi, (Q, Kt) in enumerate(maps):
                tiles = list(range(4 * qb + 4))
                if typ == "sb":
                    tiles = tiles[::-1]
                    mz = P.dve(lambda e: e.memset(acc[:, :], 0.0), waits=self.acc_users)
                    self.acc_users = [mz]
                nt = len(tiles)
                stB, stX, stC, xres = [], [], [], {}
                lastpv = None
                lastdn = None
                accw_o = self.o_free
                accw_d = self.d_free
                for ti, kt in enumerate(tiles):
                    j = kt - 4 * qb
                    diag = j >= 0
                    c0 = 128 * j if j > 0 else 0
                    cs = slice(c0, TB)
                    ks = slice(kt * 128, (kt + 1) * 128)
                    qs = slice(q0 + c0, q0 + TB)
                    sk, sbi, sw = sbanks.acquire()
                    Sb = self.bank(sbi)
                    simple = (typ == "diff") and (not diag)
                    mm = P.pe(lambda e, Sb=Sb, Kt=Kt, Q=Q, cs=cs, ks=ks, qs=qs, simple=simple: e.matmul(Sb[:, cs], lhsT=Kt[:, ks], rhs=Q[:, qs], start=True, stop=simple),
                              waits=sw + pdone)
                    if typ == "fox":
                        LT, RT = hb["R2"], hb["R3"]
                        mm = P.pe(lambda e, Sb=Sb, LT=LT, RT=RT, cs=cs, ks=ks, qs=qs, diag=diag: e.matmul(Sb[:, cs], lhsT=LT[:, ks], rhs=RT[:, qs], start=False, stop=(not diag)))
                    if diag:
                        ds = slice(c0, c0 + 128)
                        if typ == "sb":
                            mm = P.pe(lambda e, Sb=Sb, ds=ds: e.matmul(Sb[:, ds], lhsT=self.ident, rhs=self.msb, start=False, stop=False))
                        elif typ == "fox":
                            mm = P.pe(lambda e, Sb=Sb, ds=ds: e.matmul(Sb[:, ds], lhsT=self.ident, rhs=self.mfox, start=False, stop=True))
                        else:
                            dh = self.cbf[:, CB_DCORR + hi * 256:CB_DCORR + hi * 256 + 128]
                            dl = self.cbf[:, CB_DCORR + hi * 256 + 128:CB_DCORR + hi * 256 + 256]
                            P.pe(lambda e, Sb=Sb, ds=ds, dh=dh: e.matmul(Sb[:, ds], lhsT=self.ident, rhs=dh, start=False, stop=False))
                            mm = P.pe(lambda e, Sb=Sb, ds=ds, dl=dl: e.matmul(Sb[:, ds], lhsT=self.ident, rhs=dl, start=False, stop=True))
                    if typ != "sb":
                        pk_, pT, pw = self.pTs.acquire()
                        ex = P.act(lambda e, pT=pT, Sb=Sb, cs=cs: e.activation(out=pT[:, cs], in_=Sb[:, cs], func=AF.Exp), waits=[mm] + pw)
                        sbanks.use(sk, ex)

                        def fC(pT=pT, cs=cs, kt=kt, ti=ti, ex=ex, pk_=pk_, nt=nt, accw_o=accw_o, accw_d=accw_d):
                            pv = P.pe(lambda e: e.matmul(Ob[:, cs], lhsT=V[:, kt, :], rhs=pT[:, cs], start=(ti == 0), stop=(ti == nt - 1)),
                                      waits=[ex] + (accw_o if ti == 0 else []))
                            dn = P.pe(lambda e: e.matmul(Db[:, cs], lhsT=self.ones, rhs=pT[:, cs], start=(ti == 0), stop=(ti == nt - 1)),
                                      waits=(accw_d if ti == 0 else []))
                            self.pTs.use(pk_, dn)
                            return pv, dn
                        stC.append(fC)
                        if ti >= LAG:
                            lastpv, lastdn = stC[ti - LAG]()
                    else:
                        ek, e32, ew = self.tmp(F32)
                        spk, spb, spw = self.sps.acquire()
                        x1 = P.act(lambda e, e32=e32, Sb=Sb, cs=cs: e.activation(out=e32[:, cs], in_=Sb[:, cs], func=AF.Exp), waits=[mm] + ew)

                        def fB(Sb=Sb, spb=spb, cs=cs, ti=ti, spk=spk, x2h=None):
                            x2 = x2h[0]
                            b1 = P.pe(lambda e: e.matmul(Sb[:, cs], lhsT=self.negtri, rhs=spb[:, cs], start=False, stop=(ti == 0)), waits=[x2])
                            if ti > 0:
                                b1 = P.pe(lambda e: e.matmul(Sb[:, cs], lhsT=self.negones, rhs=acc[:, cs], start=False, stop=True), waits=self.acc_users)
                            au = P.dve(lambda e: e.tensor_tensor(out=acc[:, cs], in0=acc[:, cs], in1=spb[:, cs], op=ALU.add), waits=[x2, b1] + self.acc_users)
                            self.acc_users = [au]
                            self.sps.use(spk, b1, au)
                            return b1

                        def fX(b1, Sb=Sb, cs=cs, sk=sk):
                            pk_, pT, pw = self.pTs.acquire()
                            ex = P.act(lambda e: e.activation(out=pT[:, cs], in_=Sb[:, cs], func=AF.Exp), waits=[b1] + pw)
                            sbanks.use(sk, ex)
                            return pk_, pT, ex

                        def fC(pk_, pT, ex, cs=cs, kt=kt, ti=ti, nt=nt, accw_o=accw_o):
                            pv = P.pe(lambda e: e.matmul(Ob[:, cs], lhsT=V[:, kt, :], rhs=pT[:, cs], start=(ti == 0), stop=(ti == nt - 1)),
                                      waits=[ex] + (accw_o if ti == 0 else []))
                            self.pTs.use(pk_, pv)
                            return pv, None
                        x2h = [None]
                        stB.append((fB, x2h))
                        stX.append(fX)
                        stC.append(fC)
                        if ti >= LB:
                            fb, xh = stB[ti - LB]
                            xres[ti - LB] = stX[ti - LB](fb(x2h=xh))
                        x2 = P.act(lambda e, e32=e32, spb=spb, cs=cs: e.activation(out=spb[:, cs], in_=e32[:, cs], func=AF.Ln, bias=self.one_col), waits=[x1] + spw)
                        x2h[0] = x2
                        self.tmps.use(ek, x1, x2)
                        if ti >= LC:
                            lastpv, _ = stC[ti - LC](*xres[ti - LC])
                    self.run_deferred(2)
                    yield
                if typ != "sb":
                    for r in range(max(0, nt - LAG), nt):
                        lastpv, lastdn = stC[r]()
                else:
                    for i in range(nt, nt + LC):
                        if 0 <= i - LB < nt:
                            fb, xh = stB[i - LB]
                            xres[i - LB] = stX[i - LB](fb(x2h=xh))
                        if 0 <= i - LC < nt:
                            lastpv, _ = stC[i - LC](*xres[i - LC])
                last_pe = lastpv if lastdn is None else lastdn
                self.grp += 1
                self.flush_upto(self.grp - 2)
                osl = orow[:, q0:q0 + TB]
                if typ == "sb":
                    ep = P.dve(lambda e, osl=osl: e.tensor_copy(out=osl, in_=Ob), waits=[lastpv] + orow_first)
                    self.o_free = [ep]
                    o_parts.append(ep)
                else:
                    gp = self.epi_par
                    self.epi_par ^= 1
                    o32, d32 = self.epi_bufs[gp]
                    users = self.epi_users[gp]
                    cpo = P.dve(lambda e, o32=o32: e.tensor_copy(out=o32, in_=Ob), waits=[lastpv] + users)
                    cpd = P.act(lambda e, d32=d32: e.activation(out=d32, in_=Db, func=AF.Copy), waits=[lastdn] + users)
                    self.o_free = [cpo]
                    self.d_free = [cpd]
                    rec = []
                    state = {"ops": [cpo, cpd]}
                    self.epi_users[gp] = state["ops"]
                    for pc in range(4):
                        psl = slice(pc * 128, (pc + 1) * 128)

                        def frec(psl=psl, d32=d32, cpd=cpd, state=state):
                            r = P.dve(lambda e: e.reciprocal(out=d32[:, psl], in_=d32[:, psl]), waits=[cpd])
                            state["ops"].append(r)
                            state["lastrec"] = r
                        self.defer(frec)
                    if typ == "fox":
                        def fmul(o32=o32, d32=d32, cpo=cpo, osl=osl, state=state):
                            ep = P.dve(lambda e: e.tensor_tensor(out=osl, in0=o32, in1=d32, op=ALU.mult), waits=[cpo, state["lastrec"]] + orow_first)
                            state["ops"].append(ep)
                            o_parts.append(ep)
                        self.defer(fmul)
                    else:
                        om = self.dO[mi]

                        def fmul(o32=o32, d32=d32, cpo=cpo, om=om, mi=mi, state=state):
                            r2 = P.dve(lambda e: e.tensor_tensor(out=om, in0=o32, in1=d32, op=ALU.mult), waits=[cpo, state["lastrec"]] + self.dO_users[mi])
                            state["ops"].append(r2)
                            self.dO_users[mi] = [r2]
                            self.dO_last[mi] = r2
                        self.defer(fmul)
            if typ == "diff":
                o1, o2 = self.dO
                neglam = pm[:, l * 8 + 5:l * 8 + 6]
                gsub = pm[:, l * 8 + 4:l * 8 + 5]
                osl = orow[:, q0:q0 + TB]
                st = {}

                def f1(st=st, neglam=neglam):
                    st["c1"] = P.dve(lambda e: e.scalar_tensor_tensor(out=o1, in0=o2, scalar=neglam, in1=o1, op0=ALU.mult, op1=ALU.add),
                                     waits=[self.dO_last[0], self.dO_last[1]])

                def f2a(st=st):
                    sqt = self.csq
                    c2 = P.act(lambda e: e.activation(out=sqt, in_=o1, func=AF.Square), waits=[st["c1"]] + self.csq_users)
                    st.update(c2=c2)

                def f2b(st=st):
                    sqt = self.csq
                    c2 = st["c2"]
                    bk2, bank2, bw2 = self.projbanks.acquire()
                    pb2 = self.bank(bank2)
                    c3 = P.pe(lambda e: e.matmul(pb2, lhsT=self.ones, rhs=sqt, start=True, stop=True), waits=[c2] + bw2)
                    c4 = P.act(lambda e: e.activation(out=o2, in_=pb2, func=AF.Ln, bias=self.eps_col, scale=1.0 / 128.0), waits=[c3, st["c1"]])
                    c5 = P.act(lambda e: e.activation(out=o2, in_=o2, func=AF.Exp, scale=-0.5), waits=[c4])
                    self.projbanks.use(bk2, c4)
                    self.csq_users = [c2, c3]
                    st.update(c4=c4, c5=c5)

                def nop():
                    pass

                def f3(st=st, osl=osl, gsub=gsub):
                    ep = P.dve(lambda e: e.scalar_tensor_tensor(out=osl, in0=o1, scalar=gsub, in1=o2, op0=ALU.mult, op1=ALU.mult), waits=[st["c5"]] + orow_first)
                    self.dO_users[0] = [st["c1"], st["c2"], ep]
                    self.dO_users[1] = [st["c1"], st["c4"], st["c5"], ep]
                    o_parts.append(ep)
                for fn in (f1, f2a, nop, f2b, nop, nop, nop, f3):
                    self.defer(fn)
        def fstore():
            st = self.sp_dma(self.oT_d[hidx], orow, self.st_sems[ok], waits=o_parts)
            self.orows.use(ok, st)
        self.defer(fstore)
        self.head_last_grp[hidx] = self.grp
        self.att_last_pe[par] = [last_pe]
        yield

    def attention_phase(self, l, s):
        P = self.P
        self.projbanks = Slots([6, 7])
        self.sbanks = Slots([0, 1, 2, 3])
        self.pTs = Slots(self.pT_slots)
        self.sps = Slots(self.sp_slots)
        self.orows = Slots(self.orow_slots)
        self.acc_users = []
        self.dO_users = [[], []]
        self.dO_last = [None, None]
        self.o_free = []
        self.d_free = []
        self.epi_par = 0
        self.epi_users = [[], []]
        self.dq = []
        self.grp = 0
        self.head_last_grp = [None] * 16
        self.csq = self.view(self.offB + 56 * 1024, 1024, BF16)
        self.csq_users = []
        K = 1024
        ro = self.rb_off
        self.pjbufs = Slots([(self.view(ro, K, BF16), self.view(ro + 2 * K, 2 * K, F32)),
                             (self.view(ro + K, K, BF16), self.view(ro + 24 * K, 2 * K, F32))])
        self.att_last_pe = [[], []]
        self.proj_done = [None, None]
        self.fox_prep(l)
        g = self.proj_gen(l, 0, 0, [])
        for _ in g:
            pass
        for h in range(16):
            par = h % 2
            ag = self.att_gen(l, s, h, par)
            pg = self.proj_gen(l, h + 1, 1 - par, self.att_last_pe[1 - par]) if h + 1 < 16 else iter(())
            cnt = 0
            for _ in ag:
                cnt += 1
                if cnt % 3 == 0:
                    next(pg, None)
            for _ in pg:
                pass
        self.run_deferred(100000)
        P.barrier()

    def gate_phase(self, l):
        P = self.P
        K = 1024
        lds = []
        for c in range(KC):
            lds.append(self.bulk_load(self.B[:, c, :], self.oT_d[c], c))
        ro = self.rb_off
        ysum = self.view(ro, 8 * K, F32)
        gsb = Slots([self.view(ro + 8 * K + i * 4 * K, 4 * K, BF16) for i in range(2)])
        yrows = Slots([self.view(ro + 16 * K + i * 4 * K, 4 * K, BF16) for i in range(2)])
        gbanks = Slots([0, 1, 2])
        bbanks = Slots([3, 4, 5])
        branches = [(self.w_bsb, 6, 0), (self.w_bdf, 5, 6), (self.w_bfx, 5, 11)]
        ysum_users = []
        for m in range(KC):
            yk, yrow, yw = yrows.acquire()
            for i in range(3):
                wk, wp, wop = self.wfetch(self.w_gate[l, i * 16 + m], KC)
                gk, gs, gw = gsb.acquire()
                lastmm = None
                sigs = []
                for tb in range(NTB):
                    bk, bi, bw = gbanks.acquire()
                    pb = self.bank(bi)
                    for kc in range(KC):
                        lastmm = P.pe(lambda e, pb=pb, wp=wp, kc=kc, tb=tb: e.matmul(pb, lhsT=wp[:, kc, :], rhs=self.A[:, kc, tb * TB:(tb + 1) * TB],
                                                                                    start=(kc == 0), stop=(kc == KC - 1)), waits=[wop] + bw)
                    sg = P.act(lambda e, gs=gs, pb=pb, tb=tb: e.activation(out=gs[:, tb * TB:(tb + 1) * TB], in_=pb, func=AF.Sigmoid), waits=[lastmm] + gw)
                    gbanks.use(bk, sg)
                    sigs.append(sg)
                self.wring.use(wk, lastmm)
                wsrc, kcn, coff = branches[i]
                wk, wp, wop = self.wfetch(wsrc[l, m], kcn)
                for tb in range(NTB):
                    bk, bi, bw = bbanks.acquire()
                    pb = self.bank(bi)
                    for kc in range(kcn):
                        lastmm = P.pe(lambda e, pb=pb, wp=wp, kc=kc, tb=tb, coff=coff, kcn=kcn: e.matmul(pb, lhsT=wp[:, kc, :], rhs=self.B[:, coff + kc, tb * TB:(tb + 1) * TB],
                                                                                                        start=(kc == 0), stop=(kc == kcn - 1)), waits=[wop] + bw + [lds[coff + kc]])
                    sl = slice(tb * TB, (tb + 1) * TB)
                    if i == 0:
                        d = P.dve(lambda e, pb=pb, gs=gs, sl=sl: e.tensor_tensor(out=ysum[:, sl], in0=pb, in1=gs[:, sl], op=ALU.mult), waits=[lastmm, sigs[tb]] + ysum_users)
                        bbanks.use(bk, d)
                    else:
                        tk, tt, tw = self.tmp(F32)
                        d0 = P.dve(lambda e, pb=pb, gs=gs, sl=sl, tt=tt: e.tensor_tensor(out=tt, in0=pb, in1=gs[:, sl], op=ALU.mult), waits=[lastmm, sigs[tb]] + tw)
                        bbanks.use(bk, d0)
                        if i == 1:
                            d = P.dve(lambda e, sl=sl, tt=tt: e.tensor_tensor(out=ysum[:, sl], in0=ysum[:, sl], in1=tt, op=ALU.add), waits=[d0])
                        else:
                            d = P.dve(lambda e, sl=sl, tt=tt, yrow=yrow: e.tensor_tensor(out=yrow[:, sl], in0=ysum[:, sl], in1=tt, op=ALU.add), waits=[d0] + yw)
                            ysum_users = [d]
                        self.tmps.use(tk, d0, d)
                    gsb.use(gk, d)
                self.wring.use(wk, lastmm)
            st = self.sp_dma(self.yT_d[m], yrow, self.st_sems[yk], waits=[d])
            yrows.use(yk, st)
        P.barrier()

    def out_phase(self, l, src, dst, gcol_base):
        P = self.P
        K = 1024
        lds = []
        for c in range(KC):
            lds.append(self.bulk_load(self.A[:, c, :], self.yT_d[c], c))
        ro = self.rb_off
        xst = Slots([self.view(ro + i * 8 * K, 8 * K, F32) for i in range(2)])
        sqs = Slots([self.view(ro + 16 * K + i * 4 * K, 4 * K, BF16) for i in range(2)])
        rs = self.view(self.tmp_off, 8 * K, F32)
        banks = Slots([0, 1, 2, 3])
        last_mm = [None] * NTB
        xb_ops = []
        pending = []
        for m in range(KC):
            xk, xs, xw = xst.acquire()
            xl = self.sp_dma(xs, src[m], self.xs_sems[xk], waits=xw)
            wk, wp, wop = self.wfetch(self.w_out[l, m], KC)
            adds = []
            for tb in range(NTB):
                bk, bi, bw = banks.acquire()
                pb = self.bank(bi)
                for kc in range(KC):
                    lastmm = P.pe(lambda e, pb=pb, wp=wp, kc=kc, tb=tb: e.matmul(pb, lhsT=wp[:, kc, :], rhs=self.A[:, kc, tb * TB:(tb + 1) * TB],
                                                                                start=(kc == 0), stop=(kc == KC - 1)), waits=[wop] + bw + [lds[kc]])
                sl = slice(tb * TB, (tb + 1) * TB)
                a = P.dve(lambda e, pb=pb, xs=xs, sl=sl: e.tensor_tensor(out=xs[:, sl], in0=pb, in1=xs[:, sl], op=ALU.add), waits=[lastmm, xl])
                banks.use(bk, a)
                adds.append(a)
            self.wring.use(wk, lastmm)
            while pending:
                pending.pop(0)()
            st = self.sp_dma(dst[m], xs, self.st_sems[xk], waits=adds)
            k2, sq, w2 = sqs.acquire()
            sqo = P.act(lambda e, sq=sq, xs=xs: e.activation(out=sq, in_=xs, func=AF.Square), waits=adds + w2)
            g = self.pvec[:, gcol_base + m:gcol_base + m + 1]
            xb = P.dve(lambda e, m=m, xs=xs, g=g: e.tensor_scalar(out=self.B[:, m, :], in0=xs, scalar1=g, scalar2=None, op0=ALU.mult), waits=adds)
            xb_ops.append(xb)
            def ssmm(sq=sq, m=m, sqo=sqo, k2=k2):
                for tb in range(NTB):
                    last_mm[tb] = P.pe(lambda e, tb=tb: e.matmul(self.bank(4 + tb), lhsT=self.ones, rhs=sq[:, tb * TB:(tb + 1) * TB],
                                                                 start=(m == 0), stop=(m == KC - 1)), waits=[sqo])
                sqs.use(k2, last_mm[NTB - 1])
            pending.append(ssmm)
            xst.use(xk, st, sqo, xb)
        while pending:
            pending.pop(0)()
        self.finish_norm(self.B, rs, 4, last_mm, xb_ops)

    def ffn_up_phase(self, l, X):
        P = self.P
        K = 1024
        oB = self.offB
        ro = self.rb_off
        srows = Slots([self.view(ro + i * 4 * K, 4 * K, BF16) for i in range(2)])
        hrows = Slots([self.view(ro + 8 * K + i * 4 * K, 4 * K, BF16) for i in range(2)])
        gb = Slots([0, 1, 2, 3])
        ub = Slots([4, 5, 6, 7])
        for m in range(FC):
            wk, wp, wop = self.wfetch(self.w_fg[l, m], KC)
            sk, sr, sw = srows.acquire()
            sil = []
            for tb in range(NTB):
                bk, bi, bw = gb.acquire()
                pb = self.bank(bi)
                for kc in range(KC):
                    lastmm = P.pe(lambda e, pb=pb, wp=wp, kc=kc, tb=tb: e.matmul(pb, lhsT=wp[:, kc, :], rhs=X[:, kc, tb * TB:(tb + 1) * TB],
                                                                                start=(kc == 0), stop=(kc == KC - 1)), waits=[wop, self.norm_done[tb]] + bw)
                sg = P.act(lambda e, sr=sr, pb=pb, tb=tb: e.activation(out=sr[:, tb * TB:(tb + 1) * TB], in_=pb, func=AF.Silu), waits=[lastmm] + sw)
                gb.use(bk, sg)
                sil.append(sg)
            self.wring.use(wk, lastmm)
            wk, wp, wop = self.wfetch(self.w_fu[l, m], KC)
            hk, hr, hw = hrows.acquire()
            muls = []
            for tb in range(NTB):
                bk, bi, bw = ub.acquire()
                pb = self.bank(bi)
                for kc in range(KC):
                    lastmm = P.pe(lambda e, pb=pb, wp=wp, kc=kc, tb=tb: e.matmul(pb, lhsT=wp[:, kc, :], rhs=X[:, kc, tb * TB:(tb + 1) * TB],
                                                                                start=(kc == 0), stop=(kc == KC - 1)), waits=[wop, self.norm_done[tb]] + bw)
                sl = slice(tb * TB, (tb + 1) * TB)
                d = P.dve(lambda e, pb=pb, sr=sr, hr=hr, sl=sl: e.tensor_tensor(out=hr[:, sl], in0=pb, in1=sr[:, sl], op=ALU.mult), waits=[lastmm, sil[tb]] + hw)
                ub.use(bk, d)
                muls.append(d)
            self.wring.use(wk, lastmm)
            srows.use(sk, *muls)
            st = self.sp_dma(self.hT_d[m], hr, self.st_sems[hk], waits=muls)
            hrows.use(hk, st)
        P.barrier()

    def ffn_down_phase(self, l, src, dst, next_g=None):
        P = self.P
        K = 1024
        HT = 1024
        H = self.view(0, FC * HT * 2, BF16, "p (k t) -> p k t", k=FC)
        xoff = FC * HT * 2
        self.pre_ss = [None] * NTB
        for half in range(2):
            t0 = half * HT
            lds = []
            for c in range(FC):
                lds.append(self.bulk_load(H[:, c, :], self.hT_d[c][:, t0:t0 + HT], c))
            xst = Slots([self.view(xoff + i * 4 * K, 4 * K, F32) for i in range(2)])
            sqs = Slots([self.view(xoff + 8 * K + i * 2 * K, 2 * K, BF16) for i in range(2)])
            xbs = Slots([self.view(xoff + 12 * K + i * 2 * K, 2 * K, BF16) for i in range(2)])
            banks = Slots([0, 1, 2, 3])
            pending = []
            for m in range(KC):
                xk, xs, xw = xst.acquire()
                xl = self.sp_dma(xs, src[m][:, t0:t0 + HT], self.xs_sems[xk], waits=xw)
                parts = []
                for (k0, k1) in ((0, 16), (16, 32), (32, 44)):
                    wk, wp, wop = self.wfetch(self.w_fd[l, m][:, k0:k1, :], k1 - k0)
                    parts.append((wk, wp, wop, k0, k1))
                adds = []
                for tbh in range(2):
                    bk, bi, bw = banks.acquire()
                    pb = self.bank(bi)
                    for (wk, wp, wop, k0, k1) in parts:
                        for kc in range(k0, k1):
                            lastmm = P.pe(lambda e, pb=pb, wp=wp, kc=kc, k0=k0, tbh=tbh: e.matmul(pb, lhsT=wp[:, kc - k0, :], rhs=H[:, kc, tbh * TB:(tbh + 1) * TB],
                                                                                                 start=(kc == 0), stop=(kc == FC - 1)), waits=[wop] + bw + [lds[kc]])
                    sl = slice(tbh * TB, (tbh + 1) * TB)
                    a = P.dve(lambda e, pb=pb, xs=xs, sl=sl: e.tensor_tensor(out=xs[:, sl], in0=pb, in1=xs[:, sl], op=ALU.add), waits=[lastmm, xl])
                    banks.use(bk, a)
                    adds.append(a)
                for (wk, wp, wop, k0, k1) in parts:
                    self.wring.use(wk, lastmm)
                while pending:
                    pending.pop(0)()
                st = self.sp_dma(dst[m][:, t0:t0 + HT], xs, self.st_sems[xk], waits=adds)
                users = [st]
                if next_g is not None:
                    k2, sq, w2 = sqs.acquire()
                    sqo = P.act(lambda e, sq=sq, xs=xs: e.activation(out=sq, in_=xs, func=AF.Square), waits=adds + w2)
                    k3, xbt, w3 = xbs.acquire()
                    g = self.pvec[:, next_g + m:next_g + m + 1]
                    xb = P.dve(lambda e, xbt=xbt, xs=xs, g=g: e.tensor_scalar(out=xbt, in0=xs, scalar1=g, scalar2=None, op0=ALU.mult), waits=adds + w3)
                    st2 = self.sp_dma(self.xbT_d[m][:, t0:t0 + HT], xbt, self.xb_sems[k3], waits=[xb])
                    xbs.use(k3, st2)

                    def ssmm(sq=sq, m=m, sqo=sqo, k2=k2, half=half, sqs=sqs):
                        for tbh in range(2):
                            tb = half * 2 + tbh
                            self.pre_ss[tb] = P.pe(lambda e, tb=tb, tbh=tbh: e.matmul(self.bank(4 + tb), lhsT=self.ones, rhs=sq[:, tbh * TB:(tbh + 1) * TB],
                                                                                  start=(m == 0), stop=(m == KC - 1)), waits=[sqo])
                        sqs.use(k2, self.pre_ss[half * 2 + 1])
                    pending.append(ssmm)
                    users += [sqo, xb]
                xst.use(xk, *users)
            while pending:
                pending.pop(0)()
            P.barrier()

    def norm_phase_fast(self):
        K = 1024
        lds = []
        for c in range(KC):
            lds.append(self.bulk_load(self.A[:, c, :], self.xbT_d[c], c))
        rs = self.view(self.rb_off + 12 * K, 8 * K, F32)
        self.finish_norm(self.A, rs, 4, self.pre_ss, lds)

    def build(self):
        with ExitStack() as es:
            self.setup(es)
            P = self.P
            K = 1024
            cc = self.lamtmp
            self.eps_col = cc[:, 40:41]
            self.one_col = cc[:, 41:42]
            self.epsd_col = {64.0: cc[:, 42:43], 128.0: cc[:, 43:44]}
            P.dve(lambda e: e.memset(self.eps_col, EPS))
            P.dve(lambda e: e.memset(self.one_col, 1.0))
            P.dve(lambda e: e.memset(self.epsd_col[64.0], EPS * 64.0))
            P.dve(lambda e: e.memset(self.epsd_col[128.0], EPS * 128.0))
            self.onesf = self.view(self.rb_off + 26 * K, 2 * K, F32)
            P.dve(lambda e: e.memset(self.onesf, 1.0))
            self.init_phase()
            self.attn_setup()
            stop = self.stop_after
            done = False
            for s in range(self.n_seq):
                for l in range(self.n_layers):
                    src = self.xT[s] if l == 0 else self.xres[s]
                    lastl = (l == self.n_layers - 1)
                    base = l * PV_PER_L
                    if l == 0:
                        self.norm_phase(src, base + PV_GMIX)
                    else:
                        self.norm_phase_fast()
                    if self.dbg and s == 0 and l == 0:
                        for c in range(KC):
                            self.sp_dma(self.dbgA[c], self.A[:, c, :], self.misc_sem, waits=[x for x in self.norm_done if x is not None])
                        P.barrier()
                    if stop == "norm1":
                        done = True
                        break
                    self.attention_phase(l, s)
                    if stop == "attn":
                        done = True
                        break
                    self.gate_phase(l)
                    if stop == "gate":
                        done = True
                        break
                    self.out_phase(l, src, self.xres[s], base + PV_GFFN)
                    if stop == "out":
                        done = True
                        break
                    self.ffn_up_phase(l, self.B)
                    if stop == "ffn_up":
                        done = True
                        break
                    self.ffn_down_phase(l, self.xres[s], self.outT[s] if lastl else self.xres[s],
                                        next_g=(None if lastl else (l + 1) * PV_PER_L + PV_GMIX))
                if done:
                    break
            P.barrier()
            P.build()
        return self.nc


def _panels(w, kc, mc):
    L = w.shape[0]
    return np.ascontiguousarray(w.reshape(L, kc, 128, mc, 128).transpose(0, 3, 2, 1, 4))


def _hi_lo(v):
    hi = v.astype(NPBF)
    lo = (v - hi.astype(np.float32)).astype(NPBF)
    return hi, lo


def host_consts():
    cb = np.zeros((128, CB_N), np.float32)
    j = np.arange(128)[:, None]
    sidx = np.arange(128)[None, :]
    cb[:, CB_IDENT:CB_IDENT + 128] = np.eye(128)
    cb[:, CB_NEGTRI:CB_NEGTRI + 128] = np.where(j >= sidx, -1.0, 0.0)
    cb[:, CB_NEGONES:CB_NEGONES + 128] = -1.0
    cb[:, CB_ONES:CB_ONES + 128] = 1.0
    blk = np.zeros((128, 128))
    blk[:64, :64] = 1.0
    blk[64:, 64:] = 1.0
    cb[:, CB_BLK64:CB_BLK64 + 128] = blk
    kl = np.arange(128)[:, None]
    ql = np.arange(128)[None, :]
    cb[:, CB_MSB:CB_MSB + 128] = np.where(kl < ql, 0.0, NEG)
    cb[:, CB_MFOX:CB_MFOX + 128] = np.where(kl <= ql, 0.0, NEG)
    cbb = cb.astype(NPBF)
    for h in range(5):
        sl = SLOPES[h]
        v = np.where(ql >= kl, 0.0, np.where((kl // 64) <= (ql // 64), -2.0 * sl * (kl - ql), NEG)).astype(np.float32)
        hi, lo = _hi_lo(v)
        cbb[:, CB_DCORR + h * 256:CB_DCORR + h * 256 + 128] = hi
        cbb[:, CB_DCORR + h * 256 + 128:CB_DCORR + h * 256 + 256] = lo
    pos = np.arange(S, dtype=np.float64)
    augk = np.zeros((5, 4, S), NPBF)
    augq = np.zeros((5, 4, S), NPBF)
    for h in range(5):
        a = (SLOPES[h] * pos).astype(np.float32)
        r = (-SLOPES[h] * pos).astype(np.float32)
        ah, al = _hi_lo(a)
        rh, rl = _hi_lo(r)
        augk[h, 0] = 1.0
        augk[h, 1] = 1.0
        augk[h, 2] = ah
        augk[h, 3] = al
        augq[h, 0] = rh
        augq[h, 1] = rl
        augq[h, 2] = 1.0
        augq[h, 3] = 1.0
    foxinit = np.zeros((5, 2, 4, S), NPBF)
    foxinit[:, 0, 0:2, :] = 1.0
    foxinit[:, 1, 2:4, :] = 1.0
    return cbb, augk, augq, foxinit


def host_prep(inp):
    f = lambda k: np.asarray(inp[k], dtype=np.float32)
    L = L_ALL
    w_in = f("w_in")
    w_in_p = np.zeros((L, D, 49 * 128), np.float32)
    w_in_p[:, :, :6149] = w_in
    out = {}
    out["w_in"] = _panels(w_in_p, KC, 49)
    out["w_gate"] = _panels(f("w_gate"), KC, 48)
    out["w_bsb"] = _panels(f("w_branch_sb"), 6, 16)
    out["w_bdf"] = _panels(f("w_branch_diff"), 5, 16)
    out["w_bfx"] = _panels(f("w_branch_fox"), 5, 16)
    out["w_out"] = _panels(f("w_out"), KC, 16)
    out["w_fg"] = _panels(f("w_ff_gate"), KC, FC)
    out["w_fu"] = _panels(f("w_ff_up"), KC, FC)
    out["w_fd"] = _panels(f("w_ff_down"), FC, 16)
    pv = np.zeros((128, L * PV_PER_L), np.float32)
    for l in range(L):
        b = l * PV_PER_L
        pv[:, b + PV_GMIX:b + PV_GMIX + 16] = f("norm_mix")[l].reshape(16, 128).T
        pv[:, b + PV_GFFN:b + PV_GFFN + 16] = f("norm_ffn")[l].reshape(16, 128).T
        pv[:, b + PV_QD] = np.tile(f("q_norm_diff")[l], 2)
        pv[:, b + PV_KD] = np.tile(f("k_norm_diff")[l], 2)
        pv[:, b + PV_QF] = f("q_norm_fox")[l]
        pv[:, b + PV_KF] = f("k_norm_fox")[l]
        pv[:, b + PV_SUB] = f("sub_norm_diff")[l]
        pv[0:5, b + PV_BF] = f("b_forget")[l]
        for i, nm in enumerate(("lambda_q1", "lambda_k1", "lambda_q2", "lambda_k2")):
            pv[:, b + PV_LAM + 64 * i:b + PV_LAM + 64 * (i + 1)] = f(nm)[l][None, :]
    out["pvec"] = pv
    cbb, augk, augq, foxinit = host_consts()
    out["cbf"] = cbb
    out["augk"] = augk
    out["augq"] = augq
    out["foxinit"] = foxinit
    return out


_CACHE = {}


def kernel(**inputs):
    x = np.asarray(inputs["x"], dtype=np.float32)
    shared = host_prep(inputs)
    if "nc" not in _CACHE:
        _CACHE["nc"] = Builder().build()
    nc = _CACHE["nc"]
    in_maps = []
    for c in range(N_CORES):
        xs = x[c * SEQ_PER_CORE:(c + 1) * SEQ_PER_CORE]
        xT = np.ascontiguousarray(xs.transpose(0, 2, 1)).reshape(SEQ_PER_CORE, KC, 128, S)
        m = dict(shared)
        m["xT"] = xT
        in_maps.append(m)
    res = run_bass_kernel_spmd(nc, in_maps, core_ids=list(range(N_CORES)))
    outs = []
    for c in range(N_CORES):
        o = np.asarray(res.results[c]["outT"]).reshape(SEQ_PER_CORE, D, S).transpose(0, 2, 1)
        outs.append(o)
    return np.ascontiguousarray(np.concatenate(outs, axis=0)).astype(np.float32)
```

```python
import numpy as np
import ml_dtypes
from contextlib import ExitStack
import concourse.bass as bass
import concourse.mybir as mybir
from concourse.bass_utils import run_bass_kernel_spmd

F32 = mybir.dt.float32
BF16 = mybir.dt.bfloat16
AF = mybir.ActivationFunctionType
ALU = mybir.AluOpType
AX = mybir.AxisListType
NPBF = ml_dtypes.bfloat16

D = 2048
KC = 16
S = 2048
TB = 512
NTB = 4
DFF = 5632
FC = 44
L_ALL = 4
EPS = 1e-6
NEG = -30000.0
N_CORES = 8
SEQ_PER_CORE = 2
HEADS = [("sb", i) for i in range(6)] + [("diff", i) for i in range(5)] + [("fox", i) for i in range(5)]
SLOPES = [2.0 ** (-8.0 * (i + 1) / 5.0) for i in range(5)]
ENGS = ("pe", "act", "dve", "pool", "sp")
SAME_ENGINE_SYNC = True

PV_GMIX = 0
PV_GFFN = 16
PV_QD = 32
PV_KD = 33
PV_QF = 34
PV_KF = 35
PV_SUB = 36
PV_BF = 37
PV_LAM = 38
PV_PER_L = 38 + 256
CB_IDENT = 0
CB_NEGTRI = 128
CB_NEGONES = 256
CB_ONES = 384
CB_BLK64 = 512
CB_MSB = 640
CB_MFOX = 768
CB_DCORR = 896
CB_N = 896 + 1280


class Op:
    __slots__ = ("eng", "fn", "waits", "sig", "dsem", "val")


class Prog:
    def __init__(self, nc, es):
        self.nc = nc
        self.es = es
        self.q = {e: [] for e in ENGS}
        self.semh = {}
        for e in ENGS:
            self.semh[e] = es.enter_context(nc.semaphore("s_" + e))
        self.last = {e: None for e in ENGS}
        self.sp_dma_last = {}
        self.nsem = 0

    def new_sem(self, name):
        self.semh[name] = self.es.enter_context(self.nc.semaphore(name))
        return name

    def op(self, eng, fn, waits=(), dsem=None):
        o = Op()
        o.eng, o.fn, o.dsem, o.sig, o.val = eng, fn, dsem, False, None
        ws = []
        for w in waits:
            if w is None:
                continue
            if w.dsem is None and w.eng == eng:
                if eng == "pe" or not SAME_ENGINE_SYNC:
                    continue
            if w.dsem is None:
                w.sig = True
            ws.append(w)
        o.waits = ws
        self.q[eng].append(o)
        if fn is not None:
            self.last[eng] = o
            if dsem is not None and eng == "sp":
                self.sp_dma_last[dsem] = o
        return o

    def pe(self, fn, waits=()):
        return self.op("pe", fn, waits)

    def act(self, fn, waits=()):
        return self.op("act", fn, waits)

    def dve(self, fn, waits=()):
        return self.op("dve", fn, waits)

    def barrier(self):
        lasts = [self.last[e] for e in ("pe", "act", "dve") if self.last[e] is not None]
        dm = list(self.sp_dma_last.values())
        for e in ("pe", "act", "dve", "sp"):
            self.op(e, None, waits=[w for w in lasts if w.eng != e] + dm)

    def build(self):
        cnt = {}
        for e in ENGS:
            c = 0
            for o in self.q[e]:
                if o.fn is None:
                    continue
                if o.dsem is not None:
                    cnt[o.dsem] = cnt.get(o.dsem, 0) + 16
                    o.val = (o.dsem, cnt[o.dsem])
                elif o.sig:
                    c += 1
                    o.val = (e, c)
        semh = self.semh

        def run_engine(name, e):
            waited = {}
            for o in self.q[name]:
                for w in o.waits:
                    sn, v = w.val
                    if waited.get(sn, 0) >= v:
                        continue
                    waited[sn] = v
                    e.wait_ge(semh[sn], v)
                if o.fn is None:
                    continue
                ins = o.fn(e)
                if o.dsem is not None:
                    ins.then_inc(semh[o.dsem], 16)
                elif o.sig:
                    ins.then_inc(semh[name], 1)

        with self.nc.Block() as block:
            @block.tensor
            def _(e):
                run_engine("pe", e)

            @block.scalar
            def _(e):
                run_engine("act", e)

            @block.vector
            def _(e):
                run_engine("dve", e)

            @block.gpsimd
            def _(e):
                run_engine("pool", e)

            @block.sync
            def _(e):
                run_engine("sp", e)


class Slots:
    def __init__(self, items):
        self.items = list(items)
        self.n = len(self.items)
        self.i = 0
        self.users = [[] for _ in range(self.n)]

    def acquire(self):
        k = self.i
        self.i = (self.i + 1) % self.n
        w = self.users[k]
        self.users[k] = []
        return k, self.items[k], w

    def use(self, k, *ops):
        for o in ops:
            if o is not None:
                self.users[k].append(o)


class Builder:
    def __init__(self, n_layers=L_ALL, n_seq=SEQ_PER_CORE, dbg=False, stop_after=None):
        self.n_layers = n_layers
        self.n_seq = n_seq
        self.dbg = dbg
        self.stop_after = stop_after
        self.nc = bass.Bass("TRN2", target_bir_lowering=False)

    def view(self, off, nbytes, dtype=BF16, pattern=None, **kw):
        a = self.arena[:, off // 2:(off + nbytes) // 2]
        if dtype == F32:
            a = a.bitcast(F32)
        if pattern is not None:
            a = a.rearrange(pattern, **kw)
        return a

    def setup(self, es):
        nc = self.nc
        L, NS = self.n_layers, self.n_seq
        dt = nc.dram_tensor
        self.xT = dt("xT", [NS, KC, 128, S], F32, kind="ExternalInput").ap()
        self.outT = dt("outT", [NS, KC, 128, S], F32, kind="ExternalOutput").ap()
        self.w_in = dt("w_in", [L, 49, 128, KC, 128], F32, kind="ExternalInput").ap()
        self.w_gate = dt("w_gate", [L, 48, 128, KC, 128], F32, kind="ExternalInput").ap()
        self.w_bsb = dt("w_bsb", [L, 16, 128, 6, 128], F32, kind="ExternalInput").ap()
        self.w_bdf = dt("w_bdf", [L, 16, 128, 5, 128], F32, kind="ExternalInput").ap()
        self.w_bfx = dt("w_bfx", [L, 16, 128, 5, 128], F32, kind="ExternalInput").ap()
        self.w_out = dt("w_out", [L, 16, 128, KC, 128], F32, kind="ExternalInput").ap()
        self.w_fg = dt("w_fg", [L, FC, 128, KC, 128], F32, kind="ExternalInput").ap()
        self.w_fu = dt("w_fu", [L, FC, 128, KC, 128], F32, kind="ExternalInput").ap()
        self.w_fd = dt("w_fd", [L, 16, 128, FC, 128], F32, kind="ExternalInput").ap()
        self.pvec_d = dt("pvec", [128, L_ALL * PV_PER_L], F32, kind="ExternalInput").ap()
        self.cbf_d = dt("cbf", [128, CB_N], BF16, kind="ExternalInput").ap()
        self.augk_d = dt("augk", [5, 4, S], BF16, kind="ExternalInput").ap()
        self.augq_d = dt("augq", [5, 4, S], BF16, kind="ExternalInput").ap()
        self.foxinit_d = dt("foxinit", [5, 2, 4, S], BF16, kind="ExternalInput").ap()
        skind = "ExternalOutput" if self.dbg else "Internal"
        self.xres = dt("xres", [NS, KC, 128, S], F32, kind=skind).ap()
        self.oT_d = dt("oT_s", [KC, 128, S], BF16, kind=skind).ap()
        self.yT_d = dt("yT_s", [KC, 128, S], BF16, kind=skind).ap()
        self.hT_d = dt("hT_s", [FC, 128, S], BF16, kind=skind).ap()
        self.fox_d = dt("fox_s", [5, 2, 4, S], BF16, kind=skind).ap()
        self.xbT_d = dt("xbT_s", [KC, 128, S], BF16, kind=skind).ap()
        if self.dbg:
            self.dbgA = dt("dbgA", [KC, 128, S], BF16, kind="ExternalOutput").ap()

        self.P = Prog(nc, es)
        P = self.P
        total = nc.sbuf_bytes_remaining
        ARENA = 207 * 1024
        assert total >= ARENA, total
        self.arena = nc.alloc_sbuf_tensor("arena", [128, ARENA // 2], BF16)
        self.ps = nc.alloc_psum_tensor("ps", [128, 8, 512], F32)
        K = 1024
        self.A = self.view(0, 64 * K, BF16, "p (k t) -> p k t", k=KC)
        self.B = self.view(64 * K, 64 * K, BF16, "p (k t) -> p k t", k=KC)
        self.offB = 64 * K
        off = 128 * K
        self.wslots = []
        for i in range(6):
            self.wslots.append(self.view(off, 4 * K, BF16))
            off += 4 * K
        self.wring = Slots(self.wslots)
        self.wsem = [P.new_sem("w%d" % i) for i in range(6)]
        self.tmp_off = off
        tl = []
        for i in range(8):
            tl.append(off)
            off += 2 * K
        self.tmps = Slots(tl)
        self.rb_off = off
        off += 28 * K
        self.cbf = self.view(off, CB_N * 2, BF16)
        off += CB_N * 2
        npv = L_ALL * PV_PER_L
        self.pvec = self.view(off, npv * 4, F32)
        off += npv * 4
        self.pmisc = self.view(off, 64 * 4, F32)
        off += 64 * 4
        self.lamtmp = self.view(off, 64 * 4, F32)
        off += 64 * 4
        assert off <= ARENA, off
        self.ident = self.cbf[:, CB_IDENT:CB_IDENT + 128]
        self.negtri = self.cbf[:, CB_NEGTRI:CB_NEGTRI + 128]
        self.negones = self.cbf[:, CB_NEGONES:CB_NEGONES + 128]
        self.ones = self.cbf[:, CB_ONES:CB_ONES + 128]
        self.blk64 = self.cbf[:, CB_BLK64:CB_BLK64 + 128]
        self.msb = self.cbf[:, CB_MSB:CB_MSB + 128]
        self.mfox = self.cbf[:, CB_MFOX:CB_MFOX + 128]
        self.bulk_sems = [P.new_sem("bulk%d" % i) for i in range(FC)]
        self.xs_sems = [P.new_sem("xs%d" % i) for i in range(2)]
        self.st_sems = [P.new_sem("st%d" % i) for i in range(2)]
        self.aug_sems = [P.new_sem("aug%d" % i) for i in range(2)]
        self.fox_sem = P.new_sem("foxs")
        self.xb_sems = [P.new_sem("xb%d" % i) for i in range(2)]
        self.misc_sem = P.new_sem("misc")

    def bank(self, b):
        return self.ps[:, b, :]

    def tmp(self, dtype=F32, n=512):
        k, off, w = self.tmps.acquire()
        nb = n * (4 if dtype == F32 else 2)
        return k, self.view(off, nb, dtype), w

    def sp_dma(self, out, in_, sem, waits=()):
        return self.P.op("sp", lambda e: e.dma_start(out=out, in_=in_), waits=waits, dsem=sem)

    def bulk_load(self, out, in_, idx, waits=()):
        return self.sp_dma(out, in_, self.bulk_sems[idx], waits)

    def wfetch(self, src, kcn):
        k, slot, w = self.wring.acquire()
        dst = slot[:, 0:kcn * 128].rearrange("p (k m) -> p k m", k=kcn)
        op = self.P.op("pool", lambda e: e.dma_start(out=dst, in_=src), waits=w, dsem=self.wsem[k])
        return k, dst, op

    def init_phase(self):
        P = self.P
        l1 = self.bulk_load(self.cbf, self.cbf_d, 0)
        l2 = self.bulk_load(self.pvec, self.pvec_d, 1)
        l3 = self.P.op("sp", lambda e: e.dma_start(out=self.fox_d, in_=self.foxinit_d), dsem=self.misc_sem)
        self.fox_init_op = l3
        self.const_ops = [l1, l2]
        last = None
        for l in range(self.n_layers):
            base = l * PV_PER_L
            lam_init = 0.8 - 0.6 * float(np.exp(-0.3 * l))
            pm = self.pmisc
            pv = self.pvec

            def ts(dst, src, mul):
                return P.dve(lambda e: e.tensor_scalar(out=dst, in0=src, scalar1=float(mul), scalar2=None, op0=ALU.mult), waits=[l2])
            ts(pm[:, l * 8 + 0:l * 8 + 1], pv[:, base + PV_QD:base + PV_QD + 1], 64.0 ** 0.25)
            ts(pm[:, l * 8 + 1:l * 8 + 2], pv[:, base + PV_KD:base + PV_KD + 1], 64.0 ** 0.25)
            ts(pm[:, l * 8 + 2:l * 8 + 3], pv[:, base + PV_QF:base + PV_QF + 1], 128.0 ** 0.25)
            ts(pm[:, l * 8 + 3:l * 8 + 4], pv[:, base + PV_KF:base + PV_KF + 1], 128.0 ** 0.25)
            ts(pm[:, l * 8 + 4:l * 8 + 5], pv[:, base + PV_SUB:base + PV_SUB + 1], 1.0 - lam_init)
            ts(pm[:, l * 8 + 6:l * 8 + 7], pv[:, base + PV_BF:base + PV_BF + 1], -1.0)
            lt = self.lamtmp
            lq1 = pv[:, base + PV_LAM:base + PV_LAM + 64]
            lk1 = pv[:, base + PV_LAM + 64:base + PV_LAM + 128]
            lq2 = pv[:, base + PV_LAM + 128:base + PV_LAM + 192]
            lk2 = pv[:, base + PV_LAM + 192:base + PV_LAM + 256]
            prod = self.view(self.rb_off, 64 * 4, F32)
            sums = lt[:, l * 4:l * 4 + 2]
            a1 = P.dve(lambda e, lq1=lq1, lk1=lk1: e.tensor_tensor(out=prod, in0=lq1, in1=lk1, op=ALU.mult), waits=[l2, last])
            a2 = P.dve(lambda e, sums=sums: e.reduce_sum(out=sums[:, 0:1], in_=prod, axis=AX.X), waits=[a1])
            a3 = P.dve(lambda e, lq2=lq2, lk2=lk2: e.tensor_tensor(out=prod, in0=lq2, in1=lk2, op=ALU.mult), waits=[a2])
            a4 = P.dve(lambda e, sums=sums: e.reduce_sum(out=sums[:, 1:2], in_=prod, axis=AX.X), waits=[a3])
            ex = lt[:, l * 4 + 2:l * 4 + 4]
            a5 = P.act(lambda e, sums=sums, ex=ex: e.activation(out=ex, in_=sums, func=AF.Exp), waits=[a4])
            dd = lt[:, 32 + l:33 + l]
            a6 = P.dve(lambda e, ex=ex, dd=dd: e.tensor_tensor(out=dd, in0=ex[:, 1:2], in1=ex[:, 0:1], op=ALU.subtract), waits=[a5])
            nl = pm[:, l * 8 + 5:l * 8 + 6]
            last = P.dve(lambda e, dd=dd, nl=nl, li=lam_init: e.tensor_scalar(out=nl, in0=dd, scalar1=float(-li), scalar2=None, op0=ALU.add), waits=[a6])
        P.barrier()

    def finish_norm(self, X, rs, ss_bank0, last_mm, xb_ops):
        P = self.P
        t2s = []
        for tb in range(NTB):
            r = rs[:, tb * TB:(tb + 1) * TB]
            t1 = P.act(lambda e, r=r, tb=tb: e.activation(out=r, in_=self.bank(ss_bank0 + tb), func=AF.Ln, bias=self.eps_col, scale=1.0 / D), waits=[last_mm[tb]])
            t2s.append(P.act(lambda e, r=r: e.activation(out=r, in_=r, func=AF.Exp, scale=-0.5), waits=[t1]))
        self.norm_done = [None] * NTB

        def normalize(tb):
            r = rs[:, tb * TB:(tb + 1) * TB]
            for kc in range(KC):
                a = X[:, kc, tb * TB:(tb + 1) * TB]
                self.norm_done[tb] = P.dve(lambda e, a=a, r=r: e.tensor_tensor(out=a, in0=a, in1=r, op=ALU.mult), waits=[t2s[tb], xb_ops[kc]])
        normalize(0)
        P.barrier()
        for tb in range(1, NTB):
            normalize(tb)

    def norm_phase(self, src, gcol_base):
        P = self.P
        K = 1024
        oB = self.offB
        stage = Slots([self.view(oB + i * 8 * K, 8 * K, F32) for i in range(2)])
        sqs = Slots([self.view(oB + 16 * K + i * 4 * K, 4 * K, BF16) for i in range(2)])
        rs = self.view(self.rb_off + 12 * K, 8 * K, F32)
        last_mm = [None] * NTB
        xb_ops = []
        for kc in range(KC):
            k, st, w = stage.acquire()
            ld = self.sp_dma(st, src[kc], self.xs_sems[k], waits=w)
            k2, sq, w2 = sqs.acquire()
            sqo = P.act(lambda e, sq=sq, st=st: e.activation(out=sq, in_=st, func=AF.Square), waits=[ld] + w2)
            g = self.pvec[:, gcol_base + kc:gcol_base + kc + 1]
            xb = P.dve(lambda e, kc=kc, st=st, g=g: e.tensor_scalar(out=self.A[:, kc, :], in0=st, scalar1=g, scalar2=None, op0=ALU.mult), waits=[ld])
            xb_ops.append(xb)
            stage.use(k, sqo, xb)
            for tb in range(NTB):
                last_mm[tb] = P.pe(lambda e, tb=tb, sq=sq, kc=kc: e.matmul(self.bank(tb), lhsT=self.ones, rhs=sq[:, tb * TB:(tb + 1) * TB],
                                                                      start=(kc == 0), stop=(kc == KC - 1)), waits=[sqo])
            sqs.use(k2, last_mm[NTB - 1])
        self.finish_norm(self.A, rs, 0, last_mm, xb_ops)

    def attn_setup(self):
        K = 1024
        oB = self.offB
        self.hb = []
        off = oB
        for par in range(2):
            d = {}
            for nm in ("R0", "R1", "R2", "R3"):
                d[nm] = self.view(off, 4 * K, BF16)
                off += 4 * K
            d["V"] = self.view(off, 4 * K, BF16, "p (t d) -> p t d", t=16)
            off += 4 * K
            self.hb.append(d)
        self.sp_slots = [self.view(off + i * K, K, BF16) for i in range(3)]
        off += 3 * K
        self.acc_bufs = [self.view(off, K, BF16)]
        off += K
        self.pT_slots = [self.view(off + i * K, K, BF16) for i in range(4)]
        off += 4 * K
        self.orow_slots = [self.view(off + i * 4 * K, 4 * K, BF16) for i in range(2)]
        off += 8 * K
        ro = self.rb_off + 12 * K
        self.dO = [self.view(ro + i * 2 * K, 2 * K, F32) for i in range(2)]
        self.epi_bufs = [(self.view(ro + 4 * K + g * 4 * K, 2 * K, F32), self.view(ro + 6 * K + g * 4 * K, 2 * K, F32)) for g in range(2)]
        assert off <= oB + 64 * K, off - oB

    def proj_gen(self, l, hidx, par, hb_free):
        P = self.P
        typ, hi = HEADS[hidx]
        hb = self.hb[par]
        if typ == "sb":
            pq, pk, pvn = hi, 6 + hi, 12 + hi
        elif typ == "diff":
            pq, pk, pvn = 18 + hi, 23 + hi, 28 + hi
        else:
            pq, pk, pvn = 33 + hi, 38 + hi, 43 + hi
        done = []
        pm = self.pmisc
        first_writes = list(hb_free)
        need_zero = (hidx < 2) or (HEADS[hidx - 2][0] != typ)
        if typ == "diff":
            z = []
            if need_zero:
                for nm, lo in (("R0", 64), ("R1", 64), ("R2", 0), ("R3", 0)):
                    buf = hb[nm]
                    z.append(P.op("pool", lambda e, buf=buf, lo=lo: e.memset(buf[lo:lo + 64, :], 0.0), waits=first_writes))
            else:
                z = list(first_writes) * 4 if len(first_writes) == 1 else [None] * 4
            a1 = P.op("sp", lambda e: e.dma_start(out=hb["R0"][64:68, :], in_=self.augq_d[hi]), waits=[z[0]], dsem=self.aug_sems[par])
            a2 = P.op("sp", lambda e: e.dma_start(out=hb["R1"][64:68, :], in_=self.augk_d[hi]), waits=[z[1]], dsem=self.aug_sems[par])
            a3 = P.op("sp", lambda e: e.dma_start(out=hb["R2"][0:4, :], in_=self.augq_d[hi]), waits=[z[2]], dsem=self.aug_sems[par])
            a4 = P.op("sp", lambda e: e.dma_start(out=hb["R3"][0:4, :], in_=self.augk_d[hi]), waits=[z[3]], dsem=self.aug_sems[par])
            done += [a1, a2, a3, a4]
        elif typ == "fox":
            z = []
            if need_zero:
                for nm in ("R2", "R3"):
                    buf = hb[nm]
                    z.append(P.op("pool", lambda e, buf=buf: e.memset(buf[:, :], 0.0), waits=first_writes))
            else:
                z = list(first_writes) * 2 if len(first_writes) == 1 else [None] * 2
            a1 = P.op("sp", lambda e: e.dma_start(out=hb["R2"][0:4, :], in_=self.fox_d[hi, 0]), waits=[z[0], self.fox_rows_op], dsem=self.aug_sems[par])
            a2 = P.op("sp", lambda e: e.dma_start(out=hb["R3"][0:4, :], in_=self.fox_d[hi, 1]), waits=[z[1], self.fox_rows_op], dsem=self.aug_sems[par])
            done += [a1, a2]
        yield
        for which, pidx in (("q", pq), ("k", pk)):
            wk, wp, wop = self.wfetch(self.w_in[l, pidx], KC)
            lastmm = None
            for tb in range(NTB):
                bk, bank_i, bw = self.projbanks.acquire()
                pb = self.bank(bank_i)
                for kc in range(KC):
                    lastmm = P.pe(lambda e, pb=pb, wp=wp, kc=kc, tb=tb: e.matmul(pb, lhsT=wp[:, kc, :], rhs=self.A[:, kc, tb * TB:(tb + 1) * TB],
                                                                                start=(kc == 0), stop=(kc == KC - 1)), waits=[wop, self.norm_done[tb]] + bw)
                sl = slice(tb * TB, (tb + 1) * TB)
                if typ == "sb":
                    dst = hb["R0"][:, sl] if which == "q" else hb["R1"][:, sl]
                    if which == "q":
                        ep = P.dve(lambda e, dst=dst, pb=pb: e.tensor_scalar(out=dst, in0=pb, scalar1=float(128.0 ** -0.5), scalar2=None, op0=ALU.mult),
                                   waits=[lastmm] + first_writes)
                    else:
                        ep = P.dve(lambda e, dst=dst, pb=pb: e.tensor_copy(out=dst, in_=pb), waits=[lastmm] + first_writes)
                    self.projbanks.use(bk, ep)
                    done.append(ep)
                else:
                    dd = 64.0 if typ == "diff" else 128.0
                    qk_, (sqt, q32), qw = self.pjbufs.acquire()
                    sqo = P.act(lambda e, sqt=sqt, pb=pb: e.activation(out=sqt, in_=pb, func=AF.Square), waits=[lastmm] + qw)
                    cp = P.dve(lambda e, q32=q32, pb=pb: e.tensor_copy(out=q32, in_=pb), waits=[lastmm, sqo] + qw)
                    self.projbanks.use(bk, sqo, cp)
                    yield
                    bk2, bank2, bw2 = self.projbanks.acquire()
                    pb2 = self.bank(bank2)
                    red = self.blk64 if typ == "diff" else self.ones
                    ssm = P.pe(lambda e, pb2=pb2, red=red, sqt=sqt: e.matmul(pb2, lhsT=red, rhs=sqt, start=True, stop=True), waits=[sqo] + bw2)
                    t2k, rt, t2w = self.tmp(F32)
                    r1 = P.act(lambda e, rt=rt, pb2=pb2, dd=dd: e.activation(out=rt, in_=pb2, func=AF.Ln, bias=self.epsd_col[dd]), waits=[ssm] + t2w)
                    r2 = P.act(lambda e, rt=rt: e.activation(out=rt, in_=rt, func=AF.Exp, scale=-0.5), waits=[r1])
                    self.projbanks.use(bk2, r1)
                    if typ == "diff":
                        gcol = pm[:, l * 8 + (0 if which == "q" else 1):l * 8 + (0 if which == "q" else 1) + 1]
                        d1 = (hb["R0"] if which == "q" else hb["R1"])
                        d2 = (hb["R2"] if which == "q" else hb["R3"])
                        e1 = P.dve(lambda e, d1=d1, q32=q32, gcol=gcol, rt=rt, sl=sl: e.scalar_tensor_tensor(
                            out=d1[0:64, sl], in0=q32[0:64, :], scalar=gcol[0:64, :], in1=rt[0:64, :], op0=ALU.mult, op1=ALU.mult), waits=[r2, cp] + z)
                        e2 = P.dve(lambda e, d2=d2, q32=q32, gcol=gcol, rt=rt, sl=sl: e.scalar_tensor_tensor(
                            out=d2[64:128, sl], in0=q32[64:128, :], scalar=gcol[64:128, :], in1=rt[64:128, :], op0=ALU.mult, op1=ALU.mult), waits=[r2, cp] + z)
                        self.tmps.use(t2k, r1, r2, e1, e2)
                        self.pjbufs.use(qk_, sqo, cp, ssm, e1, e2)
                        done += [e1, e2]
                    else:
                        gcol = pm[:, l * 8 + (2 if which == "q" else 3):l * 8 + (2 if which == "q" else 3) + 1]
                        d1 = (hb["R0"] if which == "q" else hb["R1"])
                        e1 = P.dve(lambda e, d1=d1, q32=q32, gcol=gcol, rt=rt, sl=sl: e.scalar_tensor_tensor(
                            out=d1[:, sl], in0=q32, scalar=gcol, in1=rt, op0=ALU.mult, op1=ALU.mult), waits=[r2, cp] + first_writes)
                        self.tmps.use(t2k, r1, r2, e1)
                        self.pjbufs.use(qk_, sqo, cp, ssm, e1)
                        done += [e1]
                    continue
                yield
            self.wring.use(wk, lastmm)
        wk, wp, wop = self.wfetch(self.w_in[l, pvn], KC)
        lastmm = None
        for g4 in range(4):
            bk, bank_i, bw = self.projbanks.acquire()
            pb = self.bank(bank_i)
            for t in range(4):
                tt = g4 * 4 + t
                for kc in range(KC):
                    lastmm = P.pe(lambda e, pb=pb, wp=wp, kc=kc, t=t, tt=tt: e.matmul(pb[:, t * 128:(t + 1) * 128], lhsT=self.A[:, kc, tt * 128:(tt + 1) * 128],
                                                                                     rhs=wp[:, kc, :], start=(kc == 0), stop=(kc == KC - 1)), waits=[wop, self.norm_done[g4]] + bw)
            dst = hb["V"][:, g4 * 4:(g4 + 1) * 4, :]
            ep = P.dve(lambda e, dst=dst, pb=pb: e.tensor_copy(out=dst, in_=pb.rearrange("p (t d) -> p t d", t=4)), waits=[lastmm] + first_writes)
            self.projbanks.use(bk, ep)
            done.append(ep)
            yield
        self.wring.use(wk, lastmm)
        self.proj_done[par] = done

    def fox_prep(self, l):
        P = self.P
        wk, wp, wop = self.wfetch(self.w_in[l, 48], KC)
        negb = self.pmisc[:, l * 8 + 6:l * 8 + 7]
        prev_cl = None
        prev_scan = None
        stores = []
        lastmm = None
        NP = 32
        if not hasattr(self, "foxhl"):
            self.foxhl = Slots([self.view(self.rb_off + 4096 + i * 4096, 4096, BF16) for i in range(2)])
        for tb in range(NTB):
            bk, bank_i, bw = self.projbanks.acquire()
            pb = self.bank(bank_i)
            for kc in range(KC):
                lastmm = P.pe(lambda e, pb=pb, wp=wp, kc=kc, tb=tb: e.matmul(pb, lhsT=wp[:, kc, :], rhs=self.A[:, kc, tb * TB:(tb + 1) * TB],
                                                                            start=(kc == 0), stop=(kc == KC - 1)), waits=[wop, self.norm_done[tb]] + bw)
            k1, et, w1 = self.tmp(F32)
            e1 = P.act(lambda e, et=et, pb=pb: e.activation(out=et[0:NP, :], in_=pb[0:NP, :], func=AF.Exp, bias=negb[0:NP, :], scale=-1.0), waits=[lastmm] + w1)
            self.projbanks.use(bk, e1)
            e2 = P.act(lambda e, et=et: e.activation(out=et[0:NP, :], in_=et[0:NP, :], func=AF.Ln, bias=self.one_col[0:NP, :]), waits=[e1])
            k2, cl, w2 = self.tmp(F32)
            init = 0.0 if prev_cl is None else prev_cl[0:NP, TB - 1:TB]
            sc = P.dve(lambda e, cl=cl, et=et, init=init: e.tensor_tensor_scan(out=cl[0:NP, :], data0=self.onesf[0:NP, :], data1=et[0:NP, :], initial=init,
                                                                               op0=ALU.mult, op1=ALU.add), waits=[e2, prev_scan] + w2)
            self.tmps.use(k1, e1, e2, sc)
            k3, hl, w3 = self.foxhl.acquire()
            hl4 = hl.rearrange("p (a t) -> p a t", a=4)
            k4, rem, w4 = self.tmp(F32)
            h1 = P.dve(lambda e, hl4=hl4, cl=cl: e.tensor_copy(out=hl4[0:NP, 0, :], in_=cl[0:NP, :]), waits=[sc] + w3)
            h2 = P.dve(lambda e, hl4=hl4, cl=cl, rem=rem: e.tensor_tensor(out=rem[0:NP, :], in0=cl[0:NP, :], in1=hl4[0:NP, 0, :], op=ALU.subtract), waits=[h1] + w4)
            h3 = P.dve(lambda e, hl4=hl4, rem=rem: e.tensor_copy(out=hl4[0:NP, 1, :], in_=rem[0:NP, :]), waits=[h2])
            h4 = P.dve(lambda e, hl4=hl4: e.tensor_scalar(out=hl4[0:NP, 2:4, :], in0=hl4[0:NP, 0:2, :], scalar1=-1.0, scalar2=None, op0=ALU.mult), waits=[h3])
            s1 = self.P.op("sp", lambda e, hl4=hl4, tb=tb: e.dma_start(out=self.fox_d[:, 0, 2:4, tb * TB:(tb + 1) * TB], in_=hl4[0:5, 0:2, :]),
                           waits=[h4, self.fox_init_op], dsem=self.fox_sem)
            s2 = self.P.op("sp", lambda e, hl4=hl4, tb=tb: e.dma_start(out=self.fox_d[:, 1, 0:2, tb * TB:(tb + 1) * TB], in_=hl4[0:5, 2:4, :]),
                           waits=[h4], dsem=self.fox_sem)
            self.foxhl.use(k3, h1, h2, h3, h4, s1, s2)
            self.tmps.use(k4, h2, h3)
            if prev_cl is not None:
                self.tmps.use(self.prev_cl_k, sc)
            self.prev_cl_k = k2
            self.tmps.use(k2, sc, h1, h2)
            prev_cl = cl
            prev_scan = sc
            stores = [s1, s2]
        self.wring.use(wk, lastmm)
        self.fox_rows_op = stores[-1]

    def run_deferred(self, n):
        while n > 0 and self.dq:
            self.dq.pop(0)[1]()
            n -= 1

    def flush_upto(self, tag):
        while self.dq and self.dq[0][0] <= tag:
            self.dq.pop(0)[1]()

    def defer(self, fn):
        self.dq.append((self.grp, fn))

    def att_gen(self, l, s, hidx, par):
        P = self.P
        typ, hi = HEADS[hidx]
        hb = self.hb[par]
        pdone = self.proj_done[par]
        pm = self.pmisc
        if typ == "diff":
            maps = [(hb["R0"], hb["R1"]), (hb["R2"], hb["R3"])]
        else:
            maps = [(hb["R0"], hb["R1"])]
        V = hb["V"]
        sbanks = self.sbanks
        if hidx >= 2 and self.head_last_grp[hidx - 2] is not None:
            self.flush_upto(self.head_last_grp[hidx - 2])
        ok, orow, ow = self.orows.acquire()
        orow_first = list(ow)
        last_pe = None
        o_parts = []
        acc = self.acc_bufs[0]
        Ob = self.bank(4)
        Db = self.bank(5)
        LAG = 3
        LB, LC = 2, 3
        for qb in range(4):
            q0 = qb * TB
            mres = []
            for mi, (Q, Kt) in enumerate(maps):
                tiles = list(range(4 * qb + 4))
                if typ == "sb":
                    tiles = tiles[::-1]
                    mz = P.dve(lambda e: e.memset(acc[:, :], 0.0), waits=self.acc_users)
                    self.acc_users = [mz]
                nt = len(tiles)
                stB, stX, stC, xres = [], [], [], {}
                lastpv = None
                lastdn = None
                accw_o = self.o_free
                accw_d = self.d_free
                for ti, kt in enumerate(tiles):
                    j = kt - 4 * qb
                    diag = j >= 0
                    c0 = 128 * j if j > 0 else 0
                    cs = slice(c0, TB)
                    ks = slice(kt * 128, (kt + 1) * 128)
                    qs = slice(q0 + c0, q0 + TB)
                    sk, sbi, sw = sbanks.acquire()
                    Sb = self.bank(sbi)
                    simple = (typ == "diff") and (not diag)
                    mm = P.pe(lambda e, Sb=Sb, Kt=Kt, Q=Q, cs=cs, ks=ks, qs=qs, simple=simple: e.matmul(Sb[:, cs], lhsT=Kt[:, ks], rhs=Q[:, qs], start=True, stop=simple),
                              waits=sw + pdone)
                    if typ == "fox":
                        LT, RT = hb["R2"], hb["R3"]
                        mm = P.pe(lambda e, Sb=Sb, LT=LT, RT=RT, cs=cs, ks=ks, qs=qs, diag=diag: e.matmul(Sb[:, cs], lhsT=LT[:, ks], rhs=RT[:, qs], start=False, stop=(not diag)))
                    if diag:
                        ds = slice(c0, c0 + 128)
                        if typ == "sb":
                            mm = P.pe(lambda e, Sb=Sb, ds=ds: e.matmul(Sb[:, ds], lhsT=self.ident, rhs=self.msb, start=False, stop=False))
                        elif typ == "fox":
                            mm = P.pe(lambda e, Sb=Sb, ds=ds: e.matmul(Sb[:, ds], lhsT=self.ident, rhs=self.mfox, start=False, stop=True))
                        else:
                            dh = self.cbf[:, CB_DCORR + hi * 256:CB_DCORR + hi * 256 + 128]
                            dl = self.cbf[:, CB_DCORR + hi * 256 + 128:CB_DCORR + hi * 256 + 256]
                            P.pe(lambda e, Sb=Sb, ds=ds, dh=dh: e.matmul(Sb[:, ds], lhsT=self.ident, rhs=dh, start=False, stop=False))
                            mm = P.pe(lambda e, Sb=Sb, ds=ds, dl=dl: e.matmul(Sb[:, ds], lhsT=self.ident, rhs=dl, start=False, stop=True))
                    if typ != "sb":
                        pk_, pT, pw = self.pTs.acquire()
                        ex = P.act(lambda e, pT=pT, Sb=Sb, cs=cs: e.activation(out=pT[:, cs], in_=Sb[:, cs], func=AF.Exp), waits=[mm] + pw)
                        sbanks.use(sk, ex)

                        def fC(pT=pT, cs=cs, kt=kt, ti=ti, ex=ex, pk_=pk_, nt=nt, accw_o=accw_o, accw_d=accw_d):
                            pv = P.pe(lambda e: e.matmul(Ob[:, cs], lhsT=V[:, kt, :], rhs=pT[:, cs], start=(ti == 0), stop=(ti == nt - 1)),
                                      waits=[ex] + (accw_o if ti == 0 else []))
                            dn = P.pe(lambda e: e.matmul(Db[:, cs], lhsT=self.ones, rhs=pT[:, cs], start=(ti == 0), stop=(ti == nt - 1)),
                                      waits=(accw_d if ti == 0 else []))
                            self.pTs.use(pk_, dn)
                            return pv, dn
                        stC.append(fC)
                        if ti >= LAG:
                            lastpv, lastdn = stC[ti - LAG]()
                    else:
                        ek, e32, ew = self.tmp(F32)
                        spk, spb, spw = self.sps.acquire()
                        x1 = P.act(lambda e, e32=e32, Sb=Sb, cs=cs: e.activation(out=e32[:, cs], in_=Sb[:, cs], func=AF.Exp), waits=[mm] + ew)

                        def fB(Sb=Sb, spb=spb, cs=cs, ti=ti, spk=spk, x2h=None):
                            x2 = x2h[0]
                            b1 = P.pe(lambda e: e.matmul(Sb[:, cs], lhsT=self.negtri, rhs=spb[:, cs], start=False, stop=(ti == 0)), waits=[x2])
                            if ti > 0:
                                b1 = P.pe(lambda e: e.matmul(Sb[:, cs], lhsT=self.negones, rhs=acc[:, cs], start=False, stop=True), waits=self.acc_users)
                            au = P.dve(lambda e: e.tensor_tensor(out=acc[:, cs], in0=acc[:, cs], in1=spb[:, cs], op=ALU.add), waits=[x2, b1] + self.acc_users)
                            self.acc_users = [au]
                            self.sps.use(spk, b1, au)
                            return b1

                        def fX(b1, Sb=Sb, cs=cs, sk=sk):
                            pk_, pT, pw = self.pTs.acquire()
                            ex = P.act(lambda e: e.activation(out=pT[:, cs], in_=Sb[:, cs], func=AF.Exp), waits=[b1] + pw)
                            sbanks.use(sk, ex)
                            return pk_, pT, ex

                        def fC(pk_, pT, ex, cs=cs, kt=kt, ti=ti, nt=nt, accw_o=accw_o):
                            pv = P.pe(lambda e: e.matmul(Ob[:, cs], lhsT=V[:, kt, :], rhs=pT[:, cs], start=(ti == 0), stop=(ti == nt - 1)),
                                      waits=[ex] + (accw_o if ti == 0 else []))
                            self.pTs.use(pk_, pv)
                            return pv, None
                        x2h = [None]
                        stB.append((fB, x2h))
                        stX.append(fX)
                        stC.append(fC)
                        if ti >= LB:
                            fb, xh = stB[ti - LB]
                            xres[ti - LB] = stX[ti - LB](fb(x2h=xh))
                        x2 = P.act(lambda e, e32=e32, spb=spb, cs=cs: e.activation(out=spb[:, cs], in_=e32[:, cs], func=AF.Ln, bias=self.one_col), waits=[x1] + spw)
                        x2h[0] = x2
                        self.tmps.use(ek, x1, x2)
                        if ti >= LC:
                            lastpv, _ = stC[ti - LC](*xres[ti - LC])
                    self.run_deferred(2)
                    yield
                if typ != "sb":
                    for r in range(max(0, nt - LAG), nt):
                        lastpv, lastdn = stC[r]()
                else:
                    for i in range(nt, nt + LC):
                        if 0 <= i - LB < nt:
                            fb, xh = stB[i - LB]
                            xres[i - LB] = stX[i - LB](fb(x2h=xh))
                        if 0 <= i - LC < nt:
                            lastpv, _ = stC[i - LC](*xres[i - LC])
                last_pe = lastpv if lastdn is None else lastdn
                self.grp += 1
                self.flush_upto(self.grp - 2)
                osl = orow[:, q0:q0 + TB]
                if typ == "sb":
                    ep = P.dve(lambda e, osl=osl: e.tensor_copy(out=osl, in_=Ob), waits=[lastpv] + orow_first)
                    self.o_free = [ep]
                    o_parts.append(ep)
                else:
                    gp = self.epi_par
                    self.epi_par ^= 1
                    o32, d32 = self.epi_bufs[gp]
                    users = self.epi_users[gp]
                    cpo = P.dve(lambda e, o32=o32: e.tensor_copy(out=o32, in_=Ob), waits=[lastpv] + users)
                    cpd = P.act(lambda e, d32=d32: e.activation(out=d32, in_=Db, func=AF.Copy), waits=[lastdn] + users)
                    self.o_free = [cpo]
                    self.d_free = [cpd]
                    rec = []
                    state = {"ops": [cpo, cpd]}
                    self.epi_users[gp] = state["ops"]
                    for pc in range(4):
                        psl = slice(pc * 128, (pc + 1) * 128)

                        def frec(psl=psl, d32=d32, cpd=cpd, state=state):
                            r = P.dve(lambda e: e.reciprocal(out=d32[:, psl], in_=d32[:, psl]), waits=[cpd])
                            state["ops"].append(r)
                            state["lastrec"] = r
                        self.defer(frec)
                    if typ == "fox":
                        def fmul(o32=o32, d32=d32, cpo=cpo, osl=osl, state=state):
                            ep = P.dve(lambda e: e.tensor_tensor(out=osl, in0=o32, in1=d32, op=ALU.mult), waits=[cpo, state["lastrec"]] + orow_first)
                            state["ops"].append(ep)
                            o_parts.append(ep)
                        self.defer(fmul)
                    else:
                        om = self.dO[mi]

                        def fmul(o32=o32, d32=d32, cpo=cpo, om=om, mi=mi, state=state):
                            r2 = P.dve(lambda e: e.tensor_tensor(out=om, in0=o32, in1=d32, op=ALU.mult), waits=[cpo, state["lastrec"]] + self.dO_users[mi])
                            state["ops"].append(r2)
                            self.dO_users[mi] = [r2]
                            self.dO_last[mi] = r2
                        self.defer(fmul)
            if typ == "diff":
                o1, o2 = self.dO
                neglam = pm[:, l * 8 + 5:l * 8 + 6]
                gsub = pm[:, l * 8 + 4:l * 8 + 5]
                osl = orow[:, q0:q0 + TB]
                st = {}

                def f1(st=st, neglam=neglam):
                    st["c1"] = P.dve(lambda e: e.scalar_tensor_tensor(out=o1, in0=o2, scalar=neglam, in1=o1, op0=ALU.mult, op1=ALU.add),
                                     waits=[self.dO_last[0], self.dO_last[1]])

                def f2a(st=st):
                    sqt = self.csq
                    c2 = P.act(lambda e: e.activation(out=sqt, in_=o1, func=AF.Square), waits=[st["c1"]] + self.csq_users)
                    st.update(c2=c2)

                def f2b(st=st):
                    sqt = self.csq
                    c2 = st["c2"]
                    bk2, bank2, bw2 = self.projbanks.acquire()
                    pb2 = self.bank(bank2)
                    c3 = P.pe(lambda e: e.matmul(pb2, lhsT=self.ones, rhs=sqt, start=True, stop=True), waits=[c2] + bw2)
                    c4 = P.act(lambda e: e.activation(out=o2, in_=pb2, func=AF.Ln, bias=self.eps_col, scale=1.0 / 128.0), waits=[c3, st["c1"]])
                    c5 = P.act(lambda e: e.activation(out=o2, in_=o2, func=AF.Exp, scale=-0.5), waits=[c4])
                    self.projbanks.use(bk2, c4)
                    self.csq_users = [c2, c3]
                    st.update(c4=c4, c5=c5)

                def nop():
                    pass

                def f3(st=st, osl=osl, gsub=gsub):
                    ep = P.dve(lambda e: e.scalar_tensor_tensor(out=osl, in0=o1, scalar=gsub, in1=o2, op0=ALU.mult, op1=ALU.mult), waits=[st["c5"]] + orow_first)
                    self.dO_users[0] = [st["c1"], st["c2"], ep]
                    self.dO_users[1] = [st["c1"], st["c4"], st["c5"], ep]
                    o_parts.append(ep)
                for fn in (f1, f2a, nop, f2b, nop, nop, nop, f3):
                    self.defer(fn)
        def fstore():
            st = self.sp_dma(self.oT_d[hidx], orow, self.st_sems[ok], waits=o_parts)
            self.orows.use(ok, st)
        self.defer(fstore)
        self.head_last_grp[hidx] = self.grp
        self.att_last_pe[par] = [last_pe]
        yield

    def attention_phase(self, l, s):
        P = self.P
        self.projbanks = Slots([6, 7])
        self.sbanks = Slots([0, 1, 2, 3])
        self.pTs = Slots(self.pT_slots)
        self.sps = Slots(self.sp_slots)
        self.orows = Slots(self.orow_slots)
        self.acc_users = []
        self.dO_users = [[], []]
        self.dO_last = [None, None]
        self.o_free = []
        self.d_free = []
        self.epi_par = 0
        self.epi_users = [[], []]
        self.dq = []
        self.grp = 0
        self.head_last_grp = [None] * 16
        self.csq = self.view(self.offB + 56 * 1024, 1024, BF16)
        self.csq_users = []
        K = 1024
        ro = self.rb_off
        self.pjbufs = Slots([(self.view(ro, K, BF16), self.view(ro + 2 * K, 2 * K, F32)),
                             (self.view(ro + K, K, BF16), self.view(ro + 24 * K, 2 * K, F32))])
        self.att_last_pe = [[], []]
        self.proj_done = [None, None]
        self.fox_prep(l)
        g = self.proj_gen(l, 0, 0, [])
        for _ in g:
            pass
        for h in range(16):
            par = h % 2
            ag = self.att_gen(l, s, h, par)
            pg = self.proj_gen(l, h + 1, 1 - par, self.att_last_pe[1 - par]) if h + 1 < 16 else iter(())
            cnt = 0
            for _ in ag:
                cnt += 1
                if cnt % 3 == 0:
                    next(pg, None)
            for _ in pg:
                pass
        self.run_deferred(100000)
        P.barrier()

    def gate_phase(self, l):
        P = self.P
        K = 1024
        lds = []
        for c in range(KC):
            lds.append(self.bulk_load(self.B[:, c, :], self.oT_d[c], c))
        ro = self.rb_off
        ysum = self.view(ro, 8 * K, F32)
        gsb = Slots([self.view(ro + 8 * K + i * 4 * K, 4 * K, BF16) for i in range(2)])
        yrows = Slots([self.view(ro + 16 * K + i * 4 * K, 4 * K, BF16) for i in range(2)])
        gbanks = Slots([0, 1, 2])
        bbanks = Slots([3, 4, 5])
        branches = [(self.w_bsb, 6, 0), (self.w_bdf, 5, 6), (self.w_bfx, 5, 11)]
        ysum_users = []
        for m in range(KC):
            yk, yrow, yw = yrows.acquire()
            for i in range(3):
                wk, wp, wop = self.wfetch(self.w_gate[l, i * 16 + m], KC)
                gk, gs, gw = gsb.acquire()
                lastmm = None
                sigs = []
                for tb in range(NTB):
                    bk, bi, bw = gbanks.acquire()
                    pb = self.bank(bi)
                    for kc in range(KC):
                        lastmm = P.pe(lambda e, pb=pb, wp=wp, kc=kc, tb=tb: e.matmul(pb, lhsT=wp[:, kc, :], rhs=self.A[:, kc, tb * TB:(tb + 1) * TB],
                                                                                    start=(kc == 0), stop=(kc == KC - 1)), waits=[wop] + bw)
                    sg = P.act(lambda e, gs=gs, pb=pb, tb=tb: e.activation(out=gs[:, tb * TB:(tb + 1) * TB], in_=pb, func=AF.Sigmoid), waits=[lastmm] + gw)
                    gbanks.use(bk, sg)
                    sigs.append(sg)
                self.wring.use(wk, lastmm)
                wsrc, kcn, coff = branches[i]
                wk, wp, wop = self.wfetch(wsrc[l, m], kcn)
                for tb in range(NTB):
                    bk, bi, bw = bbanks.acquire()
                    pb = self.bank(bi)
                    for kc in range(kcn):
                        lastmm = P.pe(lambda e, pb=pb, wp=wp, kc=kc, tb=tb, coff=coff, kcn=kcn: e.matmul(pb, lhsT=wp[:, kc, :], rhs=self.B[:, coff + kc, tb * TB:(tb + 1) * TB],
                                                                                                        start=(kc == 0), stop=(kc == kcn - 1)), waits=[wop] + bw + [lds[coff + kc]])
                    sl = slice(tb * TB, (tb + 1) * TB)
                    if i == 0:
                        d = P.dve(lambda e, pb=pb, gs=gs, sl=sl: e.tensor_tensor(out=ysum[:, sl], in0=pb, in1=gs[:, sl], op=ALU.mult), waits=[lastmm, sigs[tb]] + ysum_users)
                        bbanks.use(bk, d)
                    else:
                        tk, tt, tw = self.tmp(F32)
                        d0 = P.dve(lambda e, pb=pb, gs=gs, sl=sl, tt=tt: e.tensor_tensor(out=tt, in0=pb, in1=gs[:, sl], op=ALU.mult), waits=[lastmm, sigs[tb]] + tw)
                        bbanks.use(bk, d0)
                        if i == 1:
                            d = P.dve(lambda e, sl=sl, tt=tt: e.tensor_tensor(out=ysum[:, sl], in0=ysum[:, sl], in1=tt, op=ALU.add), waits=[d0])
                        else:
                            d = P.dve(lambda e, sl=sl, tt=tt, yrow=yrow: e.tensor_tensor(out=yrow[:, sl], in0=ysum[:, sl], in1=tt, op=ALU.add), waits=[d0] + yw)
                            ysum_users = [d]
                        self.tmps.use(tk, d0, d)
                    gsb.use(gk, d)
                self.wring.use(wk, lastmm)
            st = self.sp_dma(self.yT_d[m], yrow, self.st_sems[yk], waits=[d])
            yrows.use(yk, st)
        P.barrier()

    def out_phase(self, l, src, dst, gcol_base):
        P = self.P
        K = 1024
        lds = []
        for c in range(KC):
            lds.append(self.bulk_load(self.A[:, c, :], self.yT_d[c], c))
        ro = self.rb_off
        xst = Slots([self.view(ro + i * 8 * K, 8 * K, F32) for i in range(2)])
        sqs = Slots([self.view(ro + 16 * K + i * 4 * K, 4 * K, BF16) for i in range(2)])
        rs = self.view(self.tmp_off, 8 * K, F32)
        banks = Slots([0, 1, 2, 3])
        last_mm = [None] * NTB
        xb_ops = []
        pending = []
        for m in range(KC):
            xk, xs, xw = xst.acquire()
            xl = self.sp_dma(xs, src[m], self.xs_sems[xk], waits=xw)
            wk, wp, wop = self.wfetch(self.w_out[l, m], KC)
            adds = []
            for tb in range(NTB):
                bk, bi, bw = banks.acquire()
                pb = self.bank(bi)
                for kc in range(KC):
                    lastmm = P.pe(lambda e, pb=pb, wp=wp, kc=kc, tb=tb: e.matmul(pb, lhsT=wp[:, kc, :], rhs=self.A[:, kc, tb * TB:(tb + 1) * TB],
                                                                                start=(kc == 0), stop=(kc == KC - 1)), waits=[wop] + bw + [lds[kc]])
                sl = slice(tb * TB, (tb + 1) * TB)
                a = P.dve(lambda e, pb=pb, xs=xs, sl=sl: e.tensor_tensor(out=xs[:, sl], in0=pb, in1=xs[:, sl], op=ALU.add), waits=[lastmm, xl])
                banks.use(bk, a)
                adds.append(a)
            self.wring.use(wk, lastmm)
            while pending:
                pending.pop(0)()
            st = self.sp_dma(dst[m], xs, self.st_sems[xk], waits=adds)
            k2, sq, w2 = sqs.acquire()
            sqo = P.act(lambda e, sq=sq, xs=xs: e.activation(out=sq, in_=xs, func=AF.Square), waits=adds + w2)
            g = self.pvec[:, gcol_base + m:gcol_base + m + 1]
            xb = P.dve(lambda e, m=m, xs=xs, g=g: e.tensor_scalar(out=self.B[:, m, :], in0=xs, scalar1=g, scalar2=None, op0=ALU.mult), waits=adds)
            xb_ops.append(xb)
            def ssmm(sq=sq, m=m, sqo=sqo, k2=k2):
                for tb in range(NTB):
                    last_mm[tb] = P.pe(lambda e, tb=tb: e.matmul(self.bank(4 + tb), lhsT=self.ones, rhs=sq[:, tb * TB:(tb + 1) * TB],
                                                                 start=(m == 0), stop=(m == KC - 1)), waits=[sqo])
                sqs.use(k2, last_mm[NTB - 1])
            pending.append(ssmm)
            xst.use(xk, st, sqo, xb)
        while pending:
            pending.pop(0)()
        self.finish_norm(self.B, rs, 4, last_mm, xb_ops)

    def ffn_up_phase(self, l, X):
        P = self.P
        self.h_pre = {}
        K = 1024
        oB = self.offB
        ro = self.rb_off
        srows = Slots([self.view(ro + i * 4 * K, 4 * K, BF16) for i in range(2)])
        hrows = Slots([self.view(ro + 8 * K + i * 4 * K, 4 * K, BF16) for i in range(2)])
        gb = Slots([0, 1, 2, 3])
        ub = Slots([4, 5, 6, 7])
        for m in range(FC):
            wk, wp, wop = self.wfetch(self.w_fg[l, m], KC)
            sk, sr, sw = srows.acquire()
            sil = []
            for tb in range(NTB):
                bk, bi, bw = gb.acquire()
                pb = self.bank(bi)
                for kc in range(KC):
                    lastmm = P.pe(lambda e, pb=pb, wp=wp, kc=kc, tb=tb: e.matmul(pb, lhsT=wp[:, kc, :], rhs=X[:, kc, tb * TB:(tb + 1) * TB],
                                                                                start=(kc == 0), stop=(kc == KC - 1)), waits=[wop, self.norm_done[tb]] + bw)
                sg = P.act(lambda e, sr=sr, pb=pb, tb=tb: e.activation(out=sr[:, tb * TB:(tb + 1) * TB], in_=pb, func=AF.Silu), waits=[lastmm] + sw)
                gb.use(bk, sg)
                sil.append(sg)
            self.wring.use(wk, lastmm)
            wk, wp, wop = self.wfetch(self.w_fu[l, m], KC)
            hk, hr, hw = hrows.acquire()
            muls = []
            for tb in range(NTB):
                bk, bi, bw = ub.acquire()
                pb = self.bank(bi)
                for kc in range(KC):
                    lastmm = P.pe(lambda e, pb=pb, wp=wp, kc=kc, tb=tb: e.matmul(pb, lhsT=wp[:, kc, :], rhs=X[:, kc, tb * TB:(tb + 1) * TB],
                                                                                start=(kc == 0), stop=(kc == KC - 1)), waits=[wop, self.norm_done[tb]] + bw)
                sl = slice(tb * TB, (tb + 1) * TB)
                d = P.dve(lambda e, pb=pb, sr=sr, hr=hr, sl=sl: e.tensor_tensor(out=hr[:, sl], in0=pb, in1=sr[:, sl], op=ALU.mult), waits=[lastmm, sil[tb]] + hw)
                ub.use(bk, d)
                muls.append(d)
            self.wring.use(wk, lastmm)
            srows.use(sk, *muls)
            st = self.sp_dma(self.hT_d[m], hr, self.st_sems[hk], waits=muls)
            hrows.use(hk, st)
            if m < 32:
                Hv = self.view(0, FC * 1024 * 2, BF16, "p (k t) -> p k t", k=FC)
                self.h_pre[m] = self.bulk_load(Hv[:, m, :], self.hT_d[m][:, 0:1024], m, waits=[st])
        P.barrier()

    def ffn_down_phase(self, l, src, dst, next_g=None):
        P = self.P
        K = 1024
        HT = 1024
        H = self.view(0, FC * HT * 2, BF16, "p (k t) -> p k t", k=FC)
        xoff = FC * HT * 2
        self.pre_ss = [None] * NTB
        for half in range(2):
            t0 = half * HT
            lds = []
            for c in range(FC):
                if half == 0 and c in self.h_pre:
                    lds.append(self.h_pre[c])
                else:
                    lds.append(self.bulk_load(H[:, c, :], self.hT_d[c][:, t0:t0 + HT], c))
            xst = Slots([self.view(xoff + i * 4 * K, 4 * K, F32) for i in range(2)])
            sqs = Slots([self.view(xoff + 8 * K + i * 2 * K, 2 * K, BF16) for i in range(2)])
            xbs = Slots([self.view(xoff + 12 * K + i * 2 * K, 2 * K, BF16) for i in range(2)])
            banks = Slots([0, 1, 2, 3])
            pending = []
            for m in range(KC):
                xk, xs, xw = xst.acquire()
                xl = self.sp_dma(xs, src[m][:, t0:t0 + HT], self.xs_sems[xk], waits=xw)
                parts = []
                for (k0, k1) in ((0, 16), (16, 32), (32, 44)):
                    wk, wp, wop = self.wfetch(self.w_fd[l, m][:, k0:k1, :], k1 - k0)
                    parts.append((wk, wp, wop, k0, k1))
                adds = []
                for tbh in range(2):
                    bk, bi, bw = banks.acquire()
                    pb = self.bank(bi)
                    for (wk, wp, wop, k0, k1) in parts:
                        for kc in range(k0, k1):
                            lastmm = P.pe(lambda e, pb=pb, wp=wp, kc=kc, k0=k0, tbh=tbh: e.matmul(pb, lhsT=wp[:, kc - k0, :], rhs=H[:, kc, tbh * TB:(tbh + 1) * TB],
                                                                                                 start=(kc == 0), stop=(kc == FC - 1)), waits=[wop] + bw + [lds[kc]])
                    sl = slice(tbh * TB, (tbh + 1) * TB)
                    a = P.dve(lambda e, pb=pb, xs=xs, sl=sl: e.tensor_tensor(out=xs[:, sl], in0=pb, in1=xs[:, sl], op=ALU.add), waits=[lastmm, xl])
                    banks.use(bk, a)
                    adds.append(a)
                for (wk, wp, wop, k0, k1) in parts:
                    self.wring.use(wk, lastmm)
                while pending:
                    pending.pop(0)()
                st = self.sp_dma(dst[m][:, t0:t0 + HT], xs, self.st_sems[xk], waits=adds)
                users = [st]
                if next_g is not None:
                    k2, sq, w2 = sqs.acquire()
                    sqo = P.act(lambda e, sq=sq, xs=xs: e.activation(out=sq, in_=xs, func=AF.Square), waits=adds + w2)
                    k3, xbt, w3 = xbs.acquire()
                    g = self.pvec[:, next_g + m:next_g + m + 1]
                    xb = P.dve(lambda e, xbt=xbt, xs=xs, g=g: e.tensor_scalar(out=xbt, in0=xs, scalar1=g, scalar2=None, op0=ALU.mult), waits=adds + w3)
                    st2 = self.sp_dma(self.xbT_d[m][:, t0:t0 + HT], xbt, self.xb_sems[k3], waits=[xb])
                    xbs.use(k3, st2)

                    def ssmm(sq=sq, m=m, sqo=sqo, k2=k2, half=half, sqs=sqs):
                        for tbh in range(2):
                            tb = half * 2 + tbh
                            self.pre_ss[tb] = P.pe(lambda e, tb=tb, tbh=tbh: e.matmul(self.bank(4 + tb), lhsT=self.ones, rhs=sq[:, tbh * TB:(tbh + 1) * TB],
                                                                                  start=(m == 0), stop=(m == KC - 1)), waits=[sqo])
                        sqs.use(k2, self.pre_ss[half * 2 + 1])
                    pending.append(ssmm)
                    users += [sqo, xb]
                xst.use(xk, *users)
            while pending:
                pending.pop(0)()
            P.barrier()

    def norm_phase_fast(self):
        K = 1024
        lds = []
        for c in range(KC):
            lds.append(self.bulk_load(self.A[:, c, :], self.xbT_d[c], c))
        rs = self.view(self.rb_off + 12 * K, 8 * K, F32)
        self.finish_norm(self.A, rs, 4, self.pre_ss, lds)

    def build(self):
        with ExitStack() as es:
            self.setup(es)
            P = self.P
            K = 1024
            cc = self.lamtmp
            self.eps_col = cc[:, 40:41]
            self.one_col = cc[:, 41:42]
            self.epsd_col = {64.0: cc[:, 42:43], 128.0: cc[:, 43:44]}
            P.dve(lambda e: e.memset(self.eps_col, EPS))
            P.dve(lambda e: e.memset(self.one_col, 1.0))
            P.dve(lambda e: e.memset(self.epsd_col[64.0], EPS * 64.0))
            P.dve(lambda e: e.memset(self.epsd_col[128.0], EPS * 128.0))
            self.onesf = self.view(self.rb_off + 26 * K, 2 * K, F32)
            P.dve(lambda e: e.memset(self.onesf, 1.0))
            self.init_phase()
            self.attn_setup()
            stop = self.stop_after
            done = False
            for s in range(self.n_seq):
                for l in range(self.n_layers):
                    src = self.xT[s] if l == 0 else self.xres[s]
                    lastl = (l == self.n_layers - 1)
                    base = l * PV_PER_L
                    if l == 0:
                        self.norm_phase(src, base + PV_GMIX)
                    else:
                        self.norm_phase_fast()
                    if self.dbg and s == 0 and l == 0:
                        for c in range(KC):
                            self.sp_dma(self.dbgA[c], self.A[:, c, :], self.misc_sem, waits=[x for x in self.norm_done if x is not None])
                        P.barrier()
                    if stop == "norm1":
                        done = True
                        break
                    self.attention_phase(l, s)
                    if stop == "attn":
                        done = True
                        break
                    self.gate_phase(l)
                    if stop == "gate":
                        done = True
                        break
                    self.out_phase(l, src, self.xres[s], base + PV_GFFN)
                    if stop == "out":
                        done = True
                        break
                    self.ffn_up_phase(l, self.B)
                    if stop == "ffn_up":
                        done = True
                        break
                    self.ffn_down_phase(l, self.xres[s], self.outT[s] if lastl else self.xres[s],
                                        next_g=(None if lastl else (l + 1) * PV_PER_L + PV_GMIX))
                if done:
                    break
            P.barrier()
            P.build()
        return self.nc


def _panels(w, kc, mc):
    L = w.shape[0]
    return np.ascontiguousarray(w.reshape(L, kc, 128, mc, 128).transpose(0, 3, 2, 1, 4))


def _hi_lo(v):
    hi = v.astype(NPBF)
    lo = (v - hi.astype(np.float32)).astype(NPBF)
    return hi, lo


def host_consts():
    cb = np.zeros((128, CB_N), np.float32)
    j = np.arange(128)[:, None]
    sidx = np.arange(128)[None, :]
    cb[:, CB_IDENT:CB_IDENT + 128] = np.eye(128)
    cb[:, CB_NEGTRI:CB_NEGTRI + 128] = np.where(j >= sidx, -1.0, 0.0)
    cb[:, CB_NEGONES:CB_NEGONES + 128] = -1.0
    cb[:, CB_ONES:CB_ONES + 128] = 1.0
    blk = np.zeros((128, 128))
    blk[:64, :64] = 1.0
    blk[64:, 64:] = 1.0
    cb[:, CB_BLK64:CB_BLK64 + 128] = blk
    kl = np.arange(128)[:, None]
    ql = np.arange(128)[None, :]
    cb[:, CB_MSB:CB_MSB + 128] = np.where(kl < ql, 0.0, NEG)
    cb[:, CB_MFOX:CB_MFOX + 128] = np.where(kl <= ql, 0.0, NEG)
    cbb = cb.astype(NPBF)
    for h in range(5):
        sl = SLOPES[h]
        v = np.where(ql >= kl, 0.0, np.where((kl // 64) <= (ql // 64), -2.0 * sl * (kl - ql), NEG)).astype(np.float32)
        hi, lo = _hi_lo(v)
        cbb[:, CB_DCORR + h * 256:CB_DCORR + h * 256 + 128] = hi
        cbb[:, CB_DCORR + h * 256 + 128:CB_DCORR + h * 256 + 256] = lo
    pos = np.arange(S, dtype=np.float64)
    augk = np.zeros((5, 4, S), NPBF)
    augq = np.zeros((5, 4, S), NPBF)
    for h in range(5):
        a = (SLOPES[h] * pos).astype(np.float32)
        r = (-SLOPES[h] * pos).astype(np.float32)
        ah, al = _hi_lo(a)
        rh, rl = _hi_lo(r)
        augk[h, 0] = 1.0
        augk[h, 1] = 1.0
        augk[h, 2] = ah
        augk[h, 3] = al
        augq[h, 0] = rh
        augq[h, 1] = rl
        augq[h, 2] = 1.0
        augq[h, 3] = 1.0
    foxinit = np.zeros((5, 2, 4, S), NPBF)
    foxinit[:, 0, 0:2, :] = 1.0
    foxinit[:, 1, 2:4, :] = 1.0
    return cbb, augk, augq, foxinit


def host_prep(inp):
    f = lambda k: np.asarray(inp[k], dtype=np.float32)
    L = L_ALL
    w_in = f("w_in")
    w_in_p = np.zeros((L, D, 49 * 128), np.float32)
    w_in_p[:, :, :6149] = w_in
    out = {}
    out["w_in"] = _panels(w_in_p, KC, 49)
    out["w_gate"] = _panels(f("w_gate"), KC, 48)
    out["w_bsb"] = _panels(f("w_branch_sb"), 6, 16)
    out["w_bdf"] = _panels(f("w_branch_diff"), 5, 16)
    out["w_bfx"] = _panels(f("w_branch_fox"), 5, 16)
    out["w_out"] = _panels(f("w_out"), KC, 16)
    out["w_fg"] = _panels(f("w_ff_gate"), KC, FC)
    out["w_fu"] = _panels(f("w_ff_up"), KC, FC)
    out["w_fd"] = _panels(f("w_ff_down"), FC, 16)
    pv = np.zeros((128, L * PV_PER_L), np.float32)
    for l in range(L):
        b = l * PV_PER_L
        pv[:, b + PV_GMIX:b + PV_GMIX + 16] = f("norm_mix")[l].reshape(16, 128).T
        pv[:, b + PV_GFFN:b + PV_GFFN + 16] = f("norm_ffn")[l].reshape(16, 128).T
        pv[:, b + PV_QD] = np.tile(f("q_norm_diff")[l], 2)
        pv[:, b + PV_KD] = np.tile(f("k_norm_diff")[l], 2)
        pv[:, b + PV_QF] = f("q_norm_fox")[l]
        pv[:, b + PV_KF] = f("k_norm_fox")[l]
        pv[:, b + PV_SUB] = f("sub_norm_diff")[l]
        pv[0:5, b + PV_BF] = f("b_forget")[l]
        for i, nm in enumerate(("lambda_q1", "lambda_k1", "lambda_q2", "lambda_k2")):
            pv[:, b + PV_LAM + 64 * i:b + PV_LAM + 64 * (i + 1)] = f(nm)[l][None, :]
    out["pvec"] = pv
    cbb, augk, augq, foxinit = host_consts()
    out["cbf"] = cbb
    out["augk"] = augk
    out["augq"] = augq
    out["foxinit"] = foxinit
    return out


_CACHE = {}


def kernel(**inputs):
    x = np.asarray(inputs["x"], dtype=np.float32)
    shared = host_prep(inputs)
    if "nc" not in _CACHE:
        _CACHE["nc"] = Builder().build()
    nc = _CACHE["nc"]
    in_maps = []
    for c in range(N_CORES):
        xs = x[c * SEQ_PER_CORE:(c + 1) * SEQ_PER_CORE]
        xT = np.ascontiguousarray(xs.transpose(0, 2, 1)).reshape(SEQ_PER_CORE, KC, 128, S)
        m = dict(shared)
        m["xT"] = xT
        in_maps.append(m)
    res = run_bass_kernel_spmd(nc, in_maps, core_ids=list(range(N_CORES)))
    outs = []
    for c in range(N_CORES):
        o = np.asarray(res.results[c]["outT"]).reshape(SEQ_PER_CORE, D, S).transpose(0, 2, 1)
        outs.append(o)
    return np.ascontiguousarray(np.concatenate(outs, axis=0)).astype(np.float32)
```
